# Optimizing a Trainium2 kernel written in Bass

```python
import math
import jax, jax.numpy as jnp
from jax import lax
import numpy as np

D_MODEL = 2048
BATCH = 16
SEQ = 2048
DEPTH = 4

CHUNK = 64
Q_BLOCK = 128
N_EVEN = (DEPTH + 1) // 2
N_ODD = DEPTH // 2
EPS = 1e-6
ROPE_THETA = 10000.0
MAX_OFFSET = 4096

MLA_HEADS = 8
MLA_NOPE = 128
MLA_ROPE = 64
MLA_VDIM = 128
Q_LORA = 512
KV_LORA = 256
MLA_OUT = MLA_HEADS * MLA_VDIM

SGU_BLOCK = 128
SGU_GROUPS = 8
SGU_CH = 128
SGU_WIDTH = SGU_GROUPS * SGU_CH

EVEN_IN_WIDTH = Q_LORA + KV_LORA + MLA_ROPE + 2 * SGU_WIDTH
EVEN_SPLITS = (Q_LORA, Q_LORA + KV_LORA, Q_LORA + KV_LORA + MLA_ROPE,
               Q_LORA + KV_LORA + MLA_ROPE + SGU_WIDTH)
EVEN_OUT_WIDTH = MLA_OUT + SGU_WIDTH

RET_HEADS = 8
RET_DK = D_MODEL // RET_HEADS
RET_DV = 2 * RET_DK
RET_QK_WIDTH = RET_HEADS * RET_DK
RET_V_WIDTH = RET_HEADS * RET_DV
RET_IN_WIDTH = 2 * RET_QK_WIDTH + 2 * RET_V_WIDTH
RET_SPLITS = (RET_QK_WIDTH, 2 * RET_QK_WIDTH, 2 * RET_QK_WIDTH + RET_V_WIDTH)

D_FF = 5632
CONV_WIDTH = 3

PLE_DIM = 256

kernel_name = "hybrid_mla_gmlp_retention_convffn"


def _rms_norm(x, g):
    xf = x.astype(jnp.float32)
    y = xf * lax.rsqrt(jnp.mean(xf * xf, axis=-1, keepdims=True) + EPS)
    return (y * g.astype(jnp.float32)).astype(x.dtype)


def _layer_norm(x, g, b):
    xf = x.astype(jnp.float32)
    mu = jnp.mean(xf, axis=-1, keepdims=True)
    xc = xf - mu
    y = xc * lax.rsqrt(jnp.mean(xc * xc, axis=-1, keepdims=True) + EPS)
    return (y * g.astype(jnp.float32) + b.astype(jnp.float32)).astype(x.dtype)


def _rope(x, pos):
    half = x.shape[-1] // 2
    inv_freq = ROPE_THETA ** (-jnp.arange(half, dtype=jnp.float32) / half)
    ang = pos.astype(jnp.float32)[..., None] * inv_freq
    cos = jnp.cos(ang)[:, :, None, :]
    sin = jnp.sin(ang)[:, :, None, :]
    xf = x.astype(jnp.float32)
    x1, x2 = xf[..., :half], xf[..., half:]
    return jnp.concatenate([x1 * cos - x2 * sin, x2 * cos + x1 * sin], axis=-1).astype(x.dtype)


def _mla(c_q, c_kv, k_pe, pos, q_norm_g, w_q_up, kv_norm_g, w_kv_up):
    B, S, _ = c_q.shape
    q = (_rms_norm(c_q, q_norm_g) @ w_q_up).reshape(B, S, MLA_HEADS, MLA_NOPE + MLA_ROPE)
    q = jnp.concatenate([q[..., :MLA_NOPE], _rope(q[..., MLA_NOPE:], pos)], axis=-1)
    kv = (_rms_norm(c_kv, kv_norm_g) @ w_kv_up).reshape(B, S, MLA_HEADS, MLA_NOPE + MLA_VDIM)
    k_rot = _rope(k_pe[:, :, None, :], pos)
    k = jnp.concatenate([kv[..., :MLA_NOPE],
                         jnp.broadcast_to(k_rot, (B, S, MLA_HEADS, MLA_ROPE))], axis=-1)
    v = kv[..., MLA_NOPE:]
    scale = (MLA_NOPE + MLA_ROPE) ** -0.5
    n_qb = S // Q_BLOCK
    q_blocks = q.reshape(B, n_qb, Q_BLOCK, MLA_HEADS, MLA_NOPE + MLA_ROPE).swapaxes(0, 1)
    key_chunk = jnp.arange(S) // CHUNK

    def attend(args):
        qb, qi = args
        s = jnp.einsum('bqhd,bkhd->bhqk', qb, k, preferred_element_type=jnp.float32) * scale
        q_chunk = (qi * Q_BLOCK + jnp.arange(Q_BLOCK)) // CHUNK
        mask = key_chunk[None, :] <= q_chunk[:, None]
        s = jnp.where(mask, s, -jnp.inf)
        pr = jax.nn.softmax(s, axis=-1).astype(v.dtype)
        return jnp.einsum('bhqk,bkhd->bqhd', pr, v)

    o = lax.map(attend, (q_blocks, jnp.arange(n_qb)))
    return o.swapaxes(0, 1).reshape(B, S, MLA_OUT)


def _sgu(u, v, ln_g, ln_b, w_s, b_s):
    B, S, _ = u.shape
    u = jax.nn.gelu(u)
    v = _layer_norm(jax.nn.gelu(v), ln_g, ln_b)
    vr = v.reshape(B, S // SGU_BLOCK, SGU_BLOCK, SGU_GROUPS, SGU_CH)
    pos_chunk = jnp.arange(SGU_BLOCK) // CHUNK
    w = jnp.where(pos_chunk[:, None] >= pos_chunk[None, :], w_s, 0.0)
    s = jnp.einsum('gpq,bnqgc->bnpgc', w, vr) + b_s.T[None, None, :, :, None]
    return u * s.reshape(B, S, SGU_WIDTH)


def _retention(q, k, v, pos):
    B, S, H, _ = q.shape
    n = S // CHUNK
    q = _rope(q, pos)
    k = _rope(k, pos) * (RET_DK ** -0.5)
    log_g = jnp.log1p(-(2.0 ** (-5.0 - jnp.arange(H, dtype=jnp.float32))))
    idx = jnp.arange(CHUNK, dtype=jnp.float32)
    intra_decay = jnp.exp(log_g[:, None, None] * jnp.abs(idx[:, None] - idx[None, :]))
    k_decay = jnp.exp(log_g[None, :] * (CHUNK - 1 - idx)[:, None])
    q_decay = jnp.exp(log_g[None, :] * (idx + 1.0)[:, None])
    chunk_decay = jnp.exp(log_g * CHUNK)
    qc = q.reshape(B, n, CHUNK, H, RET_DK)
    kc = k.reshape(B, n, CHUNK, H, RET_DK)
    vc = v.reshape(B, n, CHUNK, H, RET_DV)
    s = jnp.einsum('bnqhd,bnkhd->bnhqk', qc, kc, preferred_element_type=jnp.float32) * intra_decay
    intra = jnp.einsum('bnhqk,bnkhe->bnqhe', s, vc.astype(jnp.float32))

    def step(state, xs):
        q_i, k_i, v_i = xs
        cross = jnp.einsum('bqhd,bhde->bqhe', q_i.astype(jnp.float32), state) * q_decay[None, :, :, None]
        kz = k_i.astype(jnp.float32) * k_decay[None, :, :, None]
        state = state * chunk_decay[None, :, None, None] + jnp.einsum('bkhd,bkhe->bhde', kz, v_i.astype(jnp.float32))
        return state, cross

    state0 = jnp.zeros((B, H, RET_DK, RET_DV), jnp.float32)
    _, cross = lax.scan(step, state0, (qc.swapaxes(0, 1), kc.swapaxes(0, 1), vc.swapaxes(0, 1)))
    out = intra + cross.swapaxes(0, 1)
    return out.reshape(B, S, H, RET_DV)


def _conv_ffn(h, w_up, conv_w, conv_b, w_down):
    a = h @ w_up
    S = a.shape[1]
    ap = jnp.pad(a, ((0, 0), (CONV_WIDTH - 1, 0), (0, 0)))
    c = conv_b
    for j in range(CONV_WIDTH):
        c = c + ap[:, j:j + S] * conv_w[j]
    gate, val = jnp.split(c, 2, axis=-1)
    return (jax.nn.gelu(gate) * val) @ w_down


def setup_inputs(seed: int = 0) -> dict:
    key = jax.random.key(seed)
    ks = jax.random.split(key, 32)
    f32 = jnp.float32

    def nrm(k, shape, scale):
        return jax.random.normal(k, shape, f32) * scale

    def gain(k, shape):
        return 1.0 + 0.01 * jax.random.normal(k, shape, f32)

    x = nrm(ks[0], (BATCH, SEQ, D_MODEL), 1.0)
    p = nrm(ks[1], (DEPTH, BATCH, SEQ, PLE_DIM), 1.0)
    positions = (jax.random.randint(ks[2], (BATCH, 1), 0, MAX_OFFSET, dtype=jnp.int32)
                 + jnp.arange(SEQ, dtype=jnp.int32)[None, :])
    return {
        'x': x,
        'p': p,
        'positions': positions,
        'mix_norm_g': gain(ks[3], (DEPTH, D_MODEL)),
        'even_w_in': nrm(ks[4], (N_EVEN, D_MODEL, EVEN_IN_WIDTH), D_MODEL ** -0.5),
        'mla_q_norm_g': gain(ks[5], (N_EVEN, Q_LORA)),
        'mla_w_q_up': nrm(ks[6], (N_EVEN, Q_LORA, MLA_HEADS * (MLA_NOPE + MLA_ROPE)), Q_LORA ** -0.5),
        'mla_kv_norm_g': gain(ks[7], (N_EVEN, KV_LORA)),
        'mla_w_kv_up': nrm(ks[8], (N_EVEN, KV_LORA, MLA_HEADS * (MLA_NOPE + MLA_VDIM)), KV_LORA ** -0.5),
        'sgu_ln_g': gain(ks[9], (N_EVEN, SGU_WIDTH)),
        'sgu_ln_b': nrm(ks[10], (N_EVEN, SGU_WIDTH), 0.01),
        'sgu_w_s': nrm(ks[11], (N_EVEN, SGU_GROUPS, SGU_BLOCK, SGU_BLOCK), SGU_BLOCK ** -0.5),
        'sgu_b_s': gain(ks[12], (N_EVEN, SGU_GROUPS, SGU_BLOCK)),
        'even_w_out': nrm(ks[13], (N_EVEN, EVEN_OUT_WIDTH, D_MODEL), EVEN_OUT_WIDTH ** -0.5),
        'ret_w_in': nrm(ks[14], (N_ODD, D_MODEL, RET_IN_WIDTH), D_MODEL ** -0.5),
        'ret_gn_g': gain(ks[15], (N_ODD, RET_V_WIDTH)),
        'ret_gn_b': nrm(ks[16], (N_ODD, RET_V_WIDTH), 0.01),
        'ret_w_out': nrm(ks[17], (N_ODD, RET_V_WIDTH, D_MODEL), RET_V_WIDTH ** -0.5),
        'ffn_norm_g': gain(ks[18], (DEPTH, D_MODEL)),
        'ffn_w_up': nrm(ks[19], (DEPTH, D_MODEL, 2 * D_FF), D_MODEL ** -0.5),
        'ffn_conv_w': nrm(ks[20], (DEPTH, CONV_WIDTH, 2 * D_FF), CONV_WIDTH ** -0.5),
        'ffn_conv_b': nrm(ks[21], (DEPTH, 2 * D_FF), 0.01),
        'ffn_w_down': nrm(ks[22], (DEPTH, D_FF, D_MODEL), D_FF ** -0.5),
        'ple_norm_g': gain(ks[23], (DEPTH, D_MODEL)),
        'ple_w_gate': nrm(ks[24], (DEPTH, D_MODEL, D_MODEL), D_MODEL ** -0.5),
        'ple_w_up': nrm(ks[25], (DEPTH, PLE_DIM, D_MODEL), PLE_DIM ** -0.5),
        'final_norm_g': gain(ks[26], (D_MODEL,)),
    }


def reference(x, p, positions, mix_norm_g, even_w_in, mla_q_norm_g, mla_w_q_up,
              mla_kv_norm_g, mla_w_kv_up, sgu_ln_g, sgu_ln_b, sgu_w_s, sgu_b_s,
              even_w_out, ret_w_in, ret_gn_g, ret_gn_b, ret_w_out, ffn_norm_g,
              ffn_w_up, ffn_conv_w, ffn_conv_b, ffn_w_down, ple_norm_g, ple_w_gate,
              ple_w_up, final_norm_g):
    B, S, _ = x.shape
    h = x
    for i in range(DEPTH):
        hn = _rms_norm(h, mix_norm_g[i])
        if i % 2 == 0:
            j = i // 2
            z = hn @ even_w_in[j]
            c_q, c_kv, k_pe, u, v = jnp.split(z, EVEN_SPLITS, axis=-1)
            a_out = _mla(c_q, c_kv, k_pe, positions, mla_q_norm_g[j], mla_w_q_up[j],
                         mla_kv_norm_g[j], mla_w_kv_up[j])
            b_out = _sgu(u, v, sgu_ln_g[j], sgu_ln_b[j], sgu_w_s[j], sgu_b_s[j])
            h = h + jnp.concatenate([a_out, b_out], axis=-1) @ even_w_out[j]
        else:
            j = i // 2
            z = hn @ ret_w_in[j]
            q, k, v, g = jnp.split(z, RET_SPLITS, axis=-1)
            r = _retention(q.reshape(B, S, RET_HEADS, RET_DK),
                           k.reshape(B, S, RET_HEADS, RET_DK),
                           v.reshape(B, S, RET_HEADS, RET_DV), positions)
            r = _layer_norm(r, ret_gn_g[j].reshape(RET_HEADS, RET_DV),
                            ret_gn_b[j].reshape(RET_HEADS, RET_DV))
            r = r.reshape(B, S, RET_V_WIDTH).astype(h.dtype) * jax.nn.silu(g)
            h = h + r @ ret_w_out[j]
        h = h + _conv_ffn(_rms_norm(h, ffn_norm_g[i]), ffn_w_up[i], ffn_conv_w[i],
                          ffn_conv_b[i], ffn_w_down[i])
        gate = jax.nn.sigmoid(_rms_norm(h, ple_norm_g[i]) @ ple_w_gate[i])
        h = h + (p[i] @ ple_w_up[i]) * gate
    return _rms_norm(h, final_norm_g)
```

```python
import contextlib
import numpy as np
import concourse.bass as bass
import concourse.mybir as mybir
from concourse.bass_utils import run_bass_kernel_spmd

F32 = mybir.dt.float32
BF16 = mybir.dt.bfloat16
I32 = mybir.dt.int32
AF = mybir.ActivationFunctionType
ALU = mybir.AluOpType

D = 2048
SEQ = 2048
NSEQ = 2
T = NSEQ * SEQ
DEPTH = 4
DFF = 5632
EPS = 1e-6
NCORES = 8


class Sem:
    def __init__(self, h, key):
        self.h = h
        self.key = key
        self.count = 0


class Buf:
    def __init__(self, t, name="", dsem=None):
        self.t = t
        self.name = name
        self.lw = None
        self.rd = {}
        self.dsem = dsem
        self.excl = False


class Eng:
    def __init__(self, name, h, sem):
        self.name = name
        self.h = h
        self.sem = sem
        self.waited = {}


class Ring:
    def __init__(self, bufs):
        self.b = bufs
        self.i = 0

    def next(self):
        b = self.b[self.i % len(self.b)]
        self.i += 1
        return b


class Prog:
    def __init__(self, nc, es, ndma=60):
        self.nc = nc
        self.es = es
        self.sems = {}
        self.uid = 0

        def mk(name):
            h = es.enter_context(nc.semaphore(name))
            s = Sem(h, name)
            self.sems[name] = s
            return s

        self.E = {
            "pe": Eng("pe", nc.tensor, mk("c_pe")),
            "act": Eng("act", nc.scalar, mk("c_act")),
            "dve": Eng("dve", nc.vector, mk("c_dve")),
            "pool": Eng("pool", nc.gpsimd, mk("c_pool")),
            "sp": Eng("sp", nc.sync, None),
        }
        self.dpool = [mk(f"d{i}") for i in range(ndma)]
        self.dnext = 0
        self.pstack = None

    def _need(self, E, deps):
        for key, val in deps:
            if E.name == "pe" and E.sem is not None and key == E.sem.key:
                continue
            if E.waited.get(key, 0) >= val:
                continue
            E.h.wait_ge(self.sems[key].h, val)
            E.waited[key] = val

    def _deps(self, reads, writes, own=None):
        deps = []
        for b in reads:
            if b.lw:
                deps.append(b.lw)
            if b.excl:
                deps.extend((k, v) for k, v in b.rd.items() if k != own)
        for b in writes:
            if b.lw and b.lw[0] != own:
                deps.append(b.lw)
            deps.extend((k, v) for k, v in b.rd.items() if k != own)
        return deps

    def _mark(self, reads, writes, tk):
        for b in reads:
            if b.rd.get(tk[0], 0) < tk[1]:
                b.rd[tk[0]] = tk[1]
        for b in writes:
            b.lw = tk
            b.rd = {}

    def op(self, eng, fn, reads=(), writes=(), inc=True):
        E = self.E[eng]
        self._need(E, self._deps(reads, writes, E.sem.key))
        ins = fn(E.h)
        if inc:
            E.sem.count += 1
            ins.then_inc(E.sem.h, 1)
            tk = (E.sem.key, E.sem.count)
        else:
            tk = (E.sem.key, E.sem.count + 1)
        self._mark(reads, writes, tk)
        return ins

    def dma(self, q, out, in_, reads=(), writes=(), sem=None):
        Q = self.E[q]
        self._need(Q, self._deps(reads, writes))
        ins = Q.h.dma_start(out=out, in_=in_)
        sem.count += 16
        ins.then_inc(sem.h, 16)
        self._mark(reads, writes, (sem.key, sem.count))

    def barrier(self):
        allt = [(s.key, s.count) for s in self.sems.values() if s.count > 0]
        for E in self.E.values():
            self._need(E, allt)

    @contextlib.contextmanager
    def phase(self):
        self.dnext = 0
        with contextlib.ExitStack() as ps:
            self.pstack = ps
            yield
            self.barrier()
        self.pstack = None

    def sb(self, name, shape, dt, dma=False):
        self.uid += 1
        t = self.pstack.enter_context(self.nc.sbuf_tensor(f"{name}_{self.uid}", shape, dt))
        b = Buf(t, name)
        if dma:
            b.dsem = self.dpool[self.dnext]
            self.dnext += 1
        return b

    def sbring(self, name, n, shape, dt, dma=False):
        return Ring([self.sb(f"{name}{i}", shape, dt, dma) for i in range(n)])

    def psring(self, name, n, shape=(128, 512), dt=F32):
        bufs = []
        for i in range(n):
            self.uid += 1
            t = self.pstack.enter_context(self.nc.psum_tensor(f"{name}{i}_{self.uid}", list(shape), dt))
            b = Buf(t, name)
            b.excl = True
            bufs.append(b)
        return Ring(bufs)

    def load(self, buf, dst, src, q="sp"):
        self.dma(q, dst, src, writes=[buf], sem=buf.dsem)

    def store(self, buf, dst, src, q="sp"):
        self.dma(q, dst, src, reads=[buf], sem=buf.dsem)

    def act(self, out, in_, func, reads, writes, bias=None, scale=None, eng="act"):
        kw = {}
        if bias is not None:
            kw["bias"] = bias
        if scale is not None:
            kw["scale"] = scale
        return self.op(eng, lambda e: e.activation(out=out, in_=in_, func=func, **kw), reads, writes)

    def tt(self, out, in0, in1, op, reads, writes, eng="dve"):
        return self.op(eng, lambda e: e.tensor_tensor(out=out, in0=in0, in1=in1, op=op), reads, writes)

    def ts(self, out, in0, s1, s2, op0, op1, reads, writes, eng="dve"):
        if s2 is None:
            return self.op(eng, lambda e: e.tensor_scalar(out=out, in0=in0, scalar1=s1, scalar2=None, op0=op0), reads, writes)
        return self.op(eng, lambda e: e.tensor_scalar(out=out, in0=in0, scalar1=s1, scalar2=s2, op0=op0, op1=op1), reads, writes)

    def stt(self, out, in0, scalar, in1, op0, op1, reads, writes, eng="dve"):
        return self.op(eng, lambda e: e.scalar_tensor_tensor(out=out, in0=in0, scalar=scalar, in1=in1, op0=op0, op1=op1), reads, writes)

    def copy(self, out, in_, reads, writes, eng="dve"):
        if eng == "act":
            return self.op(eng, lambda e: e.copy(out=out, in_=in_), reads, writes)
        return self.op(eng, lambda e: e.tensor_copy(out=out, in_=in_), reads, writes)


def ps_ap(b):
    return b.t[:, :]


class G:
    pass


def matmul(P, ps, out_ap, lhsT, rhs, start, stop, reads):
    P.op("pe", lambda e: e.matmul(out_ap, lhsT, rhs, start=start, stop=stop),
         reads=reads, writes=[ps], inc=stop)


def norm_res(P, Kc, TT=256):
    r = G()
    r.TT = TT
    r.hst = P.sbring("hst", 2, [128, Kc, TT], F32, dma=True)
    r.sq = P.sbring("sq", 1, [128, Kc, TT], BF16)
    r.rstd = P.sbring("rstd", 2, [128, TT], F32)
    return r


def norm_block_gen(P, g, r, src, tok0, TB, gain, hn, Kc, psring, out_f32=None):
    TT = r.TT
    nfeat = Kc * 128
    for i in range(TB // TT):
        c0 = tok0 + i * TT
        hst = r.hst.next()
        P.load(hst, hst.t[:], src[0:nfeat, c0:c0 + TT].rearrange("(c p) t -> p c t", p=128))
        sq = r.sq.next()
        P.act(sq.t[:], hst.t[:], AF.Square, [hst], [sq])
        yield
        ps = psring.next()
        for c in range(Kc):
            matmul(P, ps, ps.t[:, 0:TT], g.ones.t[:], sq.t[:, c, :], c == 0, c == Kc - 1, [g.ones, sq])
        rstd = r.rstd.next()
        P.act(rstd.t[:], ps.t[:, 0:TT], AF.Sqrt, [ps], [rstd], bias=EPS, scale=1.0 / nfeat)
        P.op("dve", lambda e: e.reciprocal(out=rstd.t[:], in_=rstd.t[:]), [rstd], [rstd])
        if out_f32 is None:
            for c in range(Kc):
                P.stt(hn.t[:, c, i * TT:(i + 1) * TT], hst.t[:, c, :], gain.t[:, c:c + 1], rstd.t[:],
                      ALU.mult, ALU.mult, [hst, gain, rstd], [hn])
        else:
            stg_ring, dst_fn = out_f32
            st = stg_ring.next()
            for c in range(Kc):
                P.stt(st.t[:, c, :], hst.t[:, c, :], gain.t[:, c:c + 1], rstd.t[:],
                      ALU.mult, ALU.mult, [hst, gain, rstd], [st])
            P.store(st, dst_fn(c0, TT), st.t[:])
        yield


def norm_block(*a, **kw):
    for _ in norm_block_gen(*a, **kw):
        pass


def gemm_fm_gen(P, act, Kc, wsrc, chunks, nt, wring, psring, post, pre=None):
    n = len(chunks)
    loaded = {}

    def ld(i):
        wb = wring.next()
        P.dma("pool", wb.t[:, 0:Kc * 128], wsrc(chunks[i]), writes=[wb], sem=wb.dsem)
        loaded[i] = wb

    pf = len(wring.b) - 1
    for i in range(min(pf, n)):
        ld(i)
    for i in range(n):
        if i + pf < n:
            ld(i + pf)
        wb = loaded.pop(i)
        for ti in range(nt):
            if pre is not None:
                pre(chunks[i], ti)
            ps = psring.next()
            for kc in range(Kc):
                matmul(P, ps, ps.t[:, :], wb.t[:, kc * 128:(kc + 1) * 128], act.t[:, kc, ti * 512:(ti + 1) * 512],
                       kc == 0, kc == Kc - 1, [wb, act])
            post(chunks[i], ti, ps)
            yield


def gemm_fm(*a, **kw):
    for _ in gemm_fm_gen(*a, **kw):
        pass


def gemm_tm(P, act, Kc, wsrc, panels, ntt, wring, psring, post):
    n = len(panels)
    loaded = {}

    def ld(i):
        wb = wring.next()
        P.dma("pool", wb.t[:, 0:Kc * 512], wsrc(panels[i]), writes=[wb], sem=wb.dsem)
        loaded[i] = wb

    ld(0)
    for i in range(n):
        if i + 1 < n:
            ld(i + 1)
        wb = loaded.pop(i)
        for tt in range(ntt):
            ps = psring.next()
            for kc in range(Kc):
                matmul(P, ps, ps.t[:, :], act.t[:, kc, tt * 128:(tt + 1) * 128], wb.t[:, kc * 512:(kc + 1) * 512],
                       kc == 0, kc == Kc - 1, [wb, act])
            post(panels[i], tt, ps)


def load_small(P, name, shape, src, dt=F32, q="sp"):
    b = P.sb(name, shape, dt, dma=True)
    P.load(b, b.t[:], src, q=q)
    return b


def phase_tables(P, g):
    INV2PI = float(1.0 / (2 * np.pi))
    C1 = 6.28125
    C2 = float(2 * np.pi - 6.28125)
    PI = float(np.pi)
    with P.phase():
        posi = load_small(P, "posi", [128, T], g.pos, dt=I32)
        posf = P.sb("posf", [128, T], F32)
        P.copy(posf.t[:], posi.t[:], [posi], [posf])
        ang = P.sb("ang", [128, T], F32)
        kf = P.sb("kf", [128, T], F32)
        ki = P.sb("ki", [128, T], I32)
        out = P.sbring("tout", 2, [128, T], F32, dma=True)
        cs = g.consts
        for (fcol, dC, dS, scol) in ((1, g.tCm, g.tSm, 3), (2, g.tCr, g.tSr, None)):
            P.ts(ang.t[:], posf.t[:], cs.t[:, fcol:fcol + 1], None, ALU.mult, None, [posf, cs], [ang])
            P.ts(kf.t[:], ang.t[:], INV2PI, None, ALU.mult, None, [ang], [kf])
            P.copy(ki.t[:], kf.t[:], [kf], [ki])
            P.copy(kf.t[:], ki.t[:], [ki], [kf])
            P.stt(ang.t[:], kf.t[:], -C1, ang.t[:], ALU.mult, ALU.add, [kf, ang], [ang])
            P.stt(ang.t[:], kf.t[:], -C2, ang.t[:], ALU.mult, ALU.add, [kf, ang], [ang])
            P.ts(ang.t[:], ang.t[:], -PI, PI, ALU.max, ALU.min, [ang], [ang])
            o = out.next()
            P.act(o.t[:], ang.t[:], AF.Sin, [ang], [o])
            if scol is not None:
                P.ts(o.t[:], o.t[:], cs.t[:, scol:scol + 1], None, ALU.mult, None, [o, cs], [o])
            P.store(o, dS, o.t[:])
            P.stt(kf.t[:], ang.t[:], -1.0, ang.t[:], ALU.mult, ALU.max, [ang], [kf])
            o = out.next()
            P.act(o.t[:], kf.t[:], AF.Sin, [kf], [o], bias=float(np.pi / 2), scale=-1.0)
            P.store(o, dC, o.t[:])


def phase_E1(P, g, L, j, hsrc):
    with P.phase():
        gain = load_small(P, "gain", [128, 16], g.mix_g[L])
        lng = load_small(P, "lng", [128, 1024], g.sgu_lng[j])
        lnb = load_small(P, "lnb", [128, 1024], g.sgu_lnb[j])
        nr = norm_res(P, 16)
        hn = P.sb("hn", [128, 16, SEQ], BF16)
        psr = P.psring("ps", 8)
        wring = P.sbring("w", 3, [128, 2048], BF16, dma=True)
        wv = P.sbring("wv", 2, [128, 16 * 512], BF16, dma=True)
        stg = P.sbring("stg", 3, [128, 512], F32, dma=True)
        stgb = P.sbring("stgb", 3, [128, 512], BF16, dma=True)
        tab = P.sbring("tab", 2, [128, SEQ], F32, dma=True)
        xv = P.sbring("xv", 2, [128, 1024], F32)
        xo = P.sbring("xo", 2, [128, 1024], BF16, dma=True)
        st6 = P.sbring("st6", 2, [128, 12], F32)
        mv = P.sbring("mv", 2, [128, 2], F32)
        tmp = P.sbring("tmp", 2, [64, 512], F32)
        for s in range(NSEQ):
            tok0 = s * SEQ
            norm_block(P, g, nr, hsrc, tok0, SEQ, gain, hn, 16, psr)
            tC = tab.next()
            P.load(tC, tC.t[:], g.tCm[:, tok0:tok0 + SEQ])
            tS = tab.next()
            P.load(tS, tS.t[:], g.tSm[:, tok0:tok0 + SEQ])

            def post(ci, ti, ps, tok0=tok0):
                cols = slice(tok0 + ti * 512, tok0 + (ti + 1) * 512)
                if ci < 4:
                    st = stg.next()
                    P.copy(st.t[:], ps.t[:], [ps], [st], eng="act")
                    P.store(st, g.cqT[ci * 128:(ci + 1) * 128, cols], st.t[:])
                elif ci < 6:
                    st = stg.next()
                    P.copy(st.t[:], ps.t[:], [ps], [st], eng="act")
                    P.store(st, g.ckvT[(ci - 4) * 128:(ci - 3) * 128, cols], st.t[:])
                else:
                    st = stgb.next()
                    P.act(st.t[:], ps.t[:], AF.Gelu_apprx_tanh, [ps], [st])
                    P.store(st, g.abT[1024 + (ci - 7) * 128:1024 + (ci - 6) * 128, cols], st.t[:])

            fm = gemm_fm_gen(P, hn, 16, lambda ci: g.w_in_fm[j, ci], [0, 1, 2, 3, 4, 5, 7, 8, 9, 10, 11, 12, 13, 14], 4, wring, psr, post)
            w0 = wv.next()
            P.dma("pool", w0.t[:], g.w_in_tm[j, 0], writes=[w0], sem=w0.dsem)
            w1 = wv.next()
            P.dma("pool", w1.t[:], g.w_in_tm[j, 1], writes=[w1], sem=w1.dsem)
            for tt in range(SEQ // 128):
                for _ in range(4):
                    next(fm, None)
                x = xv.next()
                for pi, wb2 in enumerate((w0, w1)):
                    ps = psr.next()
                    for kc in range(16):
                        matmul(P, ps, ps.t[:, :], hn.t[:, kc, tt * 128:(tt + 1) * 128], wb2.t[:, kc * 512:(kc + 1) * 512], kc == 0, kc == 15, [wb2, hn])
                    P.act(x.t[:, pi * 512:(pi + 1) * 512], ps.t[:], AF.Gelu_apprx_tanh, [ps], [x])
                s6 = st6.next()
                P.op("dve", lambda e: e.bn_stats(out=s6.t[:, 0:6], in_=x.t[:, 0:512]), [x], [s6])
                P.op("dve", lambda e: e.bn_stats(out=s6.t[:, 6:12], in_=x.t[:, 512:1024]), [x], [s6])
                m = mv.next()
                P.op("dve", lambda e: e.bn_aggr(out=m.t[:], in_=s6.t[:]), [s6], [m])
                P.act(m.t[:, 1:2], m.t[:, 1:2], AF.Sqrt, [m], [m], bias=EPS, scale=1.0)
                P.op("dve", lambda e: e.reciprocal(out=m.t[:, 1:2], in_=m.t[:, 1:2]), [m], [m])
                P.ts(x.t[:], x.t[:], m.t[:, 0:1], m.t[:, 1:2], ALU.subtract, ALU.mult, [x, m], [x])
                P.tt(x.t[:], x.t[:], lng.t[:], ALU.mult, [x, lng], [x])
                o = xo.next()
                P.tt(o.t[:], x.t[:], lnb.t[:], ALU.add, [x, lnb], [o])
                P.store(o, g.vnTM[tok0 + tt * 128:tok0 + (tt + 1) * 128, :], o.t[:])
            for _ in fm:
                pass
            wb = wring.next()
            P.dma("pool", wb.t[:], g.w_in_fm[j, 6], writes=[wb], sem=wb.dsem)
            for ti in range(4):
                cols = slice(tok0 + ti * 512, tok0 + (ti + 1) * 512)
                pa = psr.next()
                for kc in range(16):
                    matmul(P, pa, pa.t[0:64, :], wb.t[:, kc * 128:kc * 128 + 64], hn.t[:, kc, ti * 512:(ti + 1) * 512], kc == 0, kc == 15, [wb, hn])
                pb = psr.next()
                for kc in range(16):
                    matmul(P, pb, pb.t[0:64, :], wb.t[:, kc * 128 + 64:kc * 128 + 128], hn.t[:, kc, ti * 512:(ti + 1) * 512], kc == 0, kc == 15, [wb, hn])
                t1 = tmp.next()
                P.tt(t1.t[:], pa.t[0:64, :], tC.t[0:64, ti * 512:(ti + 1) * 512], ALU.mult, [pa, tC], [t1])
                t2 = tmp.next()
                P.tt(t2.t[:], pb.t[0:64, :], tS.t[0:64, ti * 512:(ti + 1) * 512], ALU.mult, [pb, tS], [t2])
                st = stgb.next()
                P.tt(st.t[0:64, :], t1.t[:], t2.t[:], ALU.add, [t1, t2], [st])
                P.store(st, g.kpeT[:, cols], st.t[0:64, :])


def phase_E2(P, g, j):
    with P.phase():
        gain = load_small(P, "gain", [128, 4], g.qn_g[j])
        nr = norm_res(P, 4)
        hn = P.sb("hn", [128, 4, SEQ], BF16)
        psr = P.psring("ps", 8)
        wring = P.sbring("w", 3, [128, 512], BF16, dma=True)
        stgb = P.sbring("stgb", 3, [128, 512], BF16, dma=True)
        tab = P.sbring("tab", 2, [128, SEQ], F32, dma=True)
        tmp = P.sbring("tmp", 3, [128, 512], F32)
        for s in range(NSEQ):
            tok0 = s * SEQ
            norm_block(P, g, nr, g.cqT, tok0, SEQ, gain, hn, 4, psr)
            tC = tab.next()
            P.load(tC, tC.t[:], g.tCm[:, tok0:tok0 + SEQ])
            tS = tab.next()
            P.load(tS, tS.t[:], g.tSm[:, tok0:tok0 + SEQ])

            def post(ci, ti, ps, tok0=tok0):
                cols = slice(tok0 + ti * 512, tok0 + (ti + 1) * 512)
                st = stgb.next()
                P.copy(st.t[:], ps.t[:], [ps], [st], eng="act")
                P.store(st, g.qnT[ci * 128:(ci + 1) * 128, cols], st.t[:])

            gemm_fm(P, hn, 4, lambda ci: g.w_q_fm[j, ci], list(range(8)), 4, wring, psr, post)
            for c in range(4):
                wa = wring.next()
                P.dma("pool", wa.t[:], g.w_q_fm[j, 8 + c], writes=[wa], sem=wa.dsem)
                wb = wring.next()
                P.dma("pool", wb.t[:], g.w_q_fm[j, 12 + c], writes=[wb], sem=wb.dsem)
                for ti in range(4):
                    cols = slice(tok0 + ti * 512, tok0 + (ti + 1) * 512)
                    tsl = slice(ti * 512, (ti + 1) * 512)
                    pa = psr.next()
                    for kc in range(4):
                        matmul(P, pa, pa.t[:, :], wa.t[:, kc * 128:(kc + 1) * 128], hn.t[:, kc, tsl], kc == 0, kc == 3, [wa, hn])
                    pb = psr.next()
                    for kc in range(4):
                        matmul(P, pb, pb.t[:, :], wb.t[:, kc * 128:(kc + 1) * 128], hn.t[:, kc, tsl], kc == 0, kc == 3, [wb, hn])
                    t1 = tmp.next()
                    P.tt(t1.t[:], pa.t[:], tC.t[:, tsl], ALU.mult, [pa, tC], [t1])
                    t2 = tmp.next()
                    P.tt(t2.t[:], pb.t[:], tS.t[:, tsl], ALU.mult, [pb, tS], [t2])
                    st = stgb.next()
                    P.tt(st.t[:], t1.t[:], t2.t[:], ALU.add, [t1, t2], [st])
                    P.store(st, g.qrT[c * 128:(c + 1) * 128, cols], st.t[:])


def phase_E3(P, g, j):
    with P.phase():
        gain = load_small(P, "gain", [128, 2], g.kvn_g[j])
        nr = norm_res(P, 2)
        hn = P.sb("hn", [128, 2, SEQ], BF16)
        psr = P.psring("ps", 8)
        wring = P.sbring("w", 3, [128, 256], BF16, dma=True)
        wv = P.sbring("wv", 2, [128, 2 * 512], BF16, dma=True)
        stgb = P.sbring("stgb", 4, [128, 512], BF16, dma=True)
        for s in range(NSEQ):
            tok0 = s * SEQ
            norm_block(P, g, nr, g.ckvT, tok0, SEQ, gain, hn, 2, psr)

            def post(ci, ti, ps, tok0=tok0):
                cols = slice(tok0 + ti * 512, tok0 + (ti + 1) * 512)
                st = stgb.next()
                P.copy(st.t[:], ps.t[:], [ps], [st], eng="act")
                P.store(st, g.knT[ci * 128:(ci + 1) * 128, cols], st.t[:])

            gemm_fm(P, hn, 2, lambda ci: g.w_kv_fm[j, ci], list(range(8)), 4, wring, psr, post)

            def postv(pi, tt, ps, tok0=tok0):
                st = stgb.next()
                P.copy(st.t[:], ps.t[:], [ps], [st], eng="act")
                P.store(st, g.vmTM[tok0 + tt * 128:tok0 + (tt + 1) * 128, pi * 512:(pi + 1) * 512], st.t[:])

            gemm_tm(P, hn, 2, lambda pi: g.w_kv_tm[j, pi], [0, 1], SEQ // 128, wv, psr, postv)


def phase_E4(P, g):
    scale = float((128 + 64) ** -0.5)
    LOOK = 2
    with P.phase():
        kpe = P.sb("kpe", [64, SEQ], BF16, dma=True)
        kn = P.sbring("kn", 2, [128, SEQ], BF16, dma=True)
        qn = P.sbring("qn", 2, [128, SEQ], BF16, dma=True)
        qr = P.sbring("qr", 2, [64, SEQ], BF16, dma=True)
        vv = P.sbring("vv", 2, [128, 16, 128], BF16, dma=True)
        pT = P.sbring("pT", 4, [128, 512], BF16)
        rec = P.sbring("rec", 2, [128, 512], F32)
        ost = P.sbring("ost", 2, [128, 512], BF16, dma=True)
        ps_s = P.psring("pss", 4)
        ps_o = P.psring("pso", 2)
        ps_d = P.psring("psd", 2)
        for s in range(NSEQ):
            tok0 = s * SEQ
            P.load(kpe, kpe.t[:], g.kpeT[:, tok0:tok0 + SEQ])
            heads = {}
            acc = {}

            def head(h, tok0=tok0):
                if h not in heads:
                    k_ = kn.next()
                    P.load(k_, k_.t[:], g.knT[h * 128:(h + 1) * 128, tok0:tok0 + SEQ])
                    q_ = qn.next()
                    P.load(q_, q_.t[:], g.qnT[h * 128:(h + 1) * 128, tok0:tok0 + SEQ])
                    r_ = qr.next()
                    P.load(r_, r_.t[:], g.qrT[h * 64:(h + 1) * 64, tok0:tok0 + SEQ])
                    v_ = vv.next()
                    P.load(v_, v_.t[:], g.vmTM[tok0:tok0 + SEQ, h * 128:(h + 1) * 128].rearrange("(b p) d -> p b d", p=128))
                    heads[h] = (k_, q_, r_, v_)
                return heads[h]

            def S(step):
                h, jq, kb = step
                k_, q_, r_, v_ = head(h)
                c0 = max(0, kb - 4 * jq) * 128
                qs = slice(jq * 512 + c0, (jq + 1) * 512)
                ks = slice(kb * 128, (kb + 1) * 128)
                pss = ps_s.next()
                matmul(P, pss, pss.t[:, c0:512], k_.t[:, ks], q_.t[:, qs], True, False, [k_, q_])
                matmul(P, pss, pss.t[:, c0:512], kpe.t[0:64, ks], r_.t[0:64, qs], False, True, [kpe, r_])
                return pss, c0

            def rest(step, pss, c0, tok0=tok0):
                h, jq, kb = step
                k_, q_, r_, v_ = heads[h]
                nkb = 4 * jq + 4
                if kb == 0:
                    acc[(h, jq)] = (ps_o.next(), ps_d.next())
                po, pd = acc[(h, jq)]
                p_ = pT.next()
                P.act(p_.t[:, c0:512], pss.t[:, c0:512], AF.Exp, [pss], [p_], scale=scale)
                if kb >= 4 * jq:
                    P.op("dve", lambda e: e.memset(p_.t[64:128, c0:c0 + 64], 0.0), [], [p_])
                last = kb == nkb - 1
                P.op("pe", lambda e: e.matmul(po.t[:, c0:512], v_.t[:, kb, :], p_.t[:, c0:512], start=(kb == 0), stop=last),
                     reads=[v_, p_], writes=[po], inc=last)
                P.op("pe", lambda e: e.matmul(pd.t[:, c0:512], g.ones.t[:], p_.t[:, c0:512], start=(kb == 0), stop=last),
                     reads=[g.ones, p_], writes=[pd], inc=True)
                if last:
                    rc = rec.next()
                    P.op("dve", lambda e: e.reciprocal(out=rc.t[:], in_=pd.t[:]), [pd], [rc])
                    o = ost.next()
                    P.tt(o.t[:], po.t[:], rc.t[:], ALU.mult, [po, rc], [o])
                    P.store(o, g.abT[h * 128:(h + 1) * 128, tok0 + jq * 512:tok0 + (jq + 1) * 512], o.t[:])
                    del acc[(h, jq)]

            steps = [(h, jq, kb) for h in range(8) for jq in range(4) for kb in range(4 * jq + 4)]
            pendq = []
            for i in range(min(LOOK, len(steps))):
                pendq.append(S(steps[i]))
            for i, st in enumerate(steps):
                if i + LOOK < len(steps):
                    pendq.append(S(steps[i + LOOK]))
                pss, c0 = pendq.pop(0)
                rest(st, pss, c0)


def phase_E5(P, g, j):
    with P.phase():
        wsf = load_small(P, "wsf", [128, 1024], g.sgu_wsT[j])
        ws = P.sb("ws", [128, 1024], BF16)
        P.copy(ws.t[:], wsf.t[:], [wsf], [ws])
        for gi in range(8):
            P.op("dve", lambda e, gi=gi: e.memset(ws.t[64:128, gi * 128:gi * 128 + 64], 0.0), [], [ws])
        bs4 = load_small(P, "bs4", [128, 8 * 512], g.sgu_bs4[j])
        vn = P.sbring("vn", 2, [128, 4, 1024], BF16, dma=True)
        ug = P.sbring("ug", 16, [128, 512], BF16, dma=True)
        tmp = P.sbring("tmp", 3, [128, 512], F32)
        ost = P.sbring("ost", 4, [128, 512], BF16, dma=True)
        psr = P.psring("ps", 8)
        for ti in range(T // 512):
            t0 = ti * 512
            v_ = vn.next()
            P.load(v_, v_.t[:], g.vnTM[t0:t0 + 512, :].rearrange("(b p) c -> p b c", p=128))
            pss_, us_ = [], []
            for gi in range(8):
                u_ = ug.next()
                P.load(u_, u_.t[:], g.abT[1024 + gi * 128:1024 + (gi + 1) * 128, t0:t0 + 512])
                us_.append(u_)
                ps = psr.next()
                for b in range(4):
                    P.op("pe", lambda e, ps=ps, v_=v_, b=b, gi=gi: e.matmul(
                        ps.t[:, b * 128:(b + 1) * 128], v_.t[:, b, gi * 128:(gi + 1) * 128], ws.t[:, gi * 128:(gi + 1) * 128],
                        start=True, stop=True), reads=[v_, ws], writes=[ps], inc=(b == 3))
                pss_.append(ps)
            for gi in range(8):
                ps = pss_[gi]
                u_ = us_[gi]
                t_ = tmp.next()
                P.tt(t_.t[:], ps.t[:], bs4.t[:, gi * 512:(gi + 1) * 512], ALU.add, [ps, bs4], [t_])
                o = ost.next()
                P.tt(o.t[:], t_.t[:], u_.t[:], ALU.mult, [t_, u_], [o])
                P.store(o, g.abT[1024 + gi * 128:1024 + (gi + 1) * 128, t0:t0 + 512], o.t[:])


def phase_proj_res(P, g, src, Kc, wfm, hsrc, hdst, TB):
    big = Kc >= 44
    with P.phase():
        acts = P.sbring("act", 2, [128, Kc, TB], BF16, dma=True)
        psr = P.psring("ps", 8)
        wring = P.sbring("w", 2 if big else 3, [128, Kc * 128], BF16, dma=True)
        hres = P.sbring("hres", 2 if big else 4, [128, 512], F32, dma=True)
        pend = {}
        nblk = T // TB
        grp = max(1, (2 << 20) // (128 * TB * 2))
        qsel = [0]

        def load_act(blk):
            a = acts.next()
            tok0 = blk * TB
            for c0 in range(0, Kc, grp):
                c1 = min(Kc, c0 + grp)
                q = "sp"
                qsel[0] += 1
                P.dma(q, a.t[:, c0:c1, :], src[c0 * 128:c1 * 128, tok0:tok0 + TB].rearrange("(c p) t -> p c t", p=128),
                      writes=[a], sem=a.dsem)
            return a

        nxt = load_act(0)
        for blk in range(nblk):
            tok0 = blk * TB
            act = nxt
            if blk + 1 < nblk:
                nxt = load_act(blk + 1)

            def pre(ci, ti, tok0=tok0):
                hr = hres.next()
                P.load(hr, hr.t[:], hsrc[ci * 128:(ci + 1) * 128, tok0 + ti * 512:tok0 + (ti + 1) * 512])
                pend[(ci, ti)] = hr

            def post(ci, ti, ps, tok0=tok0):
                hr = pend.pop((ci, ti))
                P.tt(hr.t[:], ps.t[:], hr.t[:], ALU.add, [ps, hr], [hr])
                P.store(hr, hdst[ci * 128:(ci + 1) * 128, tok0 + ti * 512:tok0 + (ti + 1) * 512], hr.t[:], q="act")

            gemm_fm(P, act, Kc, lambda ci: wfm[ci], list(range(16)), TB // 512, wring, psr, post, pre)


def phase_F1(P, g, L):
    with P.phase():
        gain = load_small(P, "gain", [128, 16], g.ffn_g[L])
        cw = load_small(P, "cw", [128, 88 * 3], g.conv_w[L])
        cb = load_small(P, "cb", [128, 88], g.conv_b[L])
        nr = norm_res(P, 16, TT=128)
        hns = [P.sb("hn0", [128, 16, SEQ], BF16), P.sb("hn1", [128, 16, SEQ], BF16)]
        psr = P.psring("ps", 8)
        wring = P.sbring("w", 4, [128, 2048], BF16, dma=True)
        asb = P.sbring("asb", 4, [128, 514], F32)
        cc = P.sbring("cc", 4, [128, 512], F32)
        gl = P.sbring("gl", 2, [128, 512], F32)
        ptmp = P.sbring("ptmp", 2, [128, 512], F32)
        ost = P.sbring("ost", 3, [128, 512], BF16, dma=True)
        norm_block(P, g, nr, g.hT, 0, SEQ, gain, hns[0], 16, psr)
        for s in range(NSEQ):
            tok0 = s * SEQ
            hn = hns[s % 2]
            nxt = norm_block_gen(P, g, nr, g.hT, tok0 + SEQ, SEQ, gain, hns[(s + 1) % 2], 16, psr) if s + 1 < NSEQ else iter(())
            loaded = {}
            it = 0

            def ld(i):
                for half in (0, 1):
                    wb = wring.next()
                    P.dma("pool", wb.t[:], g.ffn_up_fm[L, i + 44 * half], writes=[wb], sem=wb.dsem)
                    loaded[(i, half)] = wb

            ld(0)
            for i in range(44):
                if i + 1 < 44:
                    ld(i + 1)
                wbs = (loaded.pop((i, 0)), loaded.pop((i, 1)))
                prev = [None, None]
                for ti in range(4):
                    it += 1
                    if it % 4 == 2:
                        next(nxt, None)
                    cres = []
                    for half in (0, 1):
                        ci = i + 44 * half
                        wb = wbs[half]
                        ps = psr.next()
                        for kc in range(16):
                            matmul(P, ps, ps.t[:, :], wb.t[:, kc * 128:(kc + 1) * 128], hn.t[:, kc, ti * 512:(ti + 1) * 512],
                                   kc == 0, kc == 15, [wb, hn])
                        a = asb.next()
                        ve = "dve"
                        P.copy(a.t[:, 2:514], ps.t[:], [ps], [a], eng="act")
                        if ti == 0:
                            P.op(ve, lambda e, a=a: e.memset(a.t[:, 0:2], 0.0), [], [a])
                        else:
                            P.copy(a.t[:, 0:2], prev[half].t[:, 512:514], [prev[half]], [a], eng=ve)
                        prev[half] = a
                        c = cc.next()
                        P.act(c.t[:], ps.t[:], AF.Identity, [ps, cw, cb], [c],
                              bias=cb.t[:, ci:ci + 1], scale=cw.t[:, ci * 3 + 2:ci * 3 + 3])
                        P.stt(c.t[:], a.t[:, 1:513], cw.t[:, ci * 3 + 1:ci * 3 + 2], c.t[:], ALU.mult, ALU.add, [a, cw, c], [c])
                        P.stt(c.t[:], a.t[:, 0:512], cw.t[:, ci * 3:ci * 3 + 1], c.t[:], ALU.mult, ALU.add, [a, cw, c], [c])
                        cres.append(c)
                    gg = gl.next()
                    P.act(gg.t[:], cres[0].t[:], AF.Gelu_apprx_tanh, [cres[0]], [gg])
                    o = ost.next()
                    P.tt(o.t[:], gg.t[:], cres[1].t[:], ALU.mult, [gg, cres[1]], [o], eng="pool")
                    P.store(o, g.ffT[i * 128:(i + 1) * 128, tok0 + ti * 512:tok0 + (ti + 1) * 512], o.t[:])


def phase_PLE(P, g, L):
    with P.phase():
        gain = load_small(P, "gain", [128, 16], g.ple_g[L])
        nr = norm_res(P, 16, TT=128)
        hns = [P.sb("hn0", [128, 16, SEQ], BF16), P.sb("hn1", [128, 16, SEQ], BF16)]
        pb = P.sb("pb", [128, 2, SEQ], BF16, dma=True)
        psr = P.psring("ps", 8)
        wring = P.sbring("w", 3, [128, 2048], BF16, dma=True)
        wup = P.sbring("wup", 3, [128, 256], BF16, dma=True)
        hres = P.sbring("hres", 4, [128, 512], F32, dma=True)
        sg = P.sbring("sg", 3, [128, 512], F32)
        norm_block(P, g, nr, g.hT, 0, SEQ, gain, hns[0], 16, psr)
        for s in range(NSEQ):
            tok0 = s * SEQ
            hn = hns[s % 2]
            nxt = norm_block_gen(P, g, nr, g.hT, tok0 + SEQ, SEQ, gain, hns[(s + 1) % 2], 16, psr) if s + 1 < NSEQ else iter(())
            it = 0
            P.dma("pool", pb.t[:], g.pT[L, :, tok0:tok0 + SEQ].rearrange("(c p) t -> p c t", p=128), writes=[pb], sem=pb.dsem)
            loaded = {}

            def ld(i):
                wb = wring.next()
                P.dma("pool", wb.t[:], g.ple_gate_fm[L, i], writes=[wb], sem=wb.dsem)
                wu = wup.next()
                P.dma("pool", wu.t[:], g.ple_up_fm[L, i], writes=[wu], sem=wu.dsem)
                loaded[i] = (wb, wu)

            ld(0)
            ld(1)
            for i in range(16):
                if i + 2 < 16:
                    ld(i + 2)
                wb, wu = loaded.pop(i)
                for ti in range(4):
                    it += 1
                    if it % 2 == 1:
                        next(nxt, None)
                    tsl = slice(ti * 512, (ti + 1) * 512)
                    cols = slice(tok0 + ti * 512, tok0 + (ti + 1) * 512)
                    hr = hres.next()
                    P.load(hr, hr.t[:], g.hT[i * 128:(i + 1) * 128, cols], q="act")
                    ps = psr.next()
                    for kc in range(16):
                        matmul(P, ps, ps.t[:, :], wb.t[:, kc * 128:(kc + 1) * 128], hn.t[:, kc, tsl], kc == 0, kc == 15, [wb, hn])
                    ps2 = psr.next()
                    for kc in range(2):
                        matmul(P, ps2, ps2.t[:, :], wu.t[:, kc * 128:(kc + 1) * 128], pb.t[:, kc, tsl], kc == 0, kc == 1, [wu, pb])
                    s_ = sg.next()
                    P.act(s_.t[:], ps.t[:], AF.Sigmoid, [ps], [s_])
                    P.tt(s_.t[:], ps2.t[:], s_.t[:], ALU.mult, [ps2, s_], [s_])
                    P.tt(hr.t[:], s_.t[:], hr.t[:], ALU.add, [s_, hr], [hr])
                    P.store(hr, g.hT[i * 128:(i + 1) * 128, cols], hr.t[:])


def phase_O1(P, g, L, j):
    with P.phase():
        gain = load_small(P, "gain", [128, 16], g.mix_g[L])
        qd4 = load_small(P, "qd4", [128, 8 * 512], g.qd4)
        nr = norm_res(P, 16)
        hn = P.sb("hn", [128, 16, SEQ], BF16)
        psr = P.psring("ps", 8)
        wring = P.sbring("w", 4, [128, 2048], BF16, dma=True)
        wv = P.sbring("wv", 2, [128, 16 * 512], BF16, dma=True)
        stgb = P.sbring("stgb", 6, [128, 512], BF16, dma=True)
        tC = P.sb("tC", [128, SEQ], F32, dma=True)
        tS = P.sb("tS", [128, SEQ], F32, dma=True)
        tmp = P.sbring("tmp", 4, [128, 512], F32)
        for s in range(NSEQ):
            tok0 = s * SEQ
            norm_block(P, g, nr, g.hT, tok0, SEQ, gain, hn, 16, psr)
            P.load(tC, tC.t[:], g.tCr[:, tok0:tok0 + SEQ])
            P.load(tS, tS.t[:], g.tSr[:, tok0:tok0 + SEQ])
            for which in (0, 1):
                sc = 1.0 if which == 0 else 1.0 / 16.0
                dst = g.rqT if which == 0 else g.rkT
                for h in range(8):
                    c1 = which * 16 + 2 * h
                    w1 = wring.next()
                    P.dma("pool", w1.t[:], g.ret_in_fm[j, c1], writes=[w1], sem=w1.dsem)
                    w2 = wring.next()
                    P.dma("pool", w2.t[:], g.ret_in_fm[j, c1 + 1], writes=[w2], sem=w2.dsem)
                    for ti in range(4):
                        tsl = slice(ti * 512, (ti + 1) * 512)
                        cols = slice(tok0 + ti * 512, tok0 + (ti + 1) * 512)
                        p1 = psr.next()
                        for kc in range(16):
                            matmul(P, p1, p1.t[:, :], w1.t[:, kc * 128:(kc + 1) * 128], hn.t[:, kc, tsl], kc == 0, kc == 15, [w1, hn])
                        p2 = psr.next()
                        for kc in range(16):
                            matmul(P, p2, p2.t[:, :], w2.t[:, kc * 128:(kc + 1) * 128], hn.t[:, kc, tsl], kc == 0, kc == 15, [w2, hn])
                        t1 = tmp.next()
                        P.stt(t1.t[:], p1.t[:], sc, tC.t[:, tsl], ALU.mult, ALU.mult, [p1, tC], [t1])
                        t2 = tmp.next()
                        P.stt(t2.t[:], p2.t[:], sc, tS.t[:, tsl], ALU.mult, ALU.mult, [p2, tS], [t2])
                        t3 = tmp.next()
                        P.stt(t3.t[:], p2.t[:], sc, tC.t[:, tsl], ALU.mult, ALU.mult, [p2, tC], [t3])
                        t4 = tmp.next()
                        P.stt(t4.t[:], p1.t[:], sc, tS.t[:, tsl], ALU.mult, ALU.mult, [p1, tS], [t4])
                        if which == 1:
                            o1 = stgb.next()
                            P.tt(o1.t[:], t1.t[:], t2.t[:], ALU.subtract, [t1, t2], [o1])
                            P.store(o1, dst[(2 * h) * 128:(2 * h + 1) * 128, cols], o1.t[:])
                            o2 = stgb.next()
                            P.tt(o2.t[:], t3.t[:], t4.t[:], ALU.add, [t3, t4], [o2])
                            P.store(o2, dst[(2 * h + 1) * 128:(2 * h + 2) * 128, cols], o2.t[:])
                        else:
                            P.tt(t1.t[:], t1.t[:], t2.t[:], ALU.subtract, [t1, t2], [t1])
                            P.tt(t3.t[:], t3.t[:], t4.t[:], ALU.add, [t3, t4], [t3])
                            for (tx, row) in ((t1, 2 * h), (t3, 2 * h + 1)):
                                o1 = stgb.next()
                                P.copy(o1.t[:], tx.t[:], [tx], [o1], eng="act")
                                P.store(o1, g.rqT[row * 128:(row + 1) * 128, cols], o1.t[:])
                                o2 = stgb.next()
                                P.tt(o2.t[:], tx.t[:], qd4.t[:, h * 512:(h + 1) * 512], ALU.mult, [tx, qd4], [o2])
                                P.store(o2, g.rqsT[row * 128:(row + 1) * 128, cols], o2.t[:])

            def postg(ci, ti, ps, tok0=tok0):
                st = stgb.next()
                P.act(st.t[:], ps.t[:], AF.Silu, [ps], [st])
                P.store(st, g.rgT[(ci - 32) * 128:(ci - 31) * 128, tok0 + ti * 512:tok0 + (ti + 1) * 512], st.t[:])

            gemm_fm(P, hn, 16, lambda ci: g.ret_in_fm[j, ci], list(range(32, 64)), 4, wring, psr, postg)

            def postv(pi, tt, ps, tok0=tok0):
                st = stgb.next()
                P.copy(st.t[:], ps.t[:], [ps], [st], eng="act")
                P.store(st, g.rvTM[tok0 + tt * 128:tok0 + (tt + 1) * 128, pi * 512:(pi + 1) * 512], st.t[:])

            gemm_tm(P, hn, 16, lambda pi: g.ret_in_tm[j, pi], list(range(8)), SEQ // 128, wv, psr, postv)


def phase_O2(P, g, j):
    NH = 2
    with P.phase():
        DT = load_small(P, "DT", [128, 8 * 128], g.DT)
        gng = load_small(P, "gng", [128, 32], g.gn_g[j])
        gnb = load_small(P, "gnb", [128, 32], g.gn_b[j])
        cs = g.consts
        kt = P.sbring("kt", 2 * NH, [128, 2, 512], BF16, dma=True)
        qt = P.sbring("qt", 2 * NH, [128, 2, 512], BF16, dma=True)
        qst = P.sbring("qst", 2 * NH, [128, 2, 512], BF16, dma=True)
        vt = P.sbring("vt", 2 * NH, [128, 4, 512], BF16, dma=True)
        gt = P.sbring("gt", 2 * NH, [128, 4, 512], BF16, dma=True)
        rst = P.sbring("rst", 2 * NH, [128, 4, 512], BF16, dma=True)
        AT = P.sbring("AT", 4, [128, 128], BF16)
        kz = P.sbring("kz", 4, [128, 256], BF16)
        states = [P.sb(f"state{i}", [128, 2, 512], F32) for i in range(NH)]
        stbfs = [P.sb(f"stbf{i}", [128, 2, 512], BF16) for i in range(NH)]
        st6 = P.sbring("st6", 4, [128, 6], F32)
        mv = P.sbring("mv", 4, [128, 2], F32)
        xh = P.sbring("xh", 4, [128, 512], BF16)
        r1 = P.sbring("r1", 4, [128, 4, 128], F32)
        psS = P.psring("pss", NH).b
        psO = P.psring("pso", NH).b
        psU = P.psring("psu", NH).b
        psB = P.psring("psb", NH, (128, 1024), BF16).b

        def step(h, hi, b, first, lastblk, bufs):
            k_, q_, qs_, v_, g_, ro = bufs
            state = states[hi]
            stbf = stbfs[hi]
            pss, po, pu, pb = psS[hi], psO[hi], psU[hi], psB[hi]
            cd128 = float(g.cd128[h])
            bs = slice(b * 128, (b + 1) * 128)
            for dc in range(2):
                matmul(P, pss, pss.t[:, 0:128], k_.t[:, dc, bs], q_.t[:, dc, bs], dc == 0, dc == 1, [k_, q_])
            yield
            a_ = AT.next()
            P.tt(a_.t[:], pss.t[:, 0:128], DT.t[:, h * 128:(h + 1) * 128], ALU.mult, [pss, DT], [a_])
            yield
            P.op("pe", lambda e: e.matmul(po.t[:, :], a_.t[:], v_.t[:, b, :], start=True, stop=first), [a_, v_], [po], inc=first)
            if not first:
                for dc in range(2):
                    P.op("pe", lambda e: e.matmul(po.t[:, :], qs_.t[:, dc, bs], stbf.t[:, dc, :], start=False, stop=(dc == 1)),
                         [qs_, stbf], [po], inc=(dc == 1))
            if not lastblk:
                for dc in range(2):
                    P.op("pe", lambda e: e.transpose(pb.t[:, dc * 128:(dc + 1) * 128], k_.t[:, dc, bs], g.ident.t[:]),
                         [k_, g.ident], [pb], inc=(dc == 1))
            yield
            if not lastblk:
                z_ = kz.next()
                P.act(z_.t[:], pb.t[:, 0:256], AF.Identity, [pb, cs], [z_], scale=cs.t[:, 5 + h:6 + h])
            s6 = st6.next()
            P.op("dve", lambda e: e.bn_stats(out=s6.t[:], in_=po.t[:]), [po], [s6])
            m = mv.next()
            P.op("dve", lambda e: e.bn_aggr(out=m.t[:], in_=s6.t[:]), [s6], [m])
            yield
            if not lastblk:
                P.op("pe", lambda e: e.matmul(pu.t[:, :], z_.t[:, 0:128], v_.t[:, b, :], start=True, stop=True), [z_, v_], [pu], inc=True)
            P.act(m.t[:, 1:2], m.t[:, 1:2], AF.Sqrt, [m], [m], bias=EPS, scale=1.0)
            yield
            if not lastblk:
                if first:
                    P.copy(state.t[:, 0, :], pu.t[:], [pu], [state], eng="dve")
                else:
                    P.stt(state.t[:, 0, :], state.t[:, 0, :], cd128, pu.t[:], ALU.mult, ALU.add, [state, pu], [state])
            P.op("dve", lambda e: e.reciprocal(out=m.t[:, 1:2], in_=m.t[:, 1:2]), [m], [m])
            x_ = xh.next()
            P.ts(x_.t[:], po.t[:], m.t[:, 0:1], m.t[:, 1:2], ALU.subtract, ALU.mult, [po, m], [x_])
            yield
            if not lastblk:
                P.op("pe", lambda e: e.matmul(pu.t[:, :], z_.t[:, 128:256], v_.t[:, b, :], start=True, stop=True), [z_, v_], [pu], inc=True)
            for ec in range(4):
                P.op("pe", lambda e: e.transpose(pb.t[:, 256 + ec * 128:256 + (ec + 1) * 128], x_.t[:, ec * 128:(ec + 1) * 128], g.ident.t[:]),
                     [x_, g.ident], [pb], inc=(ec == 3))
            yield
            if not lastblk:
                if first:
                    P.copy(state.t[:, 1, :], pu.t[:], [pu], [state], eng="dve")
                else:
                    P.stt(state.t[:, 1, :], state.t[:, 1, :], cd128, pu.t[:], ALU.mult, ALU.add, [state, pu], [state])
            r_ = r1.next()
            for ec in range(4):
                P.act(r_.t[:, ec, :], pb.t[:, 256 + ec * 128:256 + (ec + 1) * 128], AF.Identity, [pb, gng, gnb], [r_],
                      bias=gnb.t[:, h * 4 + ec:h * 4 + ec + 1], scale=gng.t[:, h * 4 + ec:h * 4 + ec + 1])
            yield
            if not lastblk:
                P.copy(stbf.t[:], state.t[:], [state], [stbf], eng="act")
            P.tt(ro.t[:, :, bs], r_.t[:], g_.t[:, :, bs], ALU.mult, [r_, g_], [ro], eng="pool")

        for s in range(NSEQ):
            for hg in range(8 // NH):
                for ti in range(4):
                    t0 = s * SEQ + ti * 512
                    bufs = {}
                    for hi in range(NH):
                        h = hg * NH + hi
                        rows2 = slice(h * 256, (h + 1) * 256)
                        k_ = kt.next()
                        P.load(k_, k_.t[:], g.rkT[rows2, t0:t0 + 512].rearrange("(c p) t -> p c t", p=128))
                        q_ = qt.next()
                        P.load(q_, q_.t[:], g.rqT[rows2, t0:t0 + 512].rearrange("(c p) t -> p c t", p=128))
                        qs_ = qst.next()
                        P.load(qs_, qs_.t[:], g.rqsT[rows2, t0:t0 + 512].rearrange("(c p) t -> p c t", p=128))
                        v_ = vt.next()
                        P.load(v_, v_.t[:], g.rvTM[t0:t0 + 512, h * 512:(h + 1) * 512].rearrange("(b p) e -> p b e", p=128))
                        g_ = gt.next()
                        P.load(g_, g_.t[:], g.rgT[h * 512:(h + 1) * 512, t0:t0 + 512].rearrange("(c p) t -> p c t", p=128))
                        bufs[hi] = (k_, q_, qs_, v_, g_, rst.next())
                    for b in range(4):
                        gens = [step(hg * NH + hi, hi, b, ti == 0 and b == 0, ti == 3 and b == 3, bufs[hi]) for hi in range(NH)]
                        while gens:
                            for gen in list(gens):
                                try:
                                    next(gen)
                                except StopIteration:
                                    gens.remove(gen)
                    for hi in range(NH):
                        h = hg * NH + hi
                        ro = bufs[hi][5]
                        P.store(ro, g.rrT[h * 512:(h + 1) * 512, t0:t0 + 512].rearrange("(c p) t -> p c t", p=128), ro.t[:], q="act")


def phase_final(P, g):
    with P.phase():
        gain = load_small(P, "gain", [128, 16], g.fin_g)
        nr = norm_res(P, 16, TT=512)
        psr = P.psring("ps", 4)
        stg = P.sbring("fst", 2, [128, 16, nr.TT], F32, dma=True)
        norm_block(P, g, nr, g.hT, 0, T, gain, None, 16, psr,
                   out_f32=(stg, lambda c0, TT: g.yT[:, c0:c0 + TT].rearrange("(c p) t -> p c t", p=128)))


def build_program(cst, nlayers=DEPTH, dbg=None, stop=None):
    nc = bass.Bass("TRN2", target_bir_lowering=False)
    g = G()
    g.cd128 = cst["cd128"]

    def ext(name, shape, dt=F32):
        return nc.dram_tensor(name, list(shape), dt, kind="ExternalInput").ap()

    def scr(name, shape, dt):
        kind = "ExternalOutput" if (dbg is not None and name in dbg.split(",")) else "Internal"
        return nc.dram_tensor(name, list(shape), dt, kind=kind).ap()

    g.xT = ext("xT", [D, T])
    g.pT = ext("pT", [DEPTH, 256, T])
    g.pos = ext("pos", [128, T], I32)
    cin = ext("consts", [128, 16])
    identf = ext("ident", [128, 128])
    g.DT = ext("DT", [128, 1024])
    g.qd4 = ext("qd4", [128, 4096])
    g.mix_g = ext("mix_g", [DEPTH, 128, 16])
    g.ffn_g = ext("ffn_g", [DEPTH, 128, 16])
    g.ple_g = ext("ple_g", [DEPTH, 128, 16])
    g.fin_g = ext("fin_g", [128, 16])
    g.qn_g = ext("qn_g", [2, 128, 4])
    g.kvn_g = ext("kvn_g", [2, 128, 2])
    g.sgu_lng = ext("sgu_lng", [2, 128, 1024])
    g.sgu_lnb = ext("sgu_lnb", [2, 128, 1024])
    g.sgu_bs4 = ext("sgu_bs4", [2, 128, 4096])
    g.sgu_wsT = ext("sgu_wsT", [2, 128, 1024])
    g.gn_g = ext("gn_g", [2, 128, 32])
    g.gn_b = ext("gn_b", [2, 128, 32])
    g.conv_w = ext("conv_w", [DEPTH, 128, 264])
    g.conv_b = ext("conv_b", [DEPTH, 128, 88])
    g.w_in_fm = ext("w_in_fm", [2, 15, 128, 2048])
    g.w_in_tm = ext("w_in_tm", [2, 2, 128, 16 * 512])
    g.w_q_fm = ext("w_q_fm", [2, 16, 128, 512])
    g.w_kv_fm = ext("w_kv_fm", [2, 8, 128, 256])
    g.w_kv_tm = ext("w_kv_tm", [2, 2, 128, 2 * 512])
    g.w_out_fm = ext("w_out_fm", [2, 16, 128, 2048])
    g.ret_in_fm = ext("ret_in_fm", [2, 64, 128, 2048])
    g.ret_in_tm = ext("ret_in_tm", [2, 8, 128, 16 * 512])
    g.ret_out_fm = ext("ret_out_fm", [2, 16, 128, 4096])
    g.ffn_up_fm = ext("ffn_up_fm", [DEPTH, 88, 128, 2048])
    g.ffn_dn_fm = ext("ffn_dn_fm", [DEPTH, 16, 128, 5632])
    g.ple_gate_fm = ext("ple_gate_fm", [DEPTH, 16, 128, 2048])
    g.ple_up_fm = ext("ple_up_fm", [DEPTH, 16, 128, 256])

    if dbg is not None and "hT" in dbg.split(","):
        g.hT = nc.dram_tensor("hT", [D, T], F32, kind="ExternalOutput").ap()
        g.yT = None
    else:
        g.hT = nc.dram_tensor("hT", [D, T], F32, kind="Internal").ap()
        g.yT = nc.dram_tensor("yT", [D, T], F32, kind="ExternalOutput").ap() if dbg is None else None
    g.cqT = scr("cqT", [512, T], F32)
    g.ckvT = scr("ckvT", [256, T], F32)
    g.kpeT = scr("kpeT", [64, T], BF16)
    g.vnTM = scr("vnTM", [T, 1024], BF16)
    g.qnT = scr("qnT", [1024, T], BF16)
    g.qrT = scr("qrT", [512, T], BF16)
    g.knT = scr("knT", [1024, T], BF16)
    g.vmTM = scr("vmTM", [T, 1024], BF16)
    g.abT = scr("abT", [2048, T], BF16)
    g.rqT = scr("rqT", [2048, T], BF16)
    g.rqsT = scr("rqsT", [2048, T], BF16)
    g.rkT = scr("rkT", [2048, T], BF16)
    g.rvTM = scr("rvTM", [T, 4096], BF16)
    g.rgT = scr("rgT", [4096, T], BF16)
    g.rrT = scr("rrT", [4096, T], BF16)
    g.ffT = scr("ffT", [DFF, T], BF16)
    g.tCm = scr("tCm", [128, T], F32)
    g.tSm = scr("tSm", [128, T], F32)
    g.tCr = scr("tCr", [128, T], F32)
    g.tSr = scr("tSr", [128, T], F32)

    with contextlib.ExitStack() as es:
        P = Prog(nc, es)
        P.pstack = es
        g.consts = P.sb("consts", [128, 16], F32, dma=True)
        g.ones = P.sb("ones", [128, 128], BF16)
        g.ident = P.sb("ident", [128, 128], BF16, dma=True)
        P.load(g.consts, g.consts.t[:], cin)
        P.dma("pool", g.ident.t[:], identf, writes=[g.ident], sem=g.ident.dsem)
        P.op("dve", lambda e: e.memset(g.ones.t[:], 1.0), [], [g.ones])
        ndma_persist = P.dnext

        def run():
            phase_tables(P, g)
            if stop == "tables":
                return
            for L in range(nlayers):
                j = L // 2
                hsrc = g.xT if L == 0 else g.hT
                if L % 2 == 0:
                    phase_E1(P, g, L, j, hsrc)
                    if stop == f"E1_{L}":
                        return
                    phase_E2(P, g, j)
                    phase_E3(P, g, j)
                    if stop == f"E3_{L}":
                        return
                    phase_E4(P, g)
                    if stop == f"E4_{L}":
                        return
                    phase_E5(P, g, j)
                    if stop == f"E5_{L}":
                        return
                    phase_proj_res(P, g, g.abT, 16, g.w_out_fm[j], hsrc, g.hT, SEQ)
                else:
                    phase_O1(P, g, L, j)
                    if stop == f"O1_{L}":
                        return
                    phase_O2(P, g, j)
                    if stop == f"O2_{L}":
                        return
                    phase_proj_res(P, g, g.rrT, 32, g.ret_out_fm[j], g.hT, g.hT, 1024)
                if stop == f"mix_{L}":
                    return
                phase_F1(P, g, L)
                if stop == f"F1_{L}":
                    return
                phase_proj_res(P, g, g.ffT, 44, g.ffn_dn_fm[L], g.hT, g.hT, 1024)
                if stop == f"ffn_{L}":
                    return
                phase_PLE(P, g, L)
                if stop == f"ple_{L}":
                    return
            if g.yT is not None:
                phase_final(P, g)

        orig_phase = P.phase

        @contextlib.contextmanager
        def phase_keep():
            with orig_phase():
                P.dnext = ndma_persist
                yield
        P.phase = phase_keep
        run()
        P.barrier()
    return nc


def _fm(W):
    K, N = W.shape
    return np.ascontiguousarray(W.reshape(K // 128, 128, N // 128, 128).transpose(2, 1, 0, 3).reshape(N // 128, 128, K))


def _tm(W):
    K, N = W.shape
    return np.ascontiguousarray(W.reshape(K // 128, 128, N // 512, 512).transpose(2, 1, 0, 3).reshape(N // 512, 128, (K // 128) * 512))


def _pc(v):
    return np.ascontiguousarray(v.reshape(-1, 128).T)


def module_constants():
    H = 8
    log_g = np.log1p(-(2.0 ** (-5.0 - np.arange(H, dtype=np.float64))))
    gam = np.exp(log_g)
    consts = np.zeros((128, 16), np.float32)
    p = np.arange(128)
    consts[:, 0] = -np.pi
    consts[:, 1] = 10000.0 ** (-(p % 32).astype(np.float64) / 32)
    consts[:, 2] = 10000.0 ** (-p.astype(np.float64) / 128)
    consts[:, 3] = np.where((p % 64) < 32, -1.0, 1.0)
    consts[:, 4] = -1.0
    for h in range(H):
        consts[:, 5 + h] = gam[h] ** (127 - p)
    i = p[:, None]
    jj = p[None, :]
    DT = np.zeros((128, H * 128), np.float32)
    for h in range(H):
        Dm = np.where((i // 64) >= (jj // 64), gam[h] ** np.abs(i - jj).astype(np.float64), 0.0)
        DT[:, h * 128:(h + 1) * 128] = Dm.T
    qd4 = np.zeros((128, H * 512), np.float32)
    for h in range(H):
        qd4[:, h * 512:(h + 1) * 512] = (gam[h] ** ((np.arange(512) % 128) + 1.0))[None, :]
    cd128 = gam ** 128
    return dict(consts=consts, DT=DT, qd4=qd4, cd128=cd128, ident=np.eye(128, dtype=np.float32))


def prep_shared(inp):
    sh = {}
    c = module_constants()
    sh["consts"] = c["consts"]
    sh["DT"] = c["DT"]
    sh["qd4"] = c["qd4"]
    sh["ident"] = c["ident"]
    f = np.float32
    sh["mix_g"] = np.stack([_pc(v) for v in inp["mix_norm_g"]]).astype(f)
    sh["ffn_g"] = np.stack([_pc(v) for v in inp["ffn_norm_g"]]).astype(f)
    sh["ple_g"] = np.stack([_pc(v) for v in inp["ple_norm_g"]]).astype(f)
    sh["fin_g"] = _pc(inp["final_norm_g"]).astype(f)
    sh["qn_g"] = np.stack([_pc(v) for v in inp["mla_q_norm_g"]]).astype(f)
    sh["kvn_g"] = np.stack([_pc(v) for v in inp["mla_kv_norm_g"]]).astype(f)
    sh["sgu_lng"] = np.ascontiguousarray(np.broadcast_to(inp["sgu_ln_g"][:, None, :], (2, 128, 1024))).astype(f)
    sh["sgu_lnb"] = np.ascontiguousarray(np.broadcast_to(inp["sgu_ln_b"][:, None, :], (2, 128, 1024))).astype(f)
    bs = np.tile(inp["sgu_b_s"][:, :, None, :], (1, 1, 4, 1)).reshape(2, 1, 8 * 512)
    sh["sgu_bs4"] = np.ascontiguousarray(np.broadcast_to(bs, (2, 128, 4096))).astype(f)
    sh["sgu_wsT"] = np.ascontiguousarray(inp["sgu_w_s"].transpose(0, 3, 1, 2).reshape(2, 128, 1024)).astype(f)
    sh["gn_g"] = np.stack([_pc(v) for v in inp["ret_gn_g"]]).astype(f)
    sh["gn_b"] = np.stack([_pc(v) for v in inp["ret_gn_b"]]).astype(f)
    cw = inp["ffn_conv_w"]
    sh["conv_w"] = np.ascontiguousarray(cw.reshape(4, 3, 88, 128).transpose(0, 3, 2, 1).reshape(4, 128, 264)).astype(f)
    sh["conv_b"] = np.stack([_pc(v) for v in inp["ffn_conv_b"]]).astype(f)
    swap64 = np.concatenate([np.arange(32, 64), np.arange(0, 32)])
    w_in_fm, w_in_tm, w_q_fm, w_kv_fm, w_kv_tm, w_out_fm = [], [], [], [], [], []
    for j in range(2):
        W = inp["even_w_in"][j]
        kpe = W[:, 768:832]
        fmcols = np.concatenate([W[:, 0:768], kpe, kpe[:, swap64], W[:, 832:1856]], axis=1)
        w_in_fm.append(_fm(fmcols))
        w_in_tm.append(_tm(W[:, 1856:2880]))
        Wq = inp["mla_w_q_up"][j].reshape(512, 8, 192)
        nope = Wq[:, :, 0:128].reshape(512, 1024)
        rope = Wq[:, :, 128:192]
        w_q_fm.append(_fm(np.concatenate([nope, rope.reshape(512, 512), rope[:, :, swap64].reshape(512, 512)], axis=1)))
        Wkv = inp["mla_w_kv_up"][j].reshape(256, 8, 256)
        w_kv_fm.append(_fm(np.ascontiguousarray(Wkv[:, :, 0:128]).reshape(256, 1024)))
        w_kv_tm.append(_tm(np.ascontiguousarray(Wkv[:, :, 128:256]).reshape(256, 1024)))
        w_out_fm.append(_fm(inp["even_w_out"][j]))
    sh["w_in_fm"] = np.stack(w_in_fm)
    sh["w_in_tm"] = np.stack(w_in_tm)
    sh["w_q_fm"] = np.stack(w_q_fm)
    sh["w_kv_fm"] = np.stack(w_kv_fm)
    sh["w_kv_tm"] = np.stack(w_kv_tm)
    sh["w_out_fm"] = np.stack(w_out_fm)
    ret_in_fm, ret_in_tm, ret_out_fm = [], [], []
    for j in range(2):
        W = inp["ret_w_in"][j]
        ret_in_fm.append(_fm(np.concatenate([W[:, 0:4096], W[:, 8192:12288]], axis=1)))
        ret_in_tm.append(_tm(W[:, 4096:8192]))
        ret_out_fm.append(_fm(inp["ret_w_out"][j]))
    sh["ret_in_fm"] = np.stack(ret_in_fm)
    sh["ret_in_tm"] = np.stack(ret_in_tm)
    sh["ret_out_fm"] = np.stack(ret_out_fm)
    sh["ffn_up_fm"] = np.stack([_fm(inp["ffn_w_up"][L]) for L in range(DEPTH)])
    sh["ffn_dn_fm"] = np.stack([_fm(inp["ffn_w_down"][L]) for L in range(DEPTH)])
    sh["ple_gate_fm"] = np.stack([_fm(inp["ple_w_gate"][L]) for L in range(DEPTH)])
    sh["ple_up_fm"] = np.stack([_fm(inp["ple_w_up"][L]) for L in range(DEPTH)])
    return sh, c


def prep_core(inp, core):
    b0 = core * NSEQ
    x = inp["x"][b0:b0 + NSEQ]
    xT = np.ascontiguousarray(x.reshape(T, D).T)
    p = inp["p"][:, b0:b0 + NSEQ]
    pT = np.ascontiguousarray(p.reshape(DEPTH, T, 256).transpose(0, 2, 1))
    pos = np.ascontiguousarray(np.broadcast_to(inp["positions"][b0:b0 + NSEQ].reshape(1, T), (128, T))).astype(np.int32)
    return {"xT": xT.astype(np.float32), "pT": pT.astype(np.float32), "pos": pos}


def kernel(**inputs):
    inp = {k: np.asarray(v) for k, v in inputs.items()}
    sh, c = prep_shared(inp)
    nc = build_program(c)
    in_maps = []
    for core in range(NCORES):
        m = dict(sh)
        m.update(prep_core(inp, core))
        in_maps.append(m)
    res = run_bass_kernel_spmd(nc, in_maps, core_ids=list(range(NCORES)))
    out = np.empty((NCORES * NSEQ, SEQ, D), np.float32)
    for core in range(NCORES):
        yT = np.asarray(res.results[core]["yT"])
        out[core * NSEQ:(core + 1) * NSEQ] = yT.T.reshape(NSEQ, SEQ, D)
    return out
```

```python
import contextlib
import numpy as np
import concourse.bass as bass
import concourse.mybir as mybir
from concourse.bass_utils import run_bass_kernel_spmd

F32 = mybir.dt.float32
BF16 = mybir.dt.bfloat16
I32 = mybir.dt.int32
AF = mybir.ActivationFunctionType
ALU = mybir.AluOpType

D = 2048
SEQ = 2048
NSEQ = 2
T = NSEQ * SEQ
DEPTH = 4
DFF = 5632
EPS = 1e-6
NCORES = 8


class Sem:
    def __init__(self, h, key):
        self.h = h
        self.key = key
        self.count = 0


class Buf:
    def __init__(self, t, name="", dsem=None):
        self.t = t
        self.name = name
        self.lw = None
        self.rd = {}
        self.dsem = dsem
        self.excl = False


class Eng:
    def __init__(self, name, h, sem):
        self.name = name
        self.h = h
        self.sem = sem
        self.waited = {}


class Ring:
    def __init__(self, bufs):
        self.b = bufs
        self.i = 0

    def next(self):
        b = self.b[self.i % len(self.b)]
        self.i += 1
        return b


class Prog:
    def __init__(self, nc, es, ndma=60):
        self.nc = nc
        self.es = es
        self.sems = {}
        self.uid = 0

        def mk(name):
            h = es.enter_context(nc.semaphore(name))
            s = Sem(h, name)
            self.sems[name] = s
            return s

        self.E = {
            "pe": Eng("pe", nc.tensor, mk("c_pe")),
            "act": Eng("act", nc.scalar, mk("c_act")),
            "dve": Eng("dve", nc.vector, mk("c_dve")),
            "pool": Eng("pool", nc.gpsimd, mk("c_pool")),
            "sp": Eng("sp", nc.sync, None),
        }
        self.dpool = [mk(f"d{i}") for i in range(ndma)]
        self.dnext = 0
        self.pstack = None

    def _need(self, E, deps):
        for key, val in deps:
            if E.name == "pe" and E.sem is not None and key == E.sem.key:
                continue
            if E.waited.get(key, 0) >= val:
                continue
            E.h.wait_ge(self.sems[key].h, val)
            E.waited[key] = val

    def _deps(self, reads, writes, own=None):
        deps = []
        for b in reads:
            if b.lw:
                deps.append(b.lw)
            if b.excl:
                deps.extend((k, v) for k, v in b.rd.items() if k != own)
        for b in writes:
            if b.lw and b.lw[0] != own:
                deps.append(b.lw)
            deps.extend((k, v) for k, v in b.rd.items() if k != own)
        return deps

    def _mark(self, reads, writes, tk):
        for b in reads:
            if b.rd.get(tk[0], 0) < tk[1]:
                b.rd[tk[0]] = tk[1]
        for b in writes:
            b.lw = tk
            b.rd = {}

    def op(self, eng, fn, reads=(), writes=(), inc=True):
        E = self.E[eng]
        self._need(E, self._deps(reads, writes, E.sem.key))
        ins = fn(E.h)
        if inc:
            E.sem.count += 1
            ins.then_inc(E.sem.h, 1)
            tk = (E.sem.key, E.sem.count)
        else:
            tk = (E.sem.key, E.sem.count + 1)
        self._mark(reads, writes, tk)
        return ins

    def dma(self, q, out, in_, reads=(), writes=(), sem=None):
        Q = self.E[q]
        self._need(Q, self._deps(reads, writes))
        ins = Q.h.dma_start(out=out, in_=in_)
        sem.count += 16
        ins.then_inc(sem.h, 16)
        self._mark(reads, writes, (sem.key, sem.count))

    def barrier(self):
        allt = [(s.key, s.count) for s in self.sems.values() if s.count > 0]
        for E in self.E.values():
            self._need(E, allt)

    @contextlib.contextmanager
    def phase(self):
        self.dnext = 0
        with contextlib.ExitStack() as ps:
            self.pstack = ps
            yield
            self.barrier()
        self.pstack = None

    def sb(self, name, shape, dt, dma=False):
        self.uid += 1
        t = self.pstack.enter_context(self.nc.sbuf_tensor(f"{name}_{self.uid}", shape, dt))
        b = Buf(t, name)
        if dma:
            b.dsem = self.dpool[self.dnext]
            self.dnext += 1
        return b

    def sbring(self, name, n, shape, dt, dma=False):
        return Ring([self.sb(f"{name}{i}", shape, dt, dma) for i in range(n)])

    def psring(self, name, n, shape=(128, 512), dt=F32):
        bufs = []
        for i in range(n):
            self.uid += 1
            t = self.pstack.enter_context(self.nc.psum_tensor(f"{name}{i}_{self.uid}", list(shape), dt))
            b = Buf(t, name)
            b.excl = True
            bufs.append(b)
        return Ring(bufs)

    def load(self, buf, dst, src, q="sp"):
        self.dma(q, dst, src, writes=[buf], sem=buf.dsem)

    def store(self, buf, dst, src, q="sp"):
        self.dma(q, dst, src, reads=[buf], sem=buf.dsem)

    def act(self, out, in_, func, reads, writes, bias=None, scale=None, eng="act"):
        kw = {}
        if bias is not None:
            kw["bias"] = bias
        if scale is not None:
            kw["scale"] = scale
        return self.op(eng, lambda e: e.activation(out=out, in_=in_, func=func, **kw), reads, writes)

    def tt(self, out, in0, in1, op, reads, writes, eng="dve"):
        return self.op(eng, lambda e: e.tensor_tensor(out=out, in0=in0, in1=in1, op=op), reads, writes)

    def ts(self, out, in0, s1, s2, op0, op1, reads, writes, eng="dve"):
        if s2 is None:
            return self.op(eng, lambda e: e.tensor_scalar(out=out, in0=in0, scalar1=s1, scalar2=None, op0=op0), reads, writes)
        return self.op(eng, lambda e: e.tensor_scalar(out=out, in0=in0, scalar1=s1, scalar2=s2, op0=op0, op1=op1), reads, writes)

    def stt(self, out, in0, scalar, in1, op0, op1, reads, writes, eng="dve"):
        return self.op(eng, lambda e: e.scalar_tensor_tensor(out=out, in0=in0, scalar=scalar, in1=in1, op0=op0, op1=op1), reads, writes)

    def copy(self, out, in_, reads, writes, eng="dve"):
        if eng == "act":
            return self.op(eng, lambda e: e.copy(out=out, in_=in_), reads, writes)
        return self.op(eng, lambda e: e.tensor_copy(out=out, in_=in_), reads, writes)


def ps_ap(b):
    return b.t[:, :]


class G:
    pass


def matmul(P, ps, out_ap, lhsT, rhs, start, stop, reads):
    P.op("pe", lambda e: e.matmul(out_ap, lhsT, rhs, start=start, stop=stop),
         reads=reads, writes=[ps], inc=stop)


def norm_res(P, Kc, TT=256):
    r = G()
    r.TT = TT
    r.hst = P.sbring("hst", 2, [128, Kc, TT], F32, dma=True)
    r.sq = P.sbring("sq", 1, [128, Kc, TT], BF16)
    r.rstd = P.sbring("rstd", 2, [128, TT], F32)
    return r


def norm_block_gen(P, g, r, src, tok0, TB, gain, hn, Kc, psring, out_f32=None):
    TT = r.TT
    nfeat = Kc * 128
    for i in range(TB // TT):
        c0 = tok0 + i * TT
        hst = r.hst.next()
        P.load(hst, hst.t[:], src[0:nfeat, c0:c0 + TT].rearrange("(c p) t -> p c t", p=128))
        sq = r.sq.next()
        P.act(sq.t[:], hst.t[:], AF.Square, [hst], [sq])
        yield
        ps = psring.next()
        for c in range(Kc):
            matmul(P, ps, ps.t[:, 0:TT], g.ones.t[:], sq.t[:, c, :], c == 0, c == Kc - 1, [g.ones, sq])
        rstd = r.rstd.next()
        P.act(rstd.t[:], ps.t[:, 0:TT], AF.Sqrt, [ps], [rstd], bias=EPS, scale=1.0 / nfeat)
        P.op("dve", lambda e: e.reciprocal(out=rstd.t[:], in_=rstd.t[:]), [rstd], [rstd])
        if out_f32 is None:
            for c in range(Kc):
                P.stt(hn.t[:, c, i * TT:(i + 1) * TT], hst.t[:, c, :], gain.t[:, c:c + 1], rstd.t[:],
                      ALU.mult, ALU.mult, [hst, gain, rstd], [hn])
        else:
            stg_ring, dst_fn = out_f32
            st = stg_ring.next()
            for c in range(Kc):
                P.stt(st.t[:, c, :], hst.t[:, c, :], gain.t[:, c:c + 1], rstd.t[:],
                      ALU.mult, ALU.mult, [hst, gain, rstd], [st])
            P.store(st, dst_fn(c0, TT), st.t[:])
        yield


def norm_block(*a, **kw):
    for _ in norm_block_gen(*a, **kw):
        pass


def gemm_fm_gen(P, act, Kc, wsrc, chunks, nt, wring, psring, post, pre=None):
    n = len(chunks)
    loaded = {}

    def ld(i):
        wb = wring.next()
        P.dma("pool", wb.t[:, 0:Kc * 128], wsrc(chunks[i]), writes=[wb], sem=wb.dsem)
        loaded[i] = wb

    pf = len(wring.b) - 1
    for i in range(min(pf, n)):
        ld(i)
    for i in range(n):
        if i + pf < n:
            ld(i + pf)
        wb = loaded.pop(i)
        for ti in range(nt):
            if pre is not None:
                pre(chunks[i], ti)
            ps = psring.next()
            for kc in range(Kc):
                matmul(P, ps, ps.t[:, :], wb.t[:, kc * 128:(kc + 1) * 128], act.t[:, kc, ti * 512:(ti + 1) * 512],
                       kc == 0, kc == Kc - 1, [wb, act])
            post(chunks[i], ti, ps)
            yield


def gemm_fm(*a, **kw):
    for _ in gemm_fm_gen(*a, **kw):
        pass


def gemm_tm(P, act, Kc, wsrc, panels, ntt, wring, psring, post):
    n = len(panels)
    loaded = {}

    def ld(i):
        wb = wring.next()
        P.dma("pool", wb.t[:, 0:Kc * 512], wsrc(panels[i]), writes=[wb], sem=wb.dsem)
        loaded[i] = wb

    ld(0)
    for i in range(n):
        if i + 1 < n:
            ld(i + 1)
        wb = loaded.pop(i)
        for tt in range(ntt):
            ps = psring.next()
            for kc in range(Kc):
                matmul(P, ps, ps.t[:, :], act.t[:, kc, tt * 128:(tt + 1) * 128], wb.t[:, kc * 512:(kc + 1) * 512],
                       kc == 0, kc == Kc - 1, [wb, act])
            post(panels[i], tt, ps)


def load_small(P, name, shape, src, dt=F32, q="sp"):
    b = P.sb(name, shape, dt, dma=True)
    P.load(b, b.t[:], src, q=q)
    return b


def phase_tables(P, g):
    INV2PI = float(1.0 / (2 * np.pi))
    C1 = 6.28125
    C2 = float(2 * np.pi - 6.28125)
    PI = float(np.pi)
    with P.phase():
        posi = load_small(P, "posi", [128, T], g.pos, dt=I32)
        posf = P.sb("posf", [128, T], F32)
        P.copy(posf.t[:], posi.t[:], [posi], [posf])
        ang = P.sb("ang", [128, T], F32)
        kf = P.sb("kf", [128, T], F32)
        ki = P.sb("ki", [128, T], I32)
        out = P.sbring("tout", 2, [128, T], F32, dma=True)
        cs = g.consts
        for (fcol, dC, dS, scol) in ((1, g.tCm, g.tSm, 3), (2, g.tCr, g.tSr, None)):
            P.ts(ang.t[:], posf.t[:], cs.t[:, fcol:fcol + 1], None, ALU.mult, None, [posf, cs], [ang])
            P.ts(kf.t[:], ang.t[:], INV2PI, None, ALU.mult, None, [ang], [kf])
            P.copy(ki.t[:], kf.t[:], [kf], [ki])
            P.copy(kf.t[:], ki.t[:], [ki], [kf])
            P.stt(ang.t[:], kf.t[:], -C1, ang.t[:], ALU.mult, ALU.add, [kf, ang], [ang])
            P.stt(ang.t[:], kf.t[:], -C2, ang.t[:], ALU.mult, ALU.add, [kf, ang], [ang])
            P.ts(ang.t[:], ang.t[:], -PI, PI, ALU.max, ALU.min, [ang], [ang])
            o = out.next()
            P.act(o.t[:], ang.t[:], AF.Sin, [ang], [o])
            if scol is not None:
                P.ts(o.t[:], o.t[:], cs.t[:, scol:scol + 1], None, ALU.mult, None, [o, cs], [o])
            P.store(o, dS, o.t[:])
            P.stt(kf.t[:], ang.t[:], -1.0, ang.t[:], ALU.mult, ALU.max, [ang], [kf])
            o = out.next()
            P.act(o.t[:], kf.t[:], AF.Sin, [kf], [o], bias=float(np.pi / 2), scale=-1.0)
            P.store(o, dC, o.t[:])


def phase_E1(P, g, L, j, hsrc):
    with P.phase():
        gain = load_small(P, "gain", [128, 16], g.mix_g[L])
        lng = load_small(P, "lng", [128, 1024], g.sgu_lng[j])
        lnb = load_small(P, "lnb", [128, 1024], g.sgu_lnb[j])
        nr = norm_res(P, 16)
        hn = P.sb("hn", [128, 16, SEQ], BF16)
        psr = P.psring("ps", 8)
        wring = P.sbring("w", 3, [128, 2048], BF16, dma=True)
        wv = P.sbring("wv", 2, [128, 16 * 512], BF16, dma=True)
        stg = P.sbring("stg", 3, [128, 512], F32, dma=True)
        stgb = P.sbring("stgb", 3, [128, 512], BF16, dma=True)
        tab = P.sbring("tab", 2, [128, SEQ], F32, dma=True)
        xv = P.sbring("xv", 2, [128, 1024], F32)
        xo = P.sbring("xo", 2, [128, 1024], BF16, dma=True)
        st6 = P.sbring("st6", 2, [128, 12], F32)
        mv = P.sbring("mv", 2, [128, 2], F32)
        tmp = P.sbring("tmp", 2, [64, 512], F32)
        for s in range(NSEQ):
            tok0 = s * SEQ
            norm_block(P, g, nr, hsrc, tok0, SEQ, gain, hn, 16, psr)
            tC = tab.next()
            P.load(tC, tC.t[:], g.tCm[:, tok0:tok0 + SEQ])
            tS = tab.next()
            P.load(tS, tS.t[:], g.tSm[:, tok0:tok0 + SEQ])

            def post(ci, ti, ps, tok0=tok0):
                cols = slice(tok0 + ti * 512, tok0 + (ti + 1) * 512)
                if ci < 4:
                    st = stg.next()
                    P.copy(st.t[:], ps.t[:], [ps], [st], eng="act")
                    P.store(st, g.cqT[ci * 128:(ci + 1) * 128, cols], st.t[:])
                elif ci < 6:
                    st = stg.next()
                    P.copy(st.t[:], ps.t[:], [ps], [st], eng="act")
                    P.store(st, g.ckvT[(ci - 4) * 128:(ci - 3) * 128, cols], st.t[:])
                else:
                    st = stgb.next()
                    P.act(st.t[:], ps.t[:], AF.Gelu_apprx_tanh, [ps], [st])
                    P.store(st, g.abT[1024 + (ci - 7) * 128:1024 + (ci - 6) * 128, cols], st.t[:])

            fm = gemm_fm_gen(P, hn, 16, lambda ci: g.w_in_fm[j, ci], [0, 1, 2, 3, 4, 5, 7, 8, 9, 10, 11, 12, 13, 14], 4, wring, psr, post)
            w0 = wv.next()
            P.dma("pool", w0.t[:], g.w_in_tm[j, 0], writes=[w0], sem=w0.dsem)
            w1 = wv.next()
            P.dma("pool", w1.t[:], g.w_in_tm[j, 1], writes=[w1], sem=w1.dsem)
            for tt in range(SEQ // 128):
                for _ in range(4):
                    next(fm, None)
                x = xv.next()
                for pi, wb2 in enumerate((w0, w1)):
                    ps = psr.next()
                    for kc in range(16):
                        matmul(P, ps, ps.t[:, :], hn.t[:, kc, tt * 128:(tt + 1) * 128], wb2.t[:, kc * 512:(kc + 1) * 512], kc == 0, kc == 15, [wb2, hn])
                    P.act(x.t[:, pi * 512:(pi + 1) * 512], ps.t[:], AF.Gelu_apprx_tanh, [ps], [x])
                s6 = st6.next()
                P.op("dve", lambda e: e.bn_stats(out=s6.t[:, 0:6], in_=x.t[:, 0:512]), [x], [s6])
                P.op("dve", lambda e: e.bn_stats(out=s6.t[:, 6:12], in_=x.t[:, 512:1024]), [x], [s6])
                m = mv.next()
                P.op("dve", lambda e: e.bn_aggr(out=m.t[:], in_=s6.t[:]), [s6], [m])
                P.act(m.t[:, 1:2], m.t[:, 1:2], AF.Sqrt, [m], [m], bias=EPS, scale=1.0)
                P.op("dve", lambda e: e.reciprocal(out=m.t[:, 1:2], in_=m.t[:, 1:2]), [m], [m])
                P.ts(x.t[:], x.t[:], m.t[:, 0:1], m.t[:, 1:2], ALU.subtract, ALU.mult, [x, m], [x])
                P.tt(x.t[:], x.t[:], lng.t[:], ALU.mult, [x, lng], [x])
                o = xo.next()
                P.tt(o.t[:], x.t[:], lnb.t[:], ALU.add, [x, lnb], [o])
                P.store(o, g.vnTM[tok0 + tt * 128:tok0 + (tt + 1) * 128, :], o.t[:])
            for _ in fm:
                pass
            wb = wring.next()
            P.dma("pool", wb.t[:], g.w_in_fm[j, 6], writes=[wb], sem=wb.dsem)
            for ti in range(4):
                cols = slice(tok0 + ti * 512, tok0 + (ti + 1) * 512)
                pa = psr.next()
                for kc in range(16):
                    matmul(P, pa, pa.t[0:64, :], wb.t[:, kc * 128:kc * 128 + 64], hn.t[:, kc, ti * 512:(ti + 1) * 512], kc == 0, kc == 15, [wb, hn])
                pb = psr.next()
                for kc in range(16):
                    matmul(P, pb, pb.t[0:64, :], wb.t[:, kc * 128 + 64:kc * 128 + 128], hn.t[:, kc, ti * 512:(ti + 1) * 512], kc == 0, kc == 15, [wb, hn])
                t1 = tmp.next()
                P.tt(t1.t[:], pa.t[0:64, :], tC.t[0:64, ti * 512:(ti + 1) * 512], ALU.mult, [pa, tC], [t1])
                t2 = tmp.next()
                P.tt(t2.t[:], pb.t[0:64, :], tS.t[0:64, ti * 512:(ti + 1) * 512], ALU.mult, [pb, tS], [t2])
                st = stgb.next()
                P.tt(st.t[0:64, :], t1.t[:], t2.t[:], ALU.add, [t1, t2], [st])
                P.store(st, g.kpeT[:, cols], st.t[0:64, :])


def phase_E2(P, g, j):
    with P.phase():
        gain = load_small(P, "gain", [128, 4], g.qn_g[j])
        nr = norm_res(P, 4)
        hn = P.sb("hn", [128, 4, SEQ], BF16)
        psr = P.psring("ps", 8)
        wring = P.sbring("w", 3, [128, 512], BF16, dma=True)
        stgb = P.sbring("stgb", 3, [128, 512], BF16, dma=True)
        tab = P.sbring("tab", 2, [128, SEQ], F32, dma=True)
        tmp = P.sbring("tmp", 3, [128, 512], F32)
        for s in range(NSEQ):
            tok0 = s * SEQ
            norm_block(P, g, nr, g.cqT, tok0, SEQ, gain, hn, 4, psr)
            tC = tab.next()
            P.load(tC, tC.t[:], g.tCm[:, tok0:tok0 + SEQ])
            tS = tab.next()
            P.load(tS, tS.t[:], g.tSm[:, tok0:tok0 + SEQ])

            def post(ci, ti, ps, tok0=tok0):
                cols = slice(tok0 + ti * 512, tok0 + (ti + 1) * 512)
                st = stgb.next()
                P.copy(st.t[:], ps.t[:], [ps], [st], eng="act")
                P.store(st, g.qnT[ci * 128:(ci + 1) * 128, cols], st.t[:])

            gemm_fm(P, hn, 4, lambda ci: g.w_q_fm[j, ci], list(range(8)), 4, wring, psr, post)
            for c in range(4):
                wa = wring.next()
                P.dma("pool", wa.t[:], g.w_q_fm[j, 8 + c], writes=[wa], sem=wa.dsem)
                wb = wring.next()
                P.dma("pool", wb.t[:], g.w_q_fm[j, 12 + c], writes=[wb], sem=wb.dsem)
                for ti in range(4):
                    cols = slice(tok0 + ti * 512, tok0 + (ti + 1) * 512)
                    tsl = slice(ti * 512, (ti + 1) * 512)
                    pa = psr.next()
                    for kc in range(4):
                        matmul(P, pa, pa.t[:, :], wa.t[:, kc * 128:(kc + 1) * 128], hn.t[:, kc, tsl], kc == 0, kc == 3, [wa, hn])
                    pb = psr.next()
                    for kc in range(4):
                        matmul(P, pb, pb.t[:, :], wb.t[:, kc * 128:(kc + 1) * 128], hn.t[:, kc, tsl], kc == 0, kc == 3, [wb, hn])
                    t1 = tmp.next()
                    P.tt(t1.t[:], pa.t[:], tC.t[:, tsl], ALU.mult, [pa, tC], [t1])
                    t2 = tmp.next()
                    P.tt(t2.t[:], pb.t[:], tS.t[:, tsl], ALU.mult, [pb, tS], [t2])
                    st = stgb.next()
                    P.tt(st.t[:], t1.t[:], t2.t[:], ALU.add, [t1, t2], [st])
                    P.store(st, g.qrT[c * 128:(c + 1) * 128, cols], st.t[:])


def phase_E3(P, g, j):
    with P.phase():
        gain = load_small(P, "gain", [128, 2], g.kvn_g[j])
        nr = norm_res(P, 2)
        hn = P.sb("hn", [128, 2, SEQ], BF16)
        psr = P.psring("ps", 8)
        wring = P.sbring("w", 3, [128, 256], BF16, dma=True)
        wv = P.sbring("wv", 2, [128, 2 * 512], BF16, dma=True)
        stgb = P.sbring("stgb", 4, [128, 512], BF16, dma=True)
        for s in range(NSEQ):
            tok0 = s * SEQ
            norm_block(P, g, nr, g.ckvT, tok0, SEQ, gain, hn, 2, psr)

            def post(ci, ti, ps, tok0=tok0):
                cols = slice(tok0 + ti * 512, tok0 + (ti + 1) * 512)
                st = stgb.next()
                P.copy(st.t[:], ps.t[:], [ps], [st], eng="act")
                P.store(st, g.knT[ci * 128:(ci + 1) * 128, cols], st.t[:])

            gemm_fm(P, hn, 2, lambda ci: g.w_kv_fm[j, ci], list(range(8)), 4, wring, psr, post)

            def postv(pi, tt, ps, tok0=tok0):
                st = stgb.next()
                P.copy(st.t[:], ps.t[:], [ps], [st], eng="act")
                P.store(st, g.vmTM[tok0 + tt * 128:tok0 + (tt + 1) * 128, pi * 512:(pi + 1) * 512], st.t[:])

            gemm_tm(P, hn, 2, lambda pi: g.w_kv_tm[j, pi], [0, 1], SEQ // 128, wv, psr, postv)


def phase_E4(P, g):
    scale = float((128 + 64) ** -0.5)
    LOOK = 2
    with P.phase():
        kpe = P.sb("kpe", [64, SEQ], BF16, dma=True)
        kn = P.sbring("kn", 2, [128, SEQ], BF16, dma=True)
        qn = P.sbring("qn", 2, [128, SEQ], BF16, dma=True)
        qr = P.sbring("qr", 2, [64, SEQ], BF16, dma=True)
        vv = P.sbring("vv", 2, [128, 16, 128], BF16, dma=True)
        pT = P.sbring("pT", 4, [128, 512], BF16)
        rec = P.sbring("rec", 2, [128, 512], F32)
        ost = P.sbring("ost", 2, [128, 512], BF16, dma=True)
        ps_s = P.psring("pss", 4)
        ps_o = P.psring("pso", 2)
        ps_d = P.psring("psd", 2)
        for s in range(NSEQ):
            tok0 = s * SEQ
            P.load(kpe, kpe.t[:], g.kpeT[:, tok0:tok0 + SEQ])
            heads = {}
            acc = {}

            def head(h, tok0=tok0):
                if h not in heads:
                    k_ = kn.next()
                    P.load(k_, k_.t[:], g.knT[h * 128:(h + 1) * 128, tok0:tok0 + SEQ])
                    q_ = qn.next()
                    P.load(q_, q_.t[:], g.qnT[h * 128:(h + 1) * 128, tok0:tok0 + SEQ])
                    r_ = qr.next()
                    P.load(r_, r_.t[:], g.qrT[h * 64:(h + 1) * 64, tok0:tok0 + SEQ])
                    v_ = vv.next()
                    P.load(v_, v_.t[:], g.vmTM[tok0:tok0 + SEQ, h * 128:(h + 1) * 128].rearrange("(b p) d -> p b d", p=128))
                    heads[h] = (k_, q_, r_, v_)
                return heads[h]

            def S(step):
                h, jq, kb = step
                k_, q_, r_, v_ = head(h)
                c0 = max(0, kb - 4 * jq) * 128
                qs = slice(jq * 512 + c0, (jq + 1) * 512)
                ks = slice(kb * 128, (kb + 1) * 128)
                pss = ps_s.next()
                matmul(P, pss, pss.t[:, c0:512], k_.t[:, ks], q_.t[:, qs], True, False, [k_, q_])
                matmul(P, pss, pss.t[:, c0:512], kpe.t[0:64, ks], r_.t[0:64, qs], False, True, [kpe, r_])
                return pss, c0

            def rest(step, pss, c0, tok0=tok0):
                h, jq, kb = step
                k_, q_, r_, v_ = heads[h]
                nkb = 4 * jq + 4
                if kb == 0:
                    acc[(h, jq)] = (ps_o.next(), ps_d.next())
                po, pd = acc[(h, jq)]
                p_ = pT.next()
                P.act(p_.t[:, c0:512], pss.t[:, c0:512], AF.Exp, [pss], [p_], scale=scale)
                if kb >= 4 * jq:
                    P.op("dve", lambda e: e.memset(p_.t[64:128, c0:c0 + 64], 0.0), [], [p_])
                last = kb == nkb - 1
                P.op("pe", lambda e: e.matmul(po.t[:, c0:512], v_.t[:, kb, :], p_.t[:, c0:512], start=(kb == 0), stop=last),
                     reads=[v_, p_], writes=[po], inc=last)
                P.op("pe", lambda e: e.matmul(pd.t[:, c0:512], g.ones.t[:], p_.t[:, c0:512], start=(kb == 0), stop=last),
                     reads=[g.ones, p_], writes=[pd], inc=True)
                if last:
                    rc = rec.next()
                    P.op("dve", lambda e: e.reciprocal(out=rc.t[:], in_=pd.t[:]), [pd], [rc])
                    o = ost.next()
                    P.tt(o.t[:], po.t[:], rc.t[:], ALU.mult, [po, rc], [o])
                    P.store(o, g.abT[h * 128:(h + 1) * 128, tok0 + jq * 512:tok0 + (jq + 1) * 512], o.t[:])
                    del acc[(h, jq)]

            steps = [(h, jq, kb) for h in range(8) for jq in range(4) for kb in range(4 * jq + 4)]
            pendq = []
            for i in range(min(LOOK, len(steps))):
                pendq.append(S(steps[i]))
            for i, st in enumerate(steps):
                if i + LOOK < len(steps):
                    pendq.append(S(steps[i + LOOK]))
                pss, c0 = pendq.pop(0)
                rest(st, pss, c0)


def phase_E5(P, g, j):
    with P.phase():
        wsf = load_small(P, "wsf", [128, 1024], g.sgu_wsT[j])
        ws = P.sb("ws", [128, 1024], BF16)
        P.copy(ws.t[:], wsf.t[:], [wsf], [ws])
        for gi in range(8):
            P.op("dve", lambda e, gi=gi: e.memset(ws.t[64:128, gi * 128:gi * 128 + 64], 0.0), [], [ws])
        bs4 = load_small(P, "bs4", [128, 8 * 512], g.sgu_bs4[j])
        vn = P.sbring("vn", 2, [128, 4, 1024], BF16, dma=True)
        ug = P.sbring("ug", 16, [128, 512], BF16, dma=True)
        tmp = P.sbring("tmp", 3, [128, 512], F32)
        ost = P.sbring("ost", 4, [128, 512], BF16, dma=True)
        psr = P.psring("ps", 8)
        for ti in range(T // 512):
            t0 = ti * 512
            v_ = vn.next()
            P.load(v_, v_.t[:], g.vnTM[t0:t0 + 512, :].rearrange("(b p) c -> p b c", p=128))
            pss_, us_ = [], []
            for gi in range(8):
                u_ = ug.next()
                P.load(u_, u_.t[:], g.abT[1024 + gi * 128:1024 + (gi + 1) * 128, t0:t0 + 512])
                us_.append(u_)
                ps = psr.next()
                for b in range(4):
                    P.op("pe", lambda e, ps=ps, v_=v_, b=b, gi=gi: e.matmul(
                        ps.t[:, b * 128:(b + 1) * 128], v_.t[:, b, gi * 128:(gi + 1) * 128], ws.t[:, gi * 128:(gi + 1) * 128],
                        start=True, stop=True), reads=[v_, ws], writes=[ps], inc=(b == 3))
                pss_.append(ps)
            for gi in range(8):
                ps = pss_[gi]
                u_ = us_[gi]
                t_ = tmp.next()
                P.tt(t_.t[:], ps.t[:], bs4.t[:, gi * 512:(gi + 1) * 512], ALU.add, [ps, bs4], [t_])
                o = ost.next()
                P.tt(o.t[:], t_.t[:], u_.t[:], ALU.mult, [t_, u_], [o])
                P.store(o, g.abT[1024 + gi * 128:1024 + (gi + 1) * 128, t0:t0 + 512], o.t[:])


def phase_proj_res(P, g, src, Kc, wfm, hsrc, hdst, TB):
    big = Kc >= 44
    with P.phase():
        acts = P.sbring("act", 2, [128, Kc, TB], BF16, dma=True)
        psr = P.psring("ps", 8)
        wring = P.sbring("w", 2 if big else 3, [128, Kc * 128], BF16, dma=True)
        hres = P.sbring("hres", 2 if big else 4, [128, 512], F32, dma=True)
        pend = {}
        nblk = T // TB
        grp = max(1, (2 << 20) // (128 * TB * 2))
        qsel = [0]

        def load_act(blk):
            a = acts.next()
            tok0 = blk * TB
            for c0 in range(0, Kc, grp):
                c1 = min(Kc, c0 + grp)
                q = "sp"
                qsel[0] += 1
                P.dma(q, a.t[:, c0:c1, :], src[c0 * 128:c1 * 128, tok0:tok0 + TB].rearrange("(c p) t -> p c t", p=128),
                      writes=[a], sem=a.dsem)
            return a

        nxt = load_act(0)
        for blk in range(nblk):
            tok0 = blk * TB
            act = nxt
            if blk + 1 < nblk:
                nxt = load_act(blk + 1)

            def pre(ci, ti, tok0=tok0):
                hr = hres.next()
                P.load(hr, hr.t[:], hsrc[ci * 128:(ci + 1) * 128, tok0 + ti * 512:tok0 + (ti + 1) * 512])
                pend[(ci, ti)] = hr

            def post(ci, ti, ps, tok0=tok0):
                hr = pend.pop((ci, ti))
                P.tt(hr.t[:], ps.t[:], hr.t[:], ALU.add, [ps, hr], [hr])
                P.store(hr, hdst[ci * 128:(ci + 1) * 128, tok0 + ti * 512:tok0 + (ti + 1) * 512], hr.t[:], q="act")

            gemm_fm(P, act, Kc, lambda ci: wfm[ci], list(range(16)), TB // 512, wring, psr, post, pre)


def phase_F1(P, g, L):
    with P.phase():
        gain = load_small(P, "gain", [128, 16], g.ffn_g[L])
        cw = load_small(P, "cw", [128, 88 * 3], g.conv_w[L])
        cb = load_small(P, "cb", [128, 88], g.conv_b[L])
        nr = norm_res(P, 16, TT=128)
        hns = [P.sb("hn0", [128, 16, SEQ], BF16), P.sb("hn1", [128, 16, SEQ], BF16)]
        psr = P.psring("ps", 8)
        wring = P.sbring("w", 4, [128, 2048], BF16, dma=True)
        asb = P.sbring("asb", 4, [128, 514], F32)
        cc = P.sbring("cc", 4, [128, 512], F32)
        gl = P.sbring("gl", 2, [128, 512], F32)
        ptmp = P.sbring("ptmp", 2, [128, 512], F32)
        ost = P.sbring("ost", 3, [128, 512], BF16, dma=True)
        norm_block(P, g, nr, g.hT, 0, SEQ, gain, hns[0], 16, psr)
        for s in range(NSEQ):
            tok0 = s * SEQ
            hn = hns[s % 2]
            nxt = norm_block_gen(P, g, nr, g.hT, tok0 + SEQ, SEQ, gain, hns[(s + 1) % 2], 16, psr) if s + 1 < NSEQ else iter(())
            loaded = {}
            it = 0

            def ld(i):
                for half in (0, 1):
                    wb = wring.next()
                    P.dma("pool", wb.t[:], g.ffn_up_fm[L, i + 44 * half], writes=[wb], sem=wb.dsem)
                    loaded[(i, half)] = wb

            ld(0)
            for i in range(44):
                if i + 1 < 44:
                    ld(i + 1)
                wbs = (loaded.pop((i, 0)), loaded.pop((i, 1)))
                prev = [None, None]
                for ti in range(4):
                    it += 1
                    if it % 4 == 2:
                        next(nxt, None)
                    cres = []
                    for half in (0, 1):
                        ci = i + 44 * half
                        wb = wbs[half]
                        ps = psr.next()
                        for kc in range(16):
                            matmul(P, ps, ps.t[:, :], wb.t[:, kc * 128:(kc + 1) * 128], hn.t[:, kc, ti * 512:(ti + 1) * 512],
                                   kc == 0, kc == 15, [wb, hn])
                        a = asb.next()
                        ve = "dve"
                        P.copy(a.t[:, 2:514], ps.t[:], [ps], [a], eng="act")
                        if ti == 0:
                            P.op(ve, lambda e, a=a: e.memset(a.t[:, 0:2], 0.0), [], [a])
                        else:
                            P.copy(a.t[:, 0:2], prev[half].t[:, 512:514], [prev[half]], [a], eng=ve)
                        prev[half] = a
                        c = cc.next()
                        P.act(c.t[:], ps.t[:], AF.Identity, [ps, cw, cb], [c],
                              bias=cb.t[:, ci:ci + 1], scale=cw.t[:, ci * 3 + 2:ci * 3 + 3])
                        P.stt(c.t[:], a.t[:, 1:513], cw.t[:, ci * 3 + 1:ci * 3 + 2], c.t[:], ALU.mult, ALU.add, [a, cw, c], [c])
                        P.stt(c.t[:], a.t[:, 0:512], cw.t[:, ci * 3:ci * 3 + 1], c.t[:], ALU.mult, ALU.add, [a, cw, c], [c])
                        cres.append(c)
                    gg = gl.next()
                    P.act(gg.t[:], cres[0].t[:], AF.Gelu_apprx_tanh, [cres[0]], [gg])
                    o = ost.next()
                    P.tt(o.t[:], gg.t[:], cres[1].t[:], ALU.mult, [gg, cres[1]], [o], eng="pool")
                    P.store(o, g.ffT[i * 128:(i + 1) * 128, tok0 + ti * 512:tok0 + (ti + 1) * 512], o.t[:])


def phase_PLE(P, g, L):
    with P.phase():
        gain = load_small(P, "gain", [128, 16], g.ple_g[L])
        nr = norm_res(P, 16, TT=128)
        hns = [P.sb("hn0", [128, 16, SEQ], BF16), P.sb("hn1", [128, 16, SEQ], BF16)]
        pb = P.sb("pb", [128, 2, SEQ], BF16, dma=True)
        psr = P.psring("ps", 8)
        wring = P.sbring("w", 3, [128, 2048], BF16, dma=True)
        wup = P.sbring("wup", 3, [128, 256], BF16, dma=True)
        hres = P.sbring("hres", 4, [128, 512], F32, dma=True)
        sg = P.sbring("sg", 3, [128, 512], F32)
        norm_block(P, g, nr, g.hT, 0, SEQ, gain, hns[0], 16, psr)
        for s in range(NSEQ):
            tok0 = s * SEQ
            hn = hns[s % 2]
            nxt = norm_block_gen(P, g, nr, g.hT, tok0 + SEQ, SEQ, gain, hns[(s + 1) % 2], 16, psr) if s + 1 < NSEQ else iter(())
            it = 0
            P.dma("pool", pb.t[:], g.pT[L, :, tok0:tok0 + SEQ].rearrange("(c p) t -> p c t", p=128), writes=[pb], sem=pb.dsem)
            loaded = {}

            def ld(i):
                wb = wring.next()
                P.dma("pool", wb.t[:], g.ple_gate_fm[L, i], writes=[wb], sem=wb.dsem)
                wu = wup.next()
                P.dma("pool", wu.t[:], g.ple_up_fm[L, i], writes=[wu], sem=wu.dsem)
                loaded[i] = (wb, wu)

            ld(0)
            ld(1)
            for i in range(16):
                if i + 2 < 16:
                    ld(i + 2)
                wb, wu = loaded.pop(i)
                for ti in range(4):
                    it += 1
                    if it % 2 == 1:
                        next(nxt, None)
                    tsl = slice(ti * 512, (ti + 1) * 512)
                    cols = slice(tok0 + ti * 512, tok0 + (ti + 1) * 512)
                    hr = hres.next()
                    P.load(hr, hr.t[:], g.hT[i * 128:(i + 1) * 128, cols], q="act")
                    ps = psr.next()
                    for kc in range(16):
                        matmul(P, ps, ps.t[:, :], wb.t[:, kc * 128:(kc + 1) * 128], hn.t[:, kc, tsl], kc == 0, kc == 15, [wb, hn])
                    ps2 = psr.next()
                    for kc in range(2):
                        matmul(P, ps2, ps2.t[:, :], wu.t[:, kc * 128:(kc + 1) * 128], pb.t[:, kc, tsl], kc == 0, kc == 1, [wu, pb])
                    s_ = sg.next()
                    P.act(s_.t[:], ps.t[:], AF.Sigmoid, [ps], [s_])
                    P.tt(s_.t[:], ps2.t[:], s_.t[:], ALU.mult, [ps2, s_], [s_])
                    P.tt(hr.t[:], s_.t[:], hr.t[:], ALU.add, [s_, hr], [hr])
                    P.store(hr, g.hT[i * 128:(i + 1) * 128, cols], hr.t[:])


def phase_O1(P, g, L, j):
    with P.phase():
        gain = load_small(P, "gain", [128, 16], g.mix_g[L])
        gng = load_small(P, "gng", [128, 32], g.gn_g[j])
        gnb = load_small(P, "gnb", [128, 32], g.gn_b[j])
        sgf = P.sbring("sgf", 3, [128, 512], F32)
        qd4 = load_small(P, "qd4", [128, 8 * 512], g.qd4)
        nr = norm_res(P, 16)
        hn = P.sb("hn", [128, 16, SEQ], BF16)
        psr = P.psring("ps", 8)
        wring = P.sbring("w", 4, [128, 2048], BF16, dma=True)
        wv = P.sbring("wv", 2, [128, 16 * 512], BF16, dma=True)
        stgb = P.sbring("stgb", 6, [128, 512], BF16, dma=True)
        tC = P.sb("tC", [128, SEQ], F32, dma=True)
        tS = P.sb("tS", [128, SEQ], F32, dma=True)
        tmp = P.sbring("tmp", 4, [128, 512], F32)
        for s in range(NSEQ):
            tok0 = s * SEQ
            norm_block(P, g, nr, g.hT, tok0, SEQ, gain, hn, 16, psr)
            P.load(tC, tC.t[:], g.tCr[:, tok0:tok0 + SEQ])
            P.load(tS, tS.t[:], g.tSr[:, tok0:tok0 + SEQ])
            for which in (0, 1):
                sc = 1.0 if which == 0 else 1.0 / 16.0
                dst = g.rqT if which == 0 else g.rkT
                for h in range(8):
                    c1 = which * 16 + 2 * h
                    w1 = wring.next()
                    P.dma("pool", w1.t[:], g.ret_in_fm[j, c1], writes=[w1], sem=w1.dsem)
                    w2 = wring.next()
                    P.dma("pool", w2.t[:], g.ret_in_fm[j, c1 + 1], writes=[w2], sem=w2.dsem)
                    for ti in range(4):
                        tsl = slice(ti * 512, (ti + 1) * 512)
                        cols = slice(tok0 + ti * 512, tok0 + (ti + 1) * 512)
                        p1 = psr.next()
                        for kc in range(16):
                            matmul(P, p1, p1.t[:, :], w1.t[:, kc * 128:(kc + 1) * 128], hn.t[:, kc, tsl], kc == 0, kc == 15, [w1, hn])
                        p2 = psr.next()
                        for kc in range(16):
                            matmul(P, p2, p2.t[:, :], w2.t[:, kc * 128:(kc + 1) * 128], hn.t[:, kc, tsl], kc == 0, kc == 15, [w2, hn])
                        t1 = tmp.next()
                        P.stt(t1.t[:], p1.t[:], sc, tC.t[:, tsl], ALU.mult, ALU.mult, [p1, tC], [t1])
                        t2 = tmp.next()
                        P.stt(t2.t[:], p2.t[:], sc, tS.t[:, tsl], ALU.mult, ALU.mult, [p2, tS], [t2])
                        t3 = tmp.next()
                        P.stt(t3.t[:], p2.t[:], sc, tC.t[:, tsl], ALU.mult, ALU.mult, [p2, tC], [t3])
                        t4 = tmp.next()
                        P.stt(t4.t[:], p1.t[:], sc, tS.t[:, tsl], ALU.mult, ALU.mult, [p1, tS], [t4])
                        if which == 1:
                            o1 = stgb.next()
                            P.tt(o1.t[:], t1.t[:], t2.t[:], ALU.subtract, [t1, t2], [o1])
                            P.store(o1, dst[(2 * h) * 128:(2 * h + 1) * 128, cols], o1.t[:])
                            o2 = stgb.next()
                            P.tt(o2.t[:], t3.t[:], t4.t[:], ALU.add, [t3, t4], [o2])
                            P.store(o2, dst[(2 * h + 1) * 128:(2 * h + 2) * 128, cols], o2.t[:])
                        else:
                            P.tt(t1.t[:], t1.t[:], t2.t[:], ALU.subtract, [t1, t2], [t1])
                            P.tt(t3.t[:], t3.t[:], t4.t[:], ALU.add, [t3, t4], [t3])
                            for (tx, row) in ((t1, 2 * h), (t3, 2 * h + 1)):
                                o1 = stgb.next()
                                P.copy(o1.t[:], tx.t[:], [tx], [o1], eng="act")
                                P.store(o1, g.rqT[row * 128:(row + 1) * 128, cols], o1.t[:])
                                o2 = stgb.next()
                                P.tt(o2.t[:], tx.t[:], qd4.t[:, h * 512:(h + 1) * 512], ALU.mult, [tx, qd4], [o2])
                                P.store(o2, g.rqsT[row * 128:(row + 1) * 128, cols], o2.t[:])

            def postg(ci, ti, ps, tok0=tok0):
                c = ci - 32
                sg_ = sgf.next()
                P.act(sg_.t[:], ps.t[:], AF.Silu, [ps], [sg_])
                st = stgb.next()
                P.ts(st.t[:], sg_.t[:], gng.t[:, c:c + 1], None, ALU.mult, None, [sg_, gng], [st])
                P.store(st, g.rgT[c * 128:(c + 1) * 128, tok0 + ti * 512:tok0 + (ti + 1) * 512], st.t[:])
                st2 = stgb.next()
                P.ts(st2.t[:], sg_.t[:], gnb.t[:, c:c + 1], None, ALU.mult, None, [sg_, gnb], [st2])
                P.store(st2, g.rg2T[c * 128:(c + 1) * 128, tok0 + ti * 512:tok0 + (ti + 1) * 512], st2.t[:])

            gemm_fm(P, hn, 16, lambda ci: g.ret_in_fm[j, ci], list(range(32, 64)), 4, wring, psr, postg)

            def postv(pi, tt, ps, tok0=tok0):
                st = stgb.next()
                P.copy(st.t[:], ps.t[:], [ps], [st], eng="act")
                P.store(st, g.rvTM[tok0 + tt * 128:tok0 + (tt + 1) * 128, pi * 512:(pi + 1) * 512], st.t[:])

            gemm_tm(P, hn, 16, lambda pi: g.ret_in_tm[j, pi], list(range(8)), SEQ // 128, wv, psr, postv)


def phase_O2(P, g, j):
    NH = 2
    with P.phase():
        DT = load_small(P, "DT", [128, 8 * 128], g.DT)
        gng = load_small(P, "gng", [128, 32], g.gn_g[j])
        gnb = load_small(P, "gnb", [128, 32], g.gn_b[j])
        cs = g.consts
        kt = P.sbring("kt", 2 * NH, [128, 2, 512], BF16, dma=True)
        qt = P.sbring("qt", 2 * NH, [128, 2, 512], BF16, dma=True)
        qst = P.sbring("qst", 2 * NH, [128, 2, 512], BF16, dma=True)
        vt = P.sbring("vt", 2 * NH, [128, 4, 512], BF16, dma=True)
        gt = P.sbring("gt", 2 * NH, [128, 4, 512], BF16, dma=True)
        gt2 = P.sbring("gt2", 2 * NH, [128, 4, 512], BF16, dma=True)
        rst = P.sbring("rst", 2 * NH, [128, 4, 512], BF16, dma=True)
        AT = P.sbring("AT", 4, [128, 128], BF16)
        kz = P.sbring("kz", 4, [128, 256], BF16)
        states = [P.sb(f"state{i}", [128, 2, 512], F32) for i in range(NH)]
        stbfs = [P.sb(f"stbf{i}", [128, 2, 512], BF16) for i in range(NH)]
        st6 = P.sbring("st6", 4, [128, 6], F32)
        mv = P.sbring("mv", 4, [128, 2], F32)
        nmr = P.sbring("nmr", 4, [128, 1], F32)
        xh = P.sbring("xh", 4, [128, 512], BF16)
        r1 = P.sbring("r1", 4, [128, 4, 128], F32)
        psS = P.psring("pss", NH).b
        psO = P.psring("pso", NH).b
        psU = P.psring("psu", NH).b
        psB = P.psring("psb", NH, (128, 1024), BF16).b

        def step(h, hi, b, first, lastblk, bufs):
            k_, q_, qs_, v_, g_, ro, g2_ = bufs
            state = states[hi]
            stbf = stbfs[hi]
            pss, po, pu, pb = psS[hi], psO[hi], psU[hi], psB[hi]
            cd128 = float(g.cd128[h])
            bs = slice(b * 128, (b + 1) * 128)
            for dc in range(2):
                matmul(P, pss, pss.t[:, 0:128], k_.t[:, dc, bs], q_.t[:, dc, bs], dc == 0, dc == 1, [k_, q_])
            yield
            a_ = AT.next()
            P.tt(a_.t[:], pss.t[:, 0:128], DT.t[:, h * 128:(h + 1) * 128], ALU.mult, [pss, DT], [a_])
            yield
            P.op("pe", lambda e: e.matmul(po.t[:, :], a_.t[:], v_.t[:, b, :], start=True, stop=first), [a_, v_], [po], inc=first)
            if not first:
                for dc in range(2):
                    P.op("pe", lambda e: e.matmul(po.t[:, :], qs_.t[:, dc, bs], stbf.t[:, dc, :], start=False, stop=(dc == 1)),
                         [qs_, stbf], [po], inc=(dc == 1))
            if not lastblk:
                for dc in range(2):
                    P.op("pe", lambda e: e.transpose(pb.t[:, dc * 128:(dc + 1) * 128], k_.t[:, dc, bs], g.ident.t[:]),
                         [k_, g.ident], [pb], inc=(dc == 1))
            yield
            if not lastblk:
                z_ = kz.next()
                P.act(z_.t[:], pb.t[:, 0:256], AF.Identity, [pb, cs], [z_], scale=cs.t[:, 5 + h:6 + h])
            s6 = st6.next()
            P.op("dve", lambda e: e.bn_stats(out=s6.t[:], in_=po.t[:]), [po], [s6])
            m = mv.next()
            P.op("dve", lambda e: e.bn_aggr(out=m.t[:], in_=s6.t[:]), [s6], [m])
            yield
            if not lastblk:
                P.op("pe", lambda e: e.matmul(pu.t[:, :], z_.t[:, 0:128], v_.t[:, b, :], start=True, stop=True), [z_, v_], [pu], inc=True)
            P.act(m.t[:, 1:2], m.t[:, 1:2], AF.Sqrt, [m], [m], bias=EPS, scale=1.0)
            yield
            if not lastblk:
                if first:
                    P.copy(state.t[:, 0, :], pu.t[:], [pu], [state], eng="dve")
                else:
                    P.stt(state.t[:, 0, :], state.t[:, 0, :], cd128, pu.t[:], ALU.mult, ALU.add, [state, pu], [state])
            P.op("dve", lambda e: e.reciprocal(out=m.t[:, 1:2], in_=m.t[:, 1:2]), [m], [m])
            nm = nmr.next()
            P.ts(nm.t[:], m.t[:, 0:1], -1.0, m.t[:, 1:2], ALU.mult, ALU.mult, [m], [nm])
            x_ = xh.next()
            P.act(x_.t[:], po.t[:], AF.Identity, [po, m, nm], [x_], bias=nm.t[:, 0:1], scale=m.t[:, 1:2])
            yield
            if not lastblk:
                P.op("pe", lambda e: e.matmul(pu.t[:, :], z_.t[:, 128:256], v_.t[:, b, :], start=True, stop=True), [z_, v_], [pu], inc=True)
            for ec in range(4):
                P.op("pe", lambda e: e.transpose(pb.t[:, 256 + ec * 128:256 + (ec + 1) * 128], x_.t[:, ec * 128:(ec + 1) * 128], g.ident.t[:]),
                     [x_, g.ident], [pb], inc=(ec == 3))
            yield
            if not lastblk:
                if first:
                    P.copy(state.t[:, 1, :], pu.t[:], [pu], [state], eng="dve")
                else:
                    P.stt(state.t[:, 1, :], state.t[:, 1, :], cd128, pu.t[:], ALU.mult, ALU.add, [state, pu], [state])
            r_ = r1.next()
            P.tt(r_.t[:], pb.t[:, 256:768].rearrange("p (c t) -> p c t", c=4), g_.t[:, :, bs], ALU.mult, [pb, g_], [r_])
            yield
            if not lastblk:
                P.copy(stbf.t[:], state.t[:], [state], [stbf], eng="act")
            P.tt(ro.t[:, :, bs], r_.t[:], g2_.t[:, :, bs], ALU.add, [r_, g2_], [ro], eng="pool")

        for s in range(NSEQ):
            for hg in range(8 // NH):
                for ti in range(4):
                    t0 = s * SEQ + ti * 512
                    bufs = {}
                    for hi in range(NH):
                        h = hg * NH + hi
                        rows2 = slice(h * 256, (h + 1) * 256)
                        k_ = kt.next()
                        P.load(k_, k_.t[:], g.rkT[rows2, t0:t0 + 512].rearrange("(c p) t -> p c t", p=128))
                        q_ = qt.next()
                        P.load(q_, q_.t[:], g.rqT[rows2, t0:t0 + 512].rearrange("(c p) t -> p c t", p=128))
                        qs_ = qst.next()
                        P.load(qs_, qs_.t[:], g.rqsT[rows2, t0:t0 + 512].rearrange("(c p) t -> p c t", p=128))
                        v_ = vt.next()
                        P.load(v_, v_.t[:], g.rvTM[t0:t0 + 512, h * 512:(h + 1) * 512].rearrange("(b p) e -> p b e", p=128))
                        g_ = gt.next()
                        P.load(g_, g_.t[:], g.rgT[h * 512:(h + 1) * 512, t0:t0 + 512].rearrange("(c p) t -> p c t", p=128))
                        g2_ = gt2.next()
                        P.load(g2_, g2_.t[:], g.rg2T[h * 512:(h + 1) * 512, t0:t0 + 512].rearrange("(c p) t -> p c t", p=128))
                        bufs[hi] = (k_, q_, qs_, v_, g_, rst.next(), g2_)
                    for b in range(4):
                        gens = [step(hg * NH + hi, hi, b, ti == 0 and b == 0, ti == 3 and b == 3, bufs[hi]) for hi in range(NH)]
                        while gens:
                            for gen in list(gens):
                                try:
                                    next(gen)
                                except StopIteration:
                                    gens.remove(gen)
                    for hi in range(NH):
                        h = hg * NH + hi
                        ro = bufs[hi][5]
                        P.store(ro, g.rrT[h * 512:(h + 1) * 512, t0:t0 + 512].rearrange("(c p) t -> p c t", p=128), ro.t[:], q="act")


def phase_final(P, g):
    with P.phase():
        gain = load_small(P, "gain", [128, 16], g.fin_g)
        nr = norm_res(P, 16, TT=512)
        psr = P.psring("ps", 4)
        stg = P.sbring("fst", 2, [128, 16, nr.TT], F32, dma=True)
        norm_block(P, g, nr, g.hT, 0, T, gain, None, 16, psr,
                   out_f32=(stg, lambda c0, TT: g.yT[:, c0:c0 + TT].rearrange("(c p) t -> p c t", p=128)))


def build_program(cst, nlayers=DEPTH, dbg=None, stop=None):
    nc = bass.Bass("TRN2", target_bir_lowering=False)
    g = G()
    g.cd128 = cst["cd128"]

    def ext(name, shape, dt=F32):
        return nc.dram_tensor(name, list(shape), dt, kind="ExternalInput").ap()

    def scr(name, shape, dt):
        kind = "ExternalOutput" if (dbg is not None and name in dbg.split(",")) else "Internal"
        return nc.dram_tensor(name, list(shape), dt, kind=kind).ap()

    g.xT = ext("xT", [D, T])
    g.pT = ext("pT", [DEPTH, 256, T])
    g.pos = ext("pos", [128, T], I32)
    cin = ext("consts", [128, 16])
    identf = ext("ident", [128, 128])
    g.DT = ext("DT", [128, 1024])
    g.qd4 = ext("qd4", [128, 4096])
    g.mix_g = ext("mix_g", [DEPTH, 128, 16])
    g.ffn_g = ext("ffn_g", [DEPTH, 128, 16])
    g.ple_g = ext("ple_g", [DEPTH, 128, 16])
    g.fin_g = ext("fin_g", [128, 16])
    g.qn_g = ext("qn_g", [2, 128, 4])
    g.kvn_g = ext("kvn_g", [2, 128, 2])
    g.sgu_lng = ext("sgu_lng", [2, 128, 1024])
    g.sgu_lnb = ext("sgu_lnb", [2, 128, 1024])
    g.sgu_bs4 = ext("sgu_bs4", [2, 128, 4096])
    g.sgu_wsT = ext("sgu_wsT", [2, 128, 1024])
    g.gn_g = ext("gn_g", [2, 128, 32])
    g.gn_b = ext("gn_b", [2, 128, 32])
    g.conv_w = ext("conv_w", [DEPTH, 128, 264])
    g.conv_b = ext("conv_b", [DEPTH, 128, 88])
    g.w_in_fm = ext("w_in_fm", [2, 15, 128, 2048])
    g.w_in_tm = ext("w_in_tm", [2, 2, 128, 16 * 512])
    g.w_q_fm = ext("w_q_fm", [2, 16, 128, 512])
    g.w_kv_fm = ext("w_kv_fm", [2, 8, 128, 256])
    g.w_kv_tm = ext("w_kv_tm", [2, 2, 128, 2 * 512])
    g.w_out_fm = ext("w_out_fm", [2, 16, 128, 2048])
    g.ret_in_fm = ext("ret_in_fm", [2, 64, 128, 2048])
    g.ret_in_tm = ext("ret_in_tm", [2, 8, 128, 16 * 512])
    g.ret_out_fm = ext("ret_out_fm", [2, 16, 128, 4096])
    g.ffn_up_fm = ext("ffn_up_fm", [DEPTH, 88, 128, 2048])
    g.ffn_dn_fm = ext("ffn_dn_fm", [DEPTH, 16, 128, 5632])
    g.ple_gate_fm = ext("ple_gate_fm", [DEPTH, 16, 128, 2048])
    g.ple_up_fm = ext("ple_up_fm", [DEPTH, 16, 128, 256])

    if dbg is not None and "hT" in dbg.split(","):
        g.hT = nc.dram_tensor("hT", [D, T], F32, kind="ExternalOutput").ap()
        g.yT = None
    else:
        g.hT = nc.dram_tensor("hT", [D, T], F32, kind="Internal").ap()
        g.yT = nc.dram_tensor("yT", [D, T], F32, kind="ExternalOutput").ap() if dbg is None else None
    g.cqT = scr("cqT", [512, T], F32)
    g.ckvT = scr("ckvT", [256, T], F32)
    g.kpeT = scr("kpeT", [64, T], BF16)
    g.vnTM = scr("vnTM", [T, 1024], BF16)
    g.qnT = scr("qnT", [1024, T], BF16)
    g.qrT = scr("qrT", [512, T], BF16)
    g.knT = scr("knT", [1024, T], BF16)
    g.vmTM = scr("vmTM", [T, 1024], BF16)
    g.abT = scr("abT", [2048, T], BF16)
    g.rqT = scr("rqT", [2048, T], BF16)
    g.rqsT = scr("rqsT", [2048, T], BF16)
    g.rkT = scr("rkT", [2048, T], BF16)
    g.rvTM = scr("rvTM", [T, 4096], BF16)
    g.rgT = scr("rgT", [4096, T], BF16)
    g.rg2T = scr("rg2T", [4096, T], BF16)
    g.rrT = scr("rrT", [4096, T], BF16)
    g.ffT = scr("ffT", [DFF, T], BF16)
    g.tCm = scr("tCm", [128, T], F32)
    g.tSm = scr("tSm", [128, T], F32)
    g.tCr = scr("tCr", [128, T], F32)
    g.tSr = scr("tSr", [128, T], F32)

    with contextlib.ExitStack() as es:
        P = Prog(nc, es)
        P.pstack = es
        g.consts = P.sb("consts", [128, 16], F32, dma=True)
        g.ones = P.sb("ones", [128, 128], BF16)
        g.ident = P.sb("ident", [128, 128], BF16, dma=True)
        P.load(g.consts, g.consts.t[:], cin)
        P.dma("pool", g.ident.t[:], identf, writes=[g.ident], sem=g.ident.dsem)
        P.op("dve", lambda e: e.memset(g.ones.t[:], 1.0), [], [g.ones])
        ndma_persist = P.dnext

        def run():
            phase_tables(P, g)
            if stop == "tables":
                return
            for L in range(nlayers):
                j = L // 2
                hsrc = g.xT if L == 0 else g.hT
                if L % 2 == 0:
                    phase_E1(P, g, L, j, hsrc)
                    if stop == f"E1_{L}":
                        return
                    phase_E2(P, g, j)
                    phase_E3(P, g, j)
                    if stop == f"E3_{L}":
                        return
                    phase_E4(P, g)
                    if stop == f"E4_{L}":
                        return
                    phase_E5(P, g, j)
                    if stop == f"E5_{L}":
                        return
                    phase_proj_res(P, g, g.abT, 16, g.w_out_fm[j], hsrc, g.hT, SEQ)
                else:
                    phase_O1(P, g, L, j)
                    if stop == f"O1_{L}":
                        return
                    phase_O2(P, g, j)
                    if stop == f"O2_{L}":
                        return
                    phase_proj_res(P, g, g.rrT, 32, g.ret_out_fm[j], g.hT, g.hT, 1024)
                if stop == f"mix_{L}":
                    return
                phase_F1(P, g, L)
                if stop == f"F1_{L}":
                    return
                phase_proj_res(P, g, g.ffT, 44, g.ffn_dn_fm[L], g.hT, g.hT, 1024)
                if stop == f"ffn_{L}":
                    return
                phase_PLE(P, g, L)
                if stop == f"ple_{L}":
                    return
            if g.yT is not None:
                phase_final(P, g)

        orig_phase = P.phase

        @contextlib.contextmanager
        def phase_keep():
            with orig_phase():
                P.dnext = ndma_persist
                yield
        P.phase = phase_keep
        run()
        P.barrier()
    return nc


def _fm(W):
    K, N = W.shape
    return np.ascontiguousarray(W.reshape(K // 128, 128, N // 128, 128).transpose(2, 1, 0, 3).reshape(N // 128, 128, K))


def _tm(W):
    K, N = W.shape
    return np.ascontiguousarray(W.reshape(K // 128, 128, N // 512, 512).transpose(2, 1, 0, 3).reshape(N // 512, 128, (K // 128) * 512))


def _pc(v):
    return np.ascontiguousarray(v.reshape(-1, 128).T)


def module_constants():
    H = 8
    log_g = np.log1p(-(2.0 ** (-5.0 - np.arange(H, dtype=np.float64))))
    gam = np.exp(log_g)
    consts = np.zeros((128, 16), np.float32)
    p = np.arange(128)
    consts[:, 0] = -np.pi
    consts[:, 1] = 10000.0 ** (-(p % 32).astype(np.float64) / 32)
    consts[:, 2] = 10000.0 ** (-p.astype(np.float64) / 128)
    consts[:, 3] = np.where((p % 64) < 32, -1.0, 1.0)
    consts[:, 4] = -1.0
    for h in range(H):
        consts[:, 5 + h] = gam[h] ** (127 - p)
    i = p[:, None]
    jj = p[None, :]
    DT = np.zeros((128, H * 128), np.float32)
    for h in range(H):
        Dm = np.where((i // 64) >= (jj // 64), gam[h] ** np.abs(i - jj).astype(np.float64), 0.0)
        DT[:, h * 128:(h + 1) * 128] = Dm.T
    qd4 = np.zeros((128, H * 512), np.float32)
    for h in range(H):
        qd4[:, h * 512:(h + 1) * 512] = (gam[h] ** ((np.arange(512) % 128) + 1.0))[None, :]
    cd128 = gam ** 128
    return dict(consts=consts, DT=DT, qd4=qd4, cd128=cd128, ident=np.eye(128, dtype=np.float32))


def prep_shared(inp):
    sh = {}
    c = module_constants()
    sh["consts"] = c["consts"]
    sh["DT"] = c["DT"]
    sh["qd4"] = c["qd4"]
    sh["ident"] = c["ident"]
    f = np.float32
    sh["mix_g"] = np.stack([_pc(v) for v in inp["mix_norm_g"]]).astype(f)
    sh["ffn_g"] = np.stack([_pc(v) for v in inp["ffn_norm_g"]]).astype(f)
    sh["ple_g"] = np.stack([_pc(v) for v in inp["ple_norm_g"]]).astype(f)
    sh["fin_g"] = _pc(inp["final_norm_g"]).astype(f)
    sh["qn_g"] = np.stack([_pc(v) for v in inp["mla_q_norm_g"]]).astype(f)
    sh["kvn_g"] = np.stack([_pc(v) for v in inp["mla_kv_norm_g"]]).astype(f)
    sh["sgu_lng"] = np.ascontiguousarray(np.broadcast_to(inp["sgu_ln_g"][:, None, :], (2, 128, 1024))).astype(f)
    sh["sgu_lnb"] = np.ascontiguousarray(np.broadcast_to(inp["sgu_ln_b"][:, None, :], (2, 128, 1024))).astype(f)
    bs = np.tile(inp["sgu_b_s"][:, :, None, :], (1, 1, 4, 1)).reshape(2, 1, 8 * 512)
    sh["sgu_bs4"] = np.ascontiguousarray(np.broadcast_to(bs, (2, 128, 4096))).astype(f)
    sh["sgu_wsT"] = np.ascontiguousarray(inp["sgu_w_s"].transpose(0, 3, 1, 2).reshape(2, 128, 1024)).astype(f)
    sh["gn_g"] = np.stack([_pc(v) for v in inp["ret_gn_g"]]).astype(f)
    sh["gn_b"] = np.stack([_pc(v) for v in inp["ret_gn_b"]]).astype(f)
    cw = inp["ffn_conv_w"]
    sh["conv_w"] = np.ascontiguousarray(cw.reshape(4, 3, 88, 128).transpose(0, 3, 2, 1).reshape(4, 128, 264)).astype(f)
    sh["conv_b"] = np.stack([_pc(v) for v in inp["ffn_conv_b"]]).astype(f)
    swap64 = np.concatenate([np.arange(32, 64), np.arange(0, 32)])
    w_in_fm, w_in_tm, w_q_fm, w_kv_fm, w_kv_tm, w_out_fm = [], [], [], [], [], []
    for j in range(2):
        W = inp["even_w_in"][j]
        kpe = W[:, 768:832]
        fmcols = np.concatenate([W[:, 0:768], kpe, kpe[:, swap64], W[:, 832:1856]], axis=1)
        w_in_fm.append(_fm(fmcols))
        w_in_tm.append(_tm(W[:, 1856:2880]))
        Wq = inp["mla_w_q_up"][j].reshape(512, 8, 192)
        nope = Wq[:, :, 0:128].reshape(512, 1024)
        rope = Wq[:, :, 128:192]
        w_q_fm.append(_fm(np.concatenate([nope, rope.reshape(512, 512), rope[:, :, swap64].reshape(512, 512)], axis=1)))
        Wkv = inp["mla_w_kv_up"][j].reshape(256, 8, 256)
        w_kv_fm.append(_fm(np.ascontiguousarray(Wkv[:, :, 0:128]).reshape(256, 1024)))
        w_kv_tm.append(_tm(np.ascontiguousarray(Wkv[:, :, 128:256]).reshape(256, 1024)))
        w_out_fm.append(_fm(inp["even_w_out"][j]))
    sh["w_in_fm"] = np.stack(w_in_fm)
    sh["w_in_tm"] = np.stack(w_in_tm)
    sh["w_q_fm"] = np.stack(w_q_fm)
    sh["w_kv_fm"] = np.stack(w_kv_fm)
    sh["w_kv_tm"] = np.stack(w_kv_tm)
    sh["w_out_fm"] = np.stack(w_out_fm)
    ret_in_fm, ret_in_tm, ret_out_fm = [], [], []
    for j in range(2):
        W = inp["ret_w_in"][j]
        ret_in_fm.append(_fm(np.concatenate([W[:, 0:4096], W[:, 8192:12288]], axis=1)))
        ret_in_tm.append(_tm(W[:, 4096:8192]))
        ret_out_fm.append(_fm(inp["ret_w_out"][j]))
    sh["ret_in_fm"] = np.stack(ret_in_fm)
    sh["ret_in_tm"] = np.stack(ret_in_tm)
    sh["ret_out_fm"] = np.stack(ret_out_fm)
    sh["ffn_up_fm"] = np.stack([_fm(inp["ffn_w_up"][L]) for L in range(DEPTH)])
    sh["ffn_dn_fm"] = np.stack([_fm(inp["ffn_w_down"][L]) for L in range(DEPTH)])
    sh["ple_gate_fm"] = np.stack([_fm(inp["ple_w_gate"][L]) for L in range(DEPTH)])
    sh["ple_up_fm"] = np.stack([_fm(inp["ple_w_up"][L]) for L in range(DEPTH)])
    return sh, c


def prep_core(inp, core):
    b0 = core * NSEQ
    x = inp["x"][b0:b0 + NSEQ]
    xT = np.ascontiguousarray(x.reshape(T, D).T)
    p = inp["p"][:, b0:b0 + NSEQ]
    pT = np.ascontiguousarray(p.reshape(DEPTH, T, 256).transpose(0, 2, 1))
    pos = np.ascontiguousarray(np.broadcast_to(inp["positions"][b0:b0 + NSEQ].reshape(1, T), (128, T))).astype(np.int32)
    return {"xT": xT.astype(np.float32), "pT": pT.astype(np.float32), "pos": pos}


def kernel(**inputs):
    inp = {k: np.asarray(v) for k, v in inputs.items()}
    sh, c = prep_shared(inp)
    nc = build_program(c)
    in_maps = []
    for core in range(NCORES):
        m = dict(sh)
        m.update(prep_core(inp, core))
        in_maps.append(m)
    res = run_bass_kernel_spmd(nc, in_maps, core_ids=list(range(NCORES)))
    out = np.empty((NCORES * NSEQ, SEQ, D), np.float32)
    for core in range(NCORES):
        yT = np.asarray(res.results[core]["yT"])
        out[core * NSEQ:(core + 1) * NSEQ] = yT.T.reshape(NSEQ, SEQ, D)
    return out
```

```python
import contextlib
import numpy as np
import concourse.bass as bass
import concourse.mybir as mybir
from concourse.bass_utils import run_bass_kernel_spmd

F32 = mybir.dt.float32
BF16 = mybir.dt.bfloat16
I32 = mybir.dt.int32
AF = mybir.ActivationFunctionType
ALU = mybir.AluOpType

D = 2048
SEQ = 2048
NSEQ = 2
T = NSEQ * SEQ
DEPTH = 4
DFF = 5632
EPS = 1e-6
NCORES = 8


class Sem:
    def __init__(self, h, key):
        self.h = h
        self.key = key
        self.count = 0


class Buf:
    def __init__(self, t, name="", dsem=None):
        self.t = t
        self.name = name
        self.lw = None
        self.rd = {}
        self.dsem = dsem
        self.excl = False


class Eng:
    def __init__(self, name, h, sem):
        self.name = name
        self.h = h
        self.sem = sem
        self.waited = {}


class Ring:
    def __init__(self, bufs):
        self.b = bufs
        self.i = 0

    def next(self):
        b = self.b[self.i % len(self.b)]
        self.i += 1
        return b


class Prog:
    def __init__(self, nc, es, ndma=60):
        self.nc = nc
        self.es = es
        self.sems = {}
        self.uid = 0

        def mk(name):
            h = es.enter_context(nc.semaphore(name))
            s = Sem(h, name)
            self.sems[name] = s
            return s

        self.E = {
            "pe": Eng("pe", nc.tensor, mk("c_pe")),
            "act": Eng("act", nc.scalar, mk("c_act")),
            "dve": Eng("dve", nc.vector, mk("c_dve")),
            "pool": Eng("pool", nc.gpsimd, mk("c_pool")),
            "sp": Eng("sp", nc.sync, None),
        }
        self.dpool = [mk(f"d{i}") for i in range(ndma)]
        self.dnext = 0
        self.pstack = None

    def _need(self, E, deps):
        for key, val in deps:
            if E.name == "pe" and E.sem is not None and key == E.sem.key:
                continue
            if E.waited.get(key, 0) >= val:
                continue
            E.h.wait_ge(self.sems[key].h, val)
            E.waited[key] = val

    def _deps(self, reads, writes, own=None):
        deps = []
        for b in reads:
            if b.lw:
                deps.append(b.lw)
            if b.excl:
                deps.extend((k, v) for k, v in b.rd.items() if k != own)
        for b in writes:
            if b.lw and b.lw[0] != own:
                deps.append(b.lw)
            deps.extend((k, v) for k, v in b.rd.items() if k != own)
        return deps

    def _mark(self, reads, writes, tk):
        for b in reads:
            if b.rd.get(tk[0], 0) < tk[1]:
                b.rd[tk[0]] = tk[1]
        for b in writes:
            b.lw = tk
            b.rd = {}

    def op(self, eng, fn, reads=(), writes=(), inc=True):
        E = self.E[eng]
        self._need(E, self._deps(reads, writes, E.sem.key))
        ins = fn(E.h)
        if inc:
            E.sem.count += 1
            ins.then_inc(E.sem.h, 1)
            tk = (E.sem.key, E.sem.count)
        else:
            tk = (E.sem.key, E.sem.count + 1)
        self._mark(reads, writes, tk)
        return ins

    def dma(self, q, out, in_, reads=(), writes=(), sem=None):
        Q = self.E[q]
        self._need(Q, self._deps(reads, writes))
        ins = Q.h.dma_start(out=out, in_=in_)
        sem.count += 16
        ins.then_inc(sem.h, 16)
        self._mark(reads, writes, (sem.key, sem.count))

    def barrier(self):
        allt = [(s.key, s.count) for s in self.sems.values() if s.count > 0]
        for E in self.E.values():
            self._need(E, allt)

    @contextlib.contextmanager
    def phase(self):
        self.dnext = 0
        with contextlib.ExitStack() as ps:
            self.pstack = ps
            yield
            self.barrier()
        self.pstack = None

    def sb(self, name, shape, dt, dma=False):
        self.uid += 1
        t = self.pstack.enter_context(self.nc.sbuf_tensor(f"{name}_{self.uid}", shape, dt))
        b = Buf(t, name)
        if dma:
            b.dsem = self.dpool[self.dnext]
            self.dnext += 1
        return b

    def sbring(self, name, n, shape, dt, dma=False):
        return Ring([self.sb(f"{name}{i}", shape, dt, dma) for i in range(n)])

    def psring(self, name, n, shape=(128, 512), dt=F32):
        bufs = []
        for i in range(n):
            self.uid += 1
            t = self.pstack.enter_context(self.nc.psum_tensor(f"{name}{i}_{self.uid}", list(shape), dt))
            b = Buf(t, name)
            b.excl = True
            bufs.append(b)
        return Ring(bufs)

    def load(self, buf, dst, src, q="sp"):
        self.dma(q, dst, src, writes=[buf], sem=buf.dsem)

    def store(self, buf, dst, src, q="sp"):
        self.dma(q, dst, src, reads=[buf], sem=buf.dsem)

    def act(self, out, in_, func, reads, writes, bias=None, scale=None, eng="act"):
        kw = {}
        if bias is not None:
            kw["bias"] = bias
        if scale is not None:
            kw["scale"] = scale
        return self.op(eng, lambda e: e.activation(out=out, in_=in_, func=func, **kw), reads, writes)

    def tt(self, out, in0, in1, op, reads, writes, eng="dve"):
        return self.op(eng, lambda e: e.tensor_tensor(out=out, in0=in0, in1=in1, op=op), reads, writes)

    def ts(self, out, in0, s1, s2, op0, op1, reads, writes, eng="dve"):
        if s2 is None:
            return self.op(eng, lambda e: e.tensor_scalar(out=out, in0=in0, scalar1=s1, scalar2=None, op0=op0), reads, writes)
        return self.op(eng, lambda e: e.tensor_scalar(out=out, in0=in0, scalar1=s1, scalar2=s2, op0=op0, op1=op1), reads, writes)

    def stt(self, out, in0, scalar, in1, op0, op1, reads, writes, eng="dve"):
        return self.op(eng, lambda e: e.scalar_tensor_tensor(out=out, in0=in0, scalar=scalar, in1=in1, op0=op0, op1=op1), reads, writes)

    def copy(self, out, in_, reads, writes, eng="dve"):
        if eng == "act":
            return self.op(eng, lambda e: e.copy(out=out, in_=in_), reads, writes)
        return self.op(eng, lambda e: e.tensor_copy(out=out, in_=in_), reads, writes)


def ps_ap(b):
    return b.t[:, :]


class G:
    pass


def matmul(P, ps, out_ap, lhsT, rhs, start, stop, reads):
    P.op("pe", lambda e: e.matmul(out_ap, lhsT, rhs, start=start, stop=stop),
         reads=reads, writes=[ps], inc=stop)


def norm_res(P, Kc, TT=256):
    r = G()
    r.TT = TT
    r.hst = P.sbring("hst", 2, [128, Kc, TT], F32, dma=True)
    r.sq = P.sbring("sq", 1, [128, Kc, TT], BF16)
    r.rstd = P.sbring("rstd", 2, [128, TT], F32)
    return r


def norm_block_gen(P, g, r, src, tok0, TB, gain, hn, Kc, psring, out_f32=None):
    TT = r.TT
    nfeat = Kc * 128
    for i in range(TB // TT):
        c0 = tok0 + i * TT
        hst = r.hst.next()
        P.load(hst, hst.t[:], src[0:nfeat, c0:c0 + TT].rearrange("(c p) t -> p c t", p=128))
        sq = r.sq.next()
        P.act(sq.t[:], hst.t[:], AF.Square, [hst], [sq])
        yield
        ps = psring.next()
        for c in range(Kc):
            matmul(P, ps, ps.t[:, 0:TT], g.ones.t[:], sq.t[:, c, :], c == 0, c == Kc - 1, [g.ones, sq])
        rstd = r.rstd.next()
        P.act(rstd.t[:], ps.t[:, 0:TT], AF.Sqrt, [ps], [rstd], bias=EPS, scale=1.0 / nfeat)
        P.op("dve", lambda e: e.reciprocal(out=rstd.t[:], in_=rstd.t[:]), [rstd], [rstd])
        if out_f32 is None:
            for c in range(Kc):
                P.stt(hn.t[:, c, i * TT:(i + 1) * TT], hst.t[:, c, :], gain.t[:, c:c + 1], rstd.t[:],
                      ALU.mult, ALU.mult, [hst, gain, rstd], [hn])
        else:
            stg_ring, dst_fn = out_f32
            st = stg_ring.next()
            for c in range(Kc):
                P.stt(st.t[:, c, :], hst.t[:, c, :], gain.t[:, c:c + 1], rstd.t[:],
                      ALU.mult, ALU.mult, [hst, gain, rstd], [st])
            P.store(st, dst_fn(c0, TT), st.t[:], q="pool")
        yield


def norm_block(*a, **kw):
    for _ in norm_block_gen(*a, **kw):
        pass


def gemm_fm_gen(P, act, Kc, wsrc, chunks, nt, wring, psring, post, pre=None):
    n = len(chunks)
    loaded = {}

    def ld(i):
        wb = wring.next()
        P.dma("pool", wb.t[:, 0:Kc * 128], wsrc(chunks[i]), writes=[wb], sem=wb.dsem)
        loaded[i] = wb

    pf = len(wring.b) - 1
    for i in range(min(pf, n)):
        ld(i)
    for i in range(n):
        if i + pf < n:
            ld(i + pf)
        wb = loaded.pop(i)
        for ti in range(nt):
            if pre is not None:
                pre(chunks[i], ti)
            ps = psring.next()
            for kc in range(Kc):
                matmul(P, ps, ps.t[:, :], wb.t[:, kc * 128:(kc + 1) * 128], act.t[:, kc, ti * 512:(ti + 1) * 512],
                       kc == 0, kc == Kc - 1, [wb, act])
            post(chunks[i], ti, ps)
            yield


def gemm_fm(*a, **kw):
    for _ in gemm_fm_gen(*a, **kw):
        pass


def gemm_tm(P, act, Kc, wsrc, panels, ntt, wring, psring, post):
    n = len(panels)
    loaded = {}

    def ld(i):
        wb = wring.next()
        P.dma("pool", wb.t[:, 0:Kc * 512], wsrc(panels[i]), writes=[wb], sem=wb.dsem)
        loaded[i] = wb

    ld(0)
    for i in range(n):
        if i + 1 < n:
            ld(i + 1)
        wb = loaded.pop(i)
        for tt in range(ntt):
            ps = psring.next()
            for kc in range(Kc):
                matmul(P, ps, ps.t[:, :], act.t[:, kc, tt * 128:(tt + 1) * 128], wb.t[:, kc * 512:(kc + 1) * 512],
                       kc == 0, kc == Kc - 1, [wb, act])
            post(panels[i], tt, ps)


def load_small(P, name, shape, src, dt=F32, q="sp"):
    b = P.sb(name, shape, dt, dma=True)
    P.load(b, b.t[:], src, q=q)
    return b


def phase_tables(P, g):
    INV2PI = float(1.0 / (2 * np.pi))
    C1 = 6.28125
    C2 = float(2 * np.pi - 6.28125)
    PI = float(np.pi)
    with P.phase():
        posi = load_small(P, "posi", [128, T], g.pos, dt=I32)
        posf = P.sb("posf", [128, T], F32)
        P.copy(posf.t[:], posi.t[:], [posi], [posf])
        ang = P.sb("ang", [128, T], F32)
        kf = P.sb("kf", [128, T], F32)
        ki = P.sb("ki", [128, T], I32)
        out = P.sbring("tout", 2, [128, T], F32, dma=True)
        cs = g.consts
        for (fcol, dC, dS, scol) in ((1, g.tCm, g.tSm, 3), (2, g.tCr, g.tSr, None)):
            P.ts(ang.t[:], posf.t[:], cs.t[:, fcol:fcol + 1], None, ALU.mult, None, [posf, cs], [ang])
            P.ts(kf.t[:], ang.t[:], INV2PI, None, ALU.mult, None, [ang], [kf])
            P.copy(ki.t[:], kf.t[:], [kf], [ki])
            P.copy(kf.t[:], ki.t[:], [ki], [kf])
            P.stt(ang.t[:], kf.t[:], -C1, ang.t[:], ALU.mult, ALU.add, [kf, ang], [ang])
            P.stt(ang.t[:], kf.t[:], -C2, ang.t[:], ALU.mult, ALU.add, [kf, ang], [ang])
            P.ts(ang.t[:], ang.t[:], -PI, PI, ALU.max, ALU.min, [ang], [ang])
            o = out.next()
            P.act(o.t[:], ang.t[:], AF.Sin, [ang], [o])
            if scol is not None:
                P.ts(o.t[:], o.t[:], cs.t[:, scol:scol + 1], None, ALU.mult, None, [o, cs], [o])
            P.store(o, dS, o.t[:])
            P.stt(kf.t[:], ang.t[:], -1.0, ang.t[:], ALU.mult, ALU.max, [ang], [kf])
            o = out.next()
            P.act(o.t[:], kf.t[:], AF.Sin, [kf], [o], bias=float(np.pi / 2), scale=-1.0)
            P.store(o, dC, o.t[:])


def phase_E1(P, g, L, j, hsrc):
    with P.phase():
        gain = load_small(P, "gain", [128, 16], g.mix_g[L])
        lng = load_small(P, "lng", [128, 1024], g.sgu_lng[j])
        lnb = load_small(P, "lnb", [128, 1024], g.sgu_lnb[j])
        nr = norm_res(P, 16)
        hn = P.sb("hn", [128, 16, SEQ], BF16)
        psr = P.psring("ps", 8)
        wring = P.sbring("w", 3, [128, 2048], BF16, dma=True)
        wv = P.sbring("wv", 2, [128, 16 * 512], BF16, dma=True)
        stg = P.sbring("stg", 3, [128, 512], F32, dma=True)
        stgb = P.sbring("stgb", 3, [128, 512], BF16, dma=True)
        tab = P.sbring("tab", 2, [128, SEQ], F32, dma=True)
        xv = P.sbring("xv", 2, [128, 1024], F32)
        xo = P.sbring("xo", 2, [128, 1024], BF16, dma=True)
        st6 = P.sbring("st6", 2, [128, 12], F32)
        mv = P.sbring("mv", 2, [128, 2], F32)
        tmp = P.sbring("tmp", 2, [64, 512], F32)
        for s in range(NSEQ):
            tok0 = s * SEQ
            norm_block(P, g, nr, hsrc, tok0, SEQ, gain, hn, 16, psr)
            tC = tab.next()
            P.load(tC, tC.t[:], g.tCm[:, tok0:tok0 + SEQ])
            tS = tab.next()
            P.load(tS, tS.t[:], g.tSm[:, tok0:tok0 + SEQ])

            def post(ci, ti, ps, tok0=tok0):
                cols = slice(tok0 + ti * 512, tok0 + (ti + 1) * 512)
                if ci < 4:
                    st = stg.next()
                    P.copy(st.t[:], ps.t[:], [ps], [st], eng="act")
                    P.store(st, g.cqT[ci * 128:(ci + 1) * 128, cols], st.t[:])
                elif ci < 6:
                    st = stg.next()
                    P.copy(st.t[:], ps.t[:], [ps], [st], eng="act")
                    P.store(st, g.ckvT[(ci - 4) * 128:(ci - 3) * 128, cols], st.t[:])
                else:
                    st = stgb.next()
                    P.act(st.t[:], ps.t[:], AF.Gelu_apprx_tanh, [ps], [st])
                    P.store(st, g.abT[1024 + (ci - 7) * 128:1024 + (ci - 6) * 128, cols], st.t[:])

            fm = gemm_fm_gen(P, hn, 16, lambda ci: g.w_in_fm[j, ci], [0, 1, 2, 3, 4, 5, 7, 8, 9, 10, 11, 12, 13, 14], 4, wring, psr, post)
            w0 = wv.next()
            P.dma("pool", w0.t[:], g.w_in_tm[j, 0], writes=[w0], sem=w0.dsem)
            w1 = wv.next()
            P.dma("pool", w1.t[:], g.w_in_tm[j, 1], writes=[w1], sem=w1.dsem)
            for tt in range(SEQ // 128):
                for _ in range(4):
                    next(fm, None)
                x = xv.next()
                for pi, wb2 in enumerate((w0, w1)):
                    ps = psr.next()
                    for kc in range(16):
                        matmul(P, ps, ps.t[:, :], hn.t[:, kc, tt * 128:(tt + 1) * 128], wb2.t[:, kc * 512:(kc + 1) * 512], kc == 0, kc == 15, [wb2, hn])
                    P.act(x.t[:, pi * 512:(pi + 1) * 512], ps.t[:], AF.Gelu_apprx_tanh, [ps], [x])
                s6 = st6.next()
                P.op("dve", lambda e: e.bn_stats(out=s6.t[:, 0:6], in_=x.t[:, 0:512]), [x], [s6])
                P.op("dve", lambda e: e.bn_stats(out=s6.t[:, 6:12], in_=x.t[:, 512:1024]), [x], [s6])
                m = mv.next()
                P.op("dve", lambda e: e.bn_aggr(out=m.t[:], in_=s6.t[:]), [s6], [m])
                P.act(m.t[:, 1:2], m.t[:, 1:2], AF.Sqrt, [m], [m], bias=EPS, scale=1.0)
                P.op("dve", lambda e: e.reciprocal(out=m.t[:, 1:2], in_=m.t[:, 1:2]), [m], [m])
                P.ts(x.t[:], x.t[:], m.t[:, 0:1], m.t[:, 1:2], ALU.subtract, ALU.mult, [x, m], [x])
                P.tt(x.t[:], x.t[:], lng.t[:], ALU.mult, [x, lng], [x])
                o = xo.next()
                P.tt(o.t[:], x.t[:], lnb.t[:], ALU.add, [x, lnb], [o])
                P.store(o, g.vnTM[tok0 + tt * 128:tok0 + (tt + 1) * 128, :], o.t[:])
            for _ in fm:
                pass
            wb = wring.next()
            P.dma("pool", wb.t[:], g.w_in_fm[j, 6], writes=[wb], sem=wb.dsem)
            for ti in range(4):
                cols = slice(tok0 + ti * 512, tok0 + (ti + 1) * 512)
                pa = psr.next()
                for kc in range(16):
                    matmul(P, pa, pa.t[0:64, :], wb.t[:, kc * 128:kc * 128 + 64], hn.t[:, kc, ti * 512:(ti + 1) * 512], kc == 0, kc == 15, [wb, hn])
                pb = psr.next()
                for kc in range(16):
                    matmul(P, pb, pb.t[0:64, :], wb.t[:, kc * 128 + 64:kc * 128 + 128], hn.t[:, kc, ti * 512:(ti + 1) * 512], kc == 0, kc == 15, [wb, hn])
                t1 = tmp.next()
                P.tt(t1.t[:], pa.t[0:64, :], tC.t[0:64, ti * 512:(ti + 1) * 512], ALU.mult, [pa, tC], [t1])
                t2 = tmp.next()
                P.tt(t2.t[:], pb.t[0:64, :], tS.t[0:64, ti * 512:(ti + 1) * 512], ALU.mult, [pb, tS], [t2])
                st = stgb.next()
                P.tt(st.t[0:64, :], t1.t[:], t2.t[:], ALU.add, [t1, t2], [st])
                P.store(st, g.kpeT[:, cols], st.t[0:64, :])


def phase_E2(P, g, j):
    with P.phase():
        gain = load_small(P, "gain", [128, 4], g.qn_g[j])
        nr = norm_res(P, 4)
        hn = P.sb("hn", [128, 4, SEQ], BF16)
        psr = P.psring("ps", 8)
        wring = P.sbring("w", 3, [128, 512], BF16, dma=True)
        stgb = P.sbring("stgb", 3, [128, 512], BF16, dma=True)
        tab = P.sbring("tab", 2, [128, SEQ], F32, dma=True)
        tmp = P.sbring("tmp", 3, [128, 512], F32)
        for s in range(NSEQ):
            tok0 = s * SEQ
            norm_block(P, g, nr, g.cqT, tok0, SEQ, gain, hn, 4, psr)
            tC = tab.next()
            P.load(tC, tC.t[:], g.tCm[:, tok0:tok0 + SEQ])
            tS = tab.next()
            P.load(tS, tS.t[:], g.tSm[:, tok0:tok0 + SEQ])

            def post(ci, ti, ps, tok0=tok0):
                cols = slice(tok0 + ti * 512, tok0 + (ti + 1) * 512)
                st = stgb.next()
                P.copy(st.t[:], ps.t[:], [ps], [st], eng="act")
                P.store(st, g.qnT[ci * 128:(ci + 1) * 128, cols], st.t[:])

            gemm_fm(P, hn, 4, lambda ci: g.w_q_fm[j, ci], list(range(8)), 4, wring, psr, post)
            for c in range(4):
                wa = wring.next()
                P.dma("pool", wa.t[:], g.w_q_fm[j, 8 + c], writes=[wa], sem=wa.dsem)
                wb = wring.next()
                P.dma("pool", wb.t[:], g.w_q_fm[j, 12 + c], writes=[wb], sem=wb.dsem)
                for ti in range(4):
                    cols = slice(tok0 + ti * 512, tok0 + (ti + 1) * 512)
                    tsl = slice(ti * 512, (ti + 1) * 512)
                    pa = psr.next()
                    for kc in range(4):
                        matmul(P, pa, pa.t[:, :], wa.t[:, kc * 128:(kc + 1) * 128], hn.t[:, kc, tsl], kc == 0, kc == 3, [wa, hn])
                    pb = psr.next()
                    for kc in range(4):
                        matmul(P, pb, pb.t[:, :], wb.t[:, kc * 128:(kc + 1) * 128], hn.t[:, kc, tsl], kc == 0, kc == 3, [wb, hn])
                    t1 = tmp.next()
                    P.tt(t1.t[:], pa.t[:], tC.t[:, tsl], ALU.mult, [pa, tC], [t1])
                    t2 = tmp.next()
                    P.tt(t2.t[:], pb.t[:], tS.t[:, tsl], ALU.mult, [pb, tS], [t2])
                    st = stgb.next()
                    P.tt(st.t[:], t1.t[:], t2.t[:], ALU.add, [t1, t2], [st])
                    P.store(st, g.qrT[c * 128:(c + 1) * 128, cols], st.t[:])


def phase_E3(P, g, j):
    with P.phase():
        gain = load_small(P, "gain", [128, 2], g.kvn_g[j])
        nr = norm_res(P, 2)
        hn = P.sb("hn", [128, 2, SEQ], BF16)
        psr = P.psring("ps", 8)
        wring = P.sbring("w", 3, [128, 256], BF16, dma=True)
        wv = P.sbring("wv", 2, [128, 2 * 512], BF16, dma=True)
        stgb = P.sbring("stgb", 4, [128, 512], BF16, dma=True)
        for s in range(NSEQ):
            tok0 = s * SEQ
            norm_block(P, g, nr, g.ckvT, tok0, SEQ, gain, hn, 2, psr)

            def post(ci, ti, ps, tok0=tok0):
                cols = slice(tok0 + ti * 512, tok0 + (ti + 1) * 512)
                st = stgb.next()
                P.copy(st.t[:], ps.t[:], [ps], [st], eng="act")
                P.store(st, g.knT[ci * 128:(ci + 1) * 128, cols], st.t[:])

            gemm_fm(P, hn, 2, lambda ci: g.w_kv_fm[j, ci], list(range(8)), 4, wring, psr, post)

            def postv(pi, tt, ps, tok0=tok0):
                st = stgb.next()
                P.copy(st.t[:], ps.t[:], [ps], [st], eng="act")
                P.store(st, g.vmTM[tok0 + tt * 128:tok0 + (tt + 1) * 128, pi * 512:(pi + 1) * 512], st.t[:])

            gemm_tm(P, hn, 2, lambda pi: g.w_kv_tm[j, pi], [0, 1], SEQ // 128, wv, psr, postv)


def phase_E4(P, g):
    scale = float((128 + 64) ** -0.5)
    LOOK = 2
    with P.phase():
        kpe = P.sb("kpe", [64, SEQ], BF16, dma=True)
        kn = P.sbring("kn", 2, [128, SEQ], BF16, dma=True)
        qn = P.sbring("qn", 2, [128, SEQ], BF16, dma=True)
        qr = P.sbring("qr", 2, [64, SEQ], BF16, dma=True)
        vv = P.sbring("vv", 2, [128, 16, 128], BF16, dma=True)
        pT = P.sbring("pT", 4, [128, 512], BF16)
        rec = P.sbring("rec", 2, [128, 512], F32)
        ost = P.sbring("ost", 2, [128, 512], BF16, dma=True)
        ps_s = P.psring("pss", 4)
        ps_o = P.psring("pso", 2)
        ps_d = P.psring("psd", 2)
        for s in range(NSEQ):
            tok0 = s * SEQ
            P.load(kpe, kpe.t[:], g.kpeT[:, tok0:tok0 + SEQ])
            heads = {}
            acc = {}

            def head(h, tok0=tok0):
                if h not in heads:
                    k_ = kn.next()
                    P.load(k_, k_.t[:], g.knT[h * 128:(h + 1) * 128, tok0:tok0 + SEQ])
                    q_ = qn.next()
                    P.load(q_, q_.t[:], g.qnT[h * 128:(h + 1) * 128, tok0:tok0 + SEQ])
                    r_ = qr.next()
                    P.load(r_, r_.t[:], g.qrT[h * 64:(h + 1) * 64, tok0:tok0 + SEQ])
                    v_ = vv.next()
                    P.load(v_, v_.t[:], g.vmTM[tok0:tok0 + SEQ, h * 128:(h + 1) * 128].rearrange("(b p) d -> p b d", p=128))
                    heads[h] = (k_, q_, r_, v_)
                return heads[h]

            def S(step):
                h, jq, kb = step
                k_, q_, r_, v_ = head(h)
                c0 = max(0, kb - 4 * jq) * 128
                qs = slice(jq * 512 + c0, (jq + 1) * 512)
                ks = slice(kb * 128, (kb + 1) * 128)
                pss = ps_s.next()
                matmul(P, pss, pss.t[:, c0:512], k_.t[:, ks], q_.t[:, qs], True, False, [k_, q_])
                matmul(P, pss, pss.t[:, c0:512], kpe.t[0:64, ks], r_.t[0:64, qs], False, True, [kpe, r_])
                return pss, c0

            def rest(step, pss, c0, tok0=tok0):
                h, jq, kb = step
                k_, q_, r_, v_ = heads[h]
                nkb = 4 * jq + 4
                if kb == 0:
                    acc[(h, jq)] = (ps_o.next(), ps_d.next())
                    if jq == 2 and h + 1 < 8:
                        head(h + 1)
                po, pd = acc[(h, jq)]
                p_ = pT.next()
                P.act(p_.t[:, c0:512], pss.t[:, c0:512], AF.Exp, [pss], [p_], scale=scale)
                if kb >= 4 * jq:
                    P.op("dve", lambda e: e.memset(p_.t[64:128, c0:c0 + 64], 0.0), [], [p_])
                last = kb == nkb - 1
                P.op("pe", lambda e: e.matmul(po.t[:, c0:512], v_.t[:, kb, :], p_.t[:, c0:512], start=(kb == 0), stop=last),
                     reads=[v_, p_], writes=[po], inc=last)
                P.op("pe", lambda e: e.matmul(pd.t[:, c0:512], g.ones.t[:], p_.t[:, c0:512], start=(kb == 0), stop=last),
                     reads=[g.ones, p_], writes=[pd], inc=True)
                if last:
                    rc = rec.next()
                    P.op("dve", lambda e: e.reciprocal(out=rc.t[:], in_=pd.t[:]), [pd], [rc])
                    o = ost.next()
                    P.tt(o.t[:], po.t[:], rc.t[:], ALU.mult, [po, rc], [o])
                    P.store(o, g.abT[h * 128:(h + 1) * 128, tok0 + jq * 512:tok0 + (jq + 1) * 512], o.t[:])
                    del acc[(h, jq)]

            steps = [(h, jq, kb) for h in range(8) for jq in range(4) for kb in range(4 * jq + 4)]
            pendq = []
            for i in range(min(LOOK, len(steps))):
                pendq.append(S(steps[i]))
            for i, st in enumerate(steps):
                if i + LOOK < len(steps):
                    pendq.append(S(steps[i + LOOK]))
                pss, c0 = pendq.pop(0)
                rest(st, pss, c0)


def phase_E5(P, g, j):
    with P.phase():
        wsf = load_small(P, "wsf", [128, 1024], g.sgu_wsT[j])
        ws = P.sb("ws", [128, 1024], BF16)
        P.copy(ws.t[:], wsf.t[:], [wsf], [ws])
        for gi in range(8):
            P.op("dve", lambda e, gi=gi: e.memset(ws.t[64:128, gi * 128:gi * 128 + 64], 0.0), [], [ws])
        bs4 = load_small(P, "bs4", [128, 8 * 512], g.sgu_bs4[j])
        vn = P.sbring("vn", 2, [128, 4, 1024], BF16, dma=True)
        ug = P.sbring("ug", 16, [128, 512], BF16, dma=True)
        tmp = P.sbring("tmp", 3, [128, 512], F32)
        ost = P.sbring("ost", 4, [128, 512], BF16, dma=True)
        psr = P.psring("ps", 8)
        for ti in range(T // 512):
            t0 = ti * 512
            v_ = vn.next()
            P.load(v_, v_.t[:], g.vnTM[t0:t0 + 512, :].rearrange("(b p) c -> p b c", p=128))
            pss_, us_ = [], []
            for gi in range(8):
                u_ = ug.next()
                P.load(u_, u_.t[:], g.abT[1024 + gi * 128:1024 + (gi + 1) * 128, t0:t0 + 512])
                us_.append(u_)
                ps = psr.next()
                for b in range(4):
                    P.op("pe", lambda e, ps=ps, v_=v_, b=b, gi=gi: e.matmul(
                        ps.t[:, b * 128:(b + 1) * 128], v_.t[:, b, gi * 128:(gi + 1) * 128], ws.t[:, gi * 128:(gi + 1) * 128],
                        start=True, stop=True), reads=[v_, ws], writes=[ps], inc=(b == 3))
                pss_.append(ps)
            for gi in range(8):
                ps = pss_[gi]
                u_ = us_[gi]
                t_ = tmp.next()
                P.tt(t_.t[:], ps.t[:], bs4.t[:, gi * 512:(gi + 1) * 512], ALU.add, [ps, bs4], [t_])
                o = ost.next()
                P.tt(o.t[:], t_.t[:], u_.t[:], ALU.mult, [t_, u_], [o])
                P.store(o, g.abT[1024 + gi * 128:1024 + (gi + 1) * 128, t0:t0 + 512], o.t[:])


def phase_proj_res(P, g, src, Kc, wfm, hsrc, hdst, TB):
    big = Kc >= 44
    with P.phase():
        acts = P.sbring("act", 2, [128, Kc, TB], BF16, dma=True)
        psr = P.psring("ps", 8)
        wring = P.sbring("w", 2 if big else 3, [128, Kc * 128], BF16, dma=True)
        hres = P.sbring("hres", 2 if big else 4, [128, 512], F32, dma=True)
        pend = {}
        nblk = T // TB
        grp = max(1, (2 << 20) // (128 * TB * 2))
        qsel = [0]

        def load_act(blk):
            a = acts.next()
            tok0 = blk * TB
            for c0 in range(0, Kc, grp):
                c1 = min(Kc, c0 + grp)
                q = "sp"
                qsel[0] += 1
                P.dma(q, a.t[:, c0:c1, :], src[c0 * 128:c1 * 128, tok0:tok0 + TB].rearrange("(c p) t -> p c t", p=128),
                      writes=[a], sem=a.dsem)
            return a

        nxt = load_act(0)
        for blk in range(nblk):
            tok0 = blk * TB
            act = nxt
            if blk + 1 < nblk:
                nxt = load_act(blk + 1)

            def pre(ci, ti, tok0=tok0):
                hr = hres.next()
                P.load(hr, hr.t[:], hsrc[ci * 128:(ci + 1) * 128, tok0 + ti * 512:tok0 + (ti + 1) * 512])
                pend[(ci, ti)] = hr

            def post(ci, ti, ps, tok0=tok0):
                hr = pend.pop((ci, ti))
                P.tt(hr.t[:], ps.t[:], hr.t[:], ALU.add, [ps, hr], [hr])
                P.store(hr, hdst[ci * 128:(ci + 1) * 128, tok0 + ti * 512:tok0 + (ti + 1) * 512], hr.t[:], q="act")

            gemm_fm(P, act, Kc, lambda ci: wfm[ci], list(range(16)), TB // 512, wring, psr, post, pre)


def phase_F1(P, g, L):
    with P.phase():
        gain = load_small(P, "gain", [128, 16], g.ffn_g[L])
        cw = load_small(P, "cw", [128, 88 * 3], g.conv_w[L])
        cb = load_small(P, "cb", [128, 88], g.conv_b[L])
        nr = norm_res(P, 16, TT=128)
        hns = [P.sb("hn0", [128, 16, SEQ], BF16), P.sb("hn1", [128, 16, SEQ], BF16)]
        psr = P.psring("ps", 8)
        wring = P.sbring("w", 4, [128, 2048], BF16, dma=True)
        asb = P.sbring("asb", 4, [128, 514], F32)
        cc = P.sbring("cc", 4, [128, 512], F32)
        gl = P.sbring("gl", 2, [128, 512], F32)
        ptmp = P.sbring("ptmp", 2, [128, 512], F32)
        ost = P.sbring("ost", 3, [128, 512], BF16, dma=True)
        norm_block(P, g, nr, g.hT, 0, SEQ, gain, hns[0], 16, psr)
        for s in range(NSEQ):
            tok0 = s * SEQ
            hn = hns[s % 2]
            nxt = norm_block_gen(P, g, nr, g.hT, tok0 + SEQ, SEQ, gain, hns[(s + 1) % 2], 16, psr) if s + 1 < NSEQ else iter(())
            loaded = {}
            it = 0

            def ld(i):
                for half in (0, 1):
                    wb = wring.next()
                    P.dma("pool", wb.t[:], g.ffn_up_fm[L, i + 44 * half], writes=[wb], sem=wb.dsem)
                    loaded[(i, half)] = wb

            ld(0)
            for i in range(44):
                if i + 1 < 44:
                    ld(i + 1)
                wbs = (loaded.pop((i, 0)), loaded.pop((i, 1)))
                prev = [None, None]
                for ti in range(4):
                    it += 1
                    if it % 4 == 2:
                        next(nxt, None)
                    cres = []
                    for half in (0, 1):
                        ci = i + 44 * half
                        wb = wbs[half]
                        ps = psr.next()
                        for kc in range(16):
                            matmul(P, ps, ps.t[:, :], wb.t[:, kc * 128:(kc + 1) * 128], hn.t[:, kc, ti * 512:(ti + 1) * 512],
                                   kc == 0, kc == 15, [wb, hn])
                        a = asb.next()
                        ve = "dve"
                        P.copy(a.t[:, 2:514], ps.t[:], [ps], [a], eng="act")
                        if ti == 0:
                            P.op(ve, lambda e, a=a: e.memset(a.t[:, 0:2], 0.0), [], [a])
                        else:
                            P.copy(a.t[:, 0:2], prev[half].t[:, 512:514], [prev[half]], [a], eng=ve)
                        prev[half] = a
                        c = cc.next()
                        P.act(c.t[:], ps.t[:], AF.Identity, [ps, cw, cb], [c],
                              bias=cb.t[:, ci:ci + 1], scale=cw.t[:, ci * 3 + 2:ci * 3 + 3])
                        P.stt(c.t[:], a.t[:, 1:513], cw.t[:, ci * 3 + 1:ci * 3 + 2], c.t[:], ALU.mult, ALU.add, [a, cw, c], [c])
                        P.stt(c.t[:], a.t[:, 0:512], cw.t[:, ci * 3:ci * 3 + 1], c.t[:], ALU.mult, ALU.add, [a, cw, c], [c])
                        cres.append(c)
                    gg = gl.next()
                    P.act(gg.t[:], cres[0].t[:], AF.Gelu_apprx_tanh, [cres[0]], [gg])
                    o = ost.next()
                    P.tt(o.t[:], gg.t[:], cres[1].t[:], ALU.mult, [gg, cres[1]], [o], eng="pool")
                    P.store(o, g.ffT[i * 128:(i + 1) * 128, tok0 + ti * 512:tok0 + (ti + 1) * 512], o.t[:])


def phase_PLE(P, g, L):
    with P.phase():
        gain = load_small(P, "gain", [128, 16], g.ple_g[L])
        nr = norm_res(P, 16, TT=128)
        hns = [P.sb("hn0", [128, 16, SEQ], BF16), P.sb("hn1", [128, 16, SEQ], BF16)]
        pb = P.sb("pb", [128, 2, SEQ], BF16, dma=True)
        psr = P.psring("ps", 8)
        wring = P.sbring("w", 3, [128, 2048], BF16, dma=True)
        wup = P.sbring("wup", 3, [128, 256], BF16, dma=True)
        hres = P.sbring("hres", 4, [128, 512], F32, dma=True)
        sg = P.sbring("sg", 3, [128, 512], F32)
        norm_block(P, g, nr, g.hT, 0, SEQ, gain, hns[0], 16, psr)
        for s in range(NSEQ):
            tok0 = s * SEQ
            hn = hns[s % 2]
            nxt = norm_block_gen(P, g, nr, g.hT, tok0 + SEQ, SEQ, gain, hns[(s + 1) % 2], 16, psr) if s + 1 < NSEQ else iter(())
            it = 0
            P.dma("pool", pb.t[:], g.pT[L, :, tok0:tok0 + SEQ].rearrange("(c p) t -> p c t", p=128), writes=[pb], sem=pb.dsem)
            loaded = {}

            def ld(i):
                wb = wring.next()
                P.dma("pool", wb.t[:], g.ple_gate_fm[L, i], writes=[wb], sem=wb.dsem)
                wu = wup.next()
                P.dma("pool", wu.t[:], g.ple_up_fm[L, i], writes=[wu], sem=wu.dsem)
                loaded[i] = (wb, wu)

            ld(0)
            ld(1)
            for i in range(16):
                if i + 2 < 16:
                    ld(i + 2)
                wb, wu = loaded.pop(i)
                for ti in range(4):
                    it += 1
                    if it % 2 == 1:
                        next(nxt, None)
                    tsl = slice(ti * 512, (ti + 1) * 512)
                    cols = slice(tok0 + ti * 512, tok0 + (ti + 1) * 512)
                    hr = hres.next()
                    P.load(hr, hr.t[:], g.hT[i * 128:(i + 1) * 128, cols], q="act")
                    ps = psr.next()
                    for kc in range(16):
                        matmul(P, ps, ps.t[:, :], wb.t[:, kc * 128:(kc + 1) * 128], hn.t[:, kc, tsl], kc == 0, kc == 15, [wb, hn])
                    ps2 = psr.next()
                    for kc in range(2):
                        matmul(P, ps2, ps2.t[:, :], wu.t[:, kc * 128:(kc + 1) * 128], pb.t[:, kc, tsl], kc == 0, kc == 1, [wu, pb])
                    s_ = sg.next()
                    P.act(s_.t[:], ps.t[:], AF.Sigmoid, [ps], [s_])
                    P.tt(s_.t[:], ps2.t[:], s_.t[:], ALU.mult, [ps2, s_], [s_])
                    P.tt(hr.t[:], s_.t[:], hr.t[:], ALU.add, [s_, hr], [hr])
                    P.store(hr, g.hT[i * 128:(i + 1) * 128, cols], hr.t[:])


def phase_O1(P, g, L, j):
    with P.phase():
        gain = load_small(P, "gain", [128, 16], g.mix_g[L])
        gng = load_small(P, "gng", [128, 32], g.gn_g[j])
        gnb = load_small(P, "gnb", [128, 32], g.gn_b[j])
        sgf = P.sbring("sgf", 3, [128, 512], F32)
        qd4 = load_small(P, "qd4", [128, 8 * 512], g.qd4)
        nr = norm_res(P, 16)
        hn = P.sb("hn", [128, 16, SEQ], BF16)
        psr = P.psring("ps", 8)
        wring = P.sbring("w", 4, [128, 2048], BF16, dma=True)
        wv = P.sbring("wv", 2, [128, 16 * 512], BF16, dma=True)
        stgb = P.sbring("stgb", 6, [128, 512], BF16, dma=True)
        tC = P.sb("tC", [128, SEQ], F32, dma=True)
        tS = P.sb("tS", [128, SEQ], F32, dma=True)
        tmp = P.sbring("tmp", 4, [128, 512], F32)
        for s in range(NSEQ):
            tok0 = s * SEQ
            norm_block(P, g, nr, g.hT, tok0, SEQ, gain, hn, 16, psr)
            P.load(tC, tC.t[:], g.tCr[:, tok0:tok0 + SEQ])
            P.load(tS, tS.t[:], g.tSr[:, tok0:tok0 + SEQ])
            for which in (0, 1):
                sc = 1.0 if which == 0 else 1.0 / 16.0
                dst = g.rqT if which == 0 else g.rkT
                for h in range(8):
                    c1 = which * 16 + 2 * h
                    w1 = wring.next()
                    P.dma("pool", w1.t[:], g.ret_in_fm[j, c1], writes=[w1], sem=w1.dsem)
                    w2 = wring.next()
                    P.dma("pool", w2.t[:], g.ret_in_fm[j, c1 + 1], writes=[w2], sem=w2.dsem)
                    for ti in range(4):
                        tsl = slice(ti * 512, (ti + 1) * 512)
                        cols = slice(tok0 + ti * 512, tok0 + (ti + 1) * 512)
                        p1 = psr.next()
                        for kc in range(16):
                            matmul(P, p1, p1.t[:, :], w1.t[:, kc * 128:(kc + 1) * 128], hn.t[:, kc, tsl], kc == 0, kc == 15, [w1, hn])
                        p2 = psr.next()
                        for kc in range(16):
                            matmul(P, p2, p2.t[:, :], w2.t[:, kc * 128:(kc + 1) * 128], hn.t[:, kc, tsl], kc == 0, kc == 15, [w2, hn])
                        t1 = tmp.next()
                        P.stt(t1.t[:], p1.t[:], sc, tC.t[:, tsl], ALU.mult, ALU.mult, [p1, tC], [t1])
                        t2 = tmp.next()
                        P.stt(t2.t[:], p2.t[:], sc, tS.t[:, tsl], ALU.mult, ALU.mult, [p2, tS], [t2])
                        t3 = tmp.next()
                        P.stt(t3.t[:], p2.t[:], sc, tC.t[:, tsl], ALU.mult, ALU.mult, [p2, tC], [t3])
                        t4 = tmp.next()
                        P.stt(t4.t[:], p1.t[:], sc, tS.t[:, tsl], ALU.mult, ALU.mult, [p1, tS], [t4])
                        if which == 1:
                            o1 = stgb.next()
                            P.tt(o1.t[:], t1.t[:], t2.t[:], ALU.subtract, [t1, t2], [o1])
                            P.store(o1, dst[(2 * h) * 128:(2 * h + 1) * 128, cols], o1.t[:])
                            o2 = stgb.next()
                            P.tt(o2.t[:], t3.t[:], t4.t[:], ALU.add, [t3, t4], [o2])
                            P.store(o2, dst[(2 * h + 1) * 128:(2 * h + 2) * 128, cols], o2.t[:])
                        else:
                            P.tt(t1.t[:], t1.t[:], t2.t[:], ALU.subtract, [t1, t2], [t1])
                            P.tt(t3.t[:], t3.t[:], t4.t[:], ALU.add, [t3, t4], [t3])
                            for (tx, row) in ((t1, 2 * h), (t3, 2 * h + 1)):
                                o1 = stgb.next()
                                P.copy(o1.t[:], tx.t[:], [tx], [o1], eng="act")
                                P.store(o1, g.rqT[row * 128:(row + 1) * 128, cols], o1.t[:])
                                o2 = stgb.next()
                                P.tt(o2.t[:], tx.t[:], qd4.t[:, h * 512:(h + 1) * 512], ALU.mult, [tx, qd4], [o2])
                                P.store(o2, g.rqsT[row * 128:(row + 1) * 128, cols], o2.t[:])

            def postg(ci, ti, ps, tok0=tok0):
                c = ci - 32
                sg_ = sgf.next()
                P.act(sg_.t[:], ps.t[:], AF.Silu, [ps], [sg_])
                st = stgb.next()
                P.ts(st.t[:], sg_.t[:], gng.t[:, c:c + 1], None, ALU.mult, None, [sg_, gng], [st])
                P.store(st, g.rgT[c * 128:(c + 1) * 128, tok0 + ti * 512:tok0 + (ti + 1) * 512], st.t[:])
                st2 = stgb.next()
                P.ts(st2.t[:], sg_.t[:], gnb.t[:, c:c + 1], None, ALU.mult, None, [sg_, gnb], [st2])
                P.store(st2, g.rg2T[c * 128:(c + 1) * 128, tok0 + ti * 512:tok0 + (ti + 1) * 512], st2.t[:])

            gemm_fm(P, hn, 16, lambda ci: g.ret_in_fm[j, ci], list(range(32, 64)), 4, wring, psr, postg)

            def postv(pi, tt, ps, tok0=tok0):
                st = stgb.next()
                P.copy(st.t[:], ps.t[:], [ps], [st], eng="act")
                P.store(st, g.rvTM[tok0 + tt * 128:tok0 + (tt + 1) * 128, pi * 512:(pi + 1) * 512], st.t[:])

            gemm_tm(P, hn, 16, lambda pi: g.ret_in_tm[j, pi], list(range(8)), SEQ // 128, wv, psr, postv)


def phase_O2(P, g, j):
    NH = 2
    with P.phase():
        DT = load_small(P, "DT", [128, 8 * 128], g.DT)
        gng = load_small(P, "gng", [128, 32], g.gn_g[j])
        gnb = load_small(P, "gnb", [128, 32], g.gn_b[j])
        cs = g.consts
        kt = P.sbring("kt", 2 * NH, [128, 2, 512], BF16, dma=True)
        qt = P.sbring("qt", 2 * NH, [128, 2, 512], BF16, dma=True)
        qst = P.sbring("qst", 2 * NH, [128, 2, 512], BF16, dma=True)
        vt = P.sbring("vt", 2 * NH, [128, 4, 512], BF16, dma=True)
        gt = P.sbring("gt", 2 * NH, [128, 4, 512], BF16, dma=True)
        gt2 = P.sbring("gt2", 2 * NH, [128, 4, 512], BF16, dma=True)
        rst = P.sbring("rst", 2 * NH, [128, 4, 512], BF16, dma=True)
        AT = P.sbring("AT", 4, [128, 128], BF16)
        kz = P.sbring("kz", 4, [128, 256], BF16)
        states = [P.sb(f"state{i}", [128, 2, 512], F32) for i in range(NH)]
        stbfs = [P.sb(f"stbf{i}", [128, 2, 512], BF16) for i in range(NH)]
        st6 = P.sbring("st6", 4, [128, 6], F32)
        mv = P.sbring("mv", 4, [128, 2], F32)
        nmr = P.sbring("nmr", 4, [128, 1], F32)
        xh = P.sbring("xh", 4, [128, 512], BF16)
        r1 = P.sbring("r1", 4, [128, 4, 128], F32)
        psS = P.psring("pss", NH).b
        psO = P.psring("pso", NH).b
        psU = P.psring("psu", NH).b
        psB = P.psring("psb", NH, (128, 1024), BF16).b

        def step(h, hi, b, first, lastblk, bufs):
            k_, q_, qs_, v_, g_, ro, g2_ = bufs
            state = states[hi]
            stbf = stbfs[hi]
            pss, po, pu, pb = psS[hi], psO[hi], psU[hi], psB[hi]
            cd128 = float(g.cd128[h])
            bs = slice(b * 128, (b + 1) * 128)
            for dc in range(2):
                matmul(P, pss, pss.t[:, 0:128], k_.t[:, dc, bs], q_.t[:, dc, bs], dc == 0, dc == 1, [k_, q_])
            yield
            a_ = AT.next()
            P.tt(a_.t[:], pss.t[:, 0:128], DT.t[:, h * 128:(h + 1) * 128], ALU.mult, [pss, DT], [a_])
            yield
            P.op("pe", lambda e: e.matmul(po.t[:, :], a_.t[:], v_.t[:, b, :], start=True, stop=first), [a_, v_], [po], inc=first)
            if not first:
                for dc in range(2):
                    P.op("pe", lambda e: e.matmul(po.t[:, :], qs_.t[:, dc, bs], stbf.t[:, dc, :], start=False, stop=(dc == 1)),
                         [qs_, stbf], [po], inc=(dc == 1))
            if not lastblk:
                for dc in range(2):
                    P.op("pe", lambda e: e.transpose(pb.t[:, dc * 128:(dc + 1) * 128], k_.t[:, dc, bs], g.ident.t[:]),
                         [k_, g.ident], [pb], inc=(dc == 1))
            yield
            if not lastblk:
                z_ = kz.next()
                P.act(z_.t[:], pb.t[:, 0:256], AF.Identity, [pb, cs], [z_], scale=cs.t[:, 5 + h:6 + h])
            s6 = st6.next()
            P.op("dve", lambda e: e.bn_stats(out=s6.t[:], in_=po.t[:]), [po], [s6])
            m = mv.next()
            P.op("dve", lambda e: e.bn_aggr(out=m.t[:], in_=s6.t[:]), [s6], [m])
            yield
            if not lastblk:
                P.op("pe", lambda e: e.matmul(pu.t[:, :], z_.t[:, 0:128], v_.t[:, b, :], start=True, stop=True), [z_, v_], [pu], inc=True)
            P.act(m.t[:, 1:2], m.t[:, 1:2], AF.Sqrt, [m], [m], bias=EPS, scale=1.0)
            yield
            if not lastblk:
                if first:
                    P.copy(state.t[:, 0, :], pu.t[:], [pu], [state], eng="dve")
                else:
                    P.stt(state.t[:, 0, :], state.t[:, 0, :], cd128, pu.t[:], ALU.mult, ALU.add, [state, pu], [state])
            P.op("dve", lambda e: e.reciprocal(out=m.t[:, 1:2], in_=m.t[:, 1:2]), [m], [m])
            nm = nmr.next()
            P.ts(nm.t[:], m.t[:, 0:1], -1.0, m.t[:, 1:2], ALU.mult, ALU.mult, [m], [nm])
            x_ = xh.next()
            P.act(x_.t[:], po.t[:], AF.Identity, [po, m, nm], [x_], bias=nm.t[:, 0:1], scale=m.t[:, 1:2])
            yield
            if not lastblk:
                P.op("pe", lambda e: e.matmul(pu.t[:, :], z_.t[:, 128:256], v_.t[:, b, :], start=True, stop=True), [z_, v_], [pu], inc=True)
            for ec in range(4):
                P.op("pe", lambda e: e.transpose(pb.t[:, 256 + ec * 128:256 + (ec + 1) * 128], x_.t[:, ec * 128:(ec + 1) * 128], g.ident.t[:]),
                     [x_, g.ident], [pb], inc=(ec == 3))
            yield
            if not lastblk:
                if first:
                    P.copy(state.t[:, 1, :], pu.t[:], [pu], [state], eng="dve")
                else:
                    P.stt(state.t[:, 1, :], state.t[:, 1, :], cd128, pu.t[:], ALU.mult, ALU.add, [state, pu], [state])
            r_ = r1.next()
            P.tt(r_.t[:], pb.t[:, 256:768].rearrange("p (c t) -> p c t", c=4), g_.t[:, :, bs], ALU.mult, [pb, g_], [r_])
            yield
            if not lastblk:
                P.copy(stbf.t[:], state.t[:], [state], [stbf], eng="act")
            P.tt(ro.t[:, :, bs], r_.t[:], g2_.t[:, :, bs], ALU.add, [r_, g2_], [ro], eng="pool")

        def load_tile(s, hg, ti):
                    t0 = s * SEQ + ti * 512
                    bufs = {}
                    for hi in range(NH):
                        h = hg * NH + hi
                        rows2 = slice(h * 256, (h + 1) * 256)
                        k_ = kt.next()
                        P.load(k_, k_.t[:], g.rkT[rows2, t0:t0 + 512].rearrange("(c p) t -> p c t", p=128))
                        q_ = qt.next()
                        P.load(q_, q_.t[:], g.rqT[rows2, t0:t0 + 512].rearrange("(c p) t -> p c t", p=128))
                        qs_ = qst.next()
                        P.load(qs_, qs_.t[:], g.rqsT[rows2, t0:t0 + 512].rearrange("(c p) t -> p c t", p=128))
                        v_ = vt.next()
                        P.load(v_, v_.t[:], g.rvTM[t0:t0 + 512, h * 512:(h + 1) * 512].rearrange("(b p) e -> p b e", p=128))
                        g_ = gt.next()
                        P.load(g_, g_.t[:], g.rgT[h * 512:(h + 1) * 512, t0:t0 + 512].rearrange("(c p) t -> p c t", p=128))
                        g2_ = gt2.next()
                        P.load(g2_, g2_.t[:], g.rg2T[h * 512:(h + 1) * 512, t0:t0 + 512].rearrange("(c p) t -> p c t", p=128))
                        bufs[hi] = (k_, q_, qs_, v_, g_, rst.next(), g2_)
                    return bufs

        items = [(s, hg, ti) for s in range(NSEQ) for hg in range(8 // NH) for ti in range(4)]
        nxt_bufs = load_tile(*items[0])
        for idx, (s, hg, ti) in enumerate(items):
                    t0 = s * SEQ + ti * 512
                    bufs = nxt_bufs
                    if idx + 1 < len(items):
                        nxt_bufs = load_tile(*items[idx + 1])
                    for b in range(4):
                        gens = [step(hg * NH + hi, hi, b, ti == 0 and b == 0, ti == 3 and b == 3, bufs[hi]) for hi in range(NH)]
                        while gens:
                            for gen in list(gens):
                                try:
                                    next(gen)
                                except StopIteration:
                                    gens.remove(gen)
                    for hi in range(NH):
                        h = hg * NH + hi
                        ro = bufs[hi][5]
                        P.store(ro, g.rrT[h * 512:(h + 1) * 512, t0:t0 + 512].rearrange("(c p) t -> p c t", p=128), ro.t[:], q="act")


def phase_final(P, g):
    with P.phase():
        gain = load_small(P, "gain", [128, 16], g.fin_g)
        nr = norm_res(P, 16, TT=512)
        psr = P.psring("ps", 4)
        stg = P.sbring("fst", 2, [128, 16, nr.TT], F32, dma=True)
        norm_block(P, g, nr, g.hT, 0, T, gain, None, 16, psr,
                   out_f32=(stg, lambda c0, TT: g.yT[:, c0:c0 + TT].rearrange("(c p) t -> p c t", p=128)))


def build_program(cst, nlayers=DEPTH, dbg=None, stop=None):
    nc = bass.Bass("TRN2", target_bir_lowering=False)
    g = G()
    g.cd128 = cst["cd128"]

    def ext(name, shape, dt=F32):
        return nc.dram_tensor(name, list(shape), dt, kind="ExternalInput").ap()

    def scr(name, shape, dt):
        kind = "ExternalOutput" if (dbg is not None and name in dbg.split(",")) else "Internal"
        return nc.dram_tensor(name, list(shape), dt, kind=kind).ap()

    g.xT = ext("xT", [D, T])
    g.pT = ext("pT", [DEPTH, 256, T])
    g.pos = ext("pos", [128, T], I32)
    cin = ext("consts", [128, 16])
    identf = ext("ident", [128, 128])
    g.DT = ext("DT", [128, 1024])
    g.qd4 = ext("qd4", [128, 4096])
    g.mix_g = ext("mix_g", [DEPTH, 128, 16])
    g.ffn_g = ext("ffn_g", [DEPTH, 128, 16])
    g.ple_g = ext("ple_g", [DEPTH, 128, 16])
    g.fin_g = ext("fin_g", [128, 16])
    g.qn_g = ext("qn_g", [2, 128, 4])
    g.kvn_g = ext("kvn_g", [2, 128, 2])
    g.sgu_lng = ext("sgu_lng", [2, 128, 1024])
    g.sgu_lnb = ext("sgu_lnb", [2, 128, 1024])
    g.sgu_bs4 = ext("sgu_bs4", [2, 128, 4096])
    g.sgu_wsT = ext("sgu_wsT", [2, 128, 1024])
    g.gn_g = ext("gn_g", [2, 128, 32])
    g.gn_b = ext("gn_b", [2, 128, 32])
    g.conv_w = ext("conv_w", [DEPTH, 128, 264])
    g.conv_b = ext("conv_b", [DEPTH, 128, 88])
    g.w_in_fm = ext("w_in_fm", [2, 15, 128, 2048])
    g.w_in_tm = ext("w_in_tm", [2, 2, 128, 16 * 512])
    g.w_q_fm = ext("w_q_fm", [2, 16, 128, 512])
    g.w_kv_fm = ext("w_kv_fm", [2, 8, 128, 256])
    g.w_kv_tm = ext("w_kv_tm", [2, 2, 128, 2 * 512])
    g.w_out_fm = ext("w_out_fm", [2, 16, 128, 2048])
    g.ret_in_fm = ext("ret_in_fm", [2, 64, 128, 2048])
    g.ret_in_tm = ext("ret_in_tm", [2, 8, 128, 16 * 512])
    g.ret_out_fm = ext("ret_out_fm", [2, 16, 128, 4096])
    g.ffn_up_fm = ext("ffn_up_fm", [DEPTH, 88, 128, 2048])
    g.ffn_dn_fm = ext("ffn_dn_fm", [DEPTH, 16, 128, 5632])
    g.ple_gate_fm = ext("ple_gate_fm", [DEPTH, 16, 128, 2048])
    g.ple_up_fm = ext("ple_up_fm", [DEPTH, 16, 128, 256])

    if dbg is not None and "hT" in dbg.split(","):
        g.hT = nc.dram_tensor("hT", [D, T], F32, kind="ExternalOutput").ap()
        g.yT = None
    else:
        g.hT = nc.dram_tensor("hT", [D, T], F32, kind="Internal").ap()
        g.yT = nc.dram_tensor("yT", [D, T], F32, kind="ExternalOutput").ap() if dbg is None else None
    g.cqT = scr("cqT", [512, T], F32)
    g.ckvT = scr("ckvT", [256, T], F32)
    g.kpeT = scr("kpeT", [64, T], BF16)
    g.vnTM = scr("vnTM", [T, 1024], BF16)
    g.qnT = scr("qnT", [1024, T], BF16)
    g.qrT = scr("qrT", [512, T], BF16)
    g.knT = scr("knT", [1024, T], BF16)
    g.vmTM = scr("vmTM", [T, 1024], BF16)
    g.abT = scr("abT", [2048, T], BF16)
    g.rqT = scr("rqT", [2048, T], BF16)
    g.rqsT = scr("rqsT", [2048, T], BF16)
    g.rkT = scr("rkT", [2048, T], BF16)
    g.rvTM = scr("rvTM", [T, 4096], BF16)
    g.rgT = scr("rgT", [4096, T], BF16)
    g.rg2T = scr("rg2T", [4096, T], BF16)
    g.rrT = scr("rrT", [4096, T], BF16)
    g.ffT = scr("ffT", [DFF, T], BF16)
    g.tCm = scr("tCm", [128, T], F32)
    g.tSm = scr("tSm", [128, T], F32)
    g.tCr = scr("tCr", [128, T], F32)
    g.tSr = scr("tSr", [128, T], F32)

    with contextlib.ExitStack() as es:
        P = Prog(nc, es)
        P.pstack = es
        g.consts = P.sb("consts", [128, 16], F32, dma=True)
        g.ones = P.sb("ones", [128, 128], BF16)
        g.ident = P.sb("ident", [128, 128], BF16, dma=True)
        P.load(g.consts, g.consts.t[:], cin)
        P.dma("pool", g.ident.t[:], identf, writes=[g.ident], sem=g.ident.dsem)
        P.op("dve", lambda e: e.memset(g.ones.t[:], 1.0), [], [g.ones])
        ndma_persist = P.dnext

        def run():
            phase_tables(P, g)
            if stop == "tables":
                return
            for L in range(nlayers):
                j = L // 2
                hsrc = g.xT if L == 0 else g.hT
                if L % 2 == 0:
                    phase_E1(P, g, L, j, hsrc)
                    if stop == f"E1_{L}":
                        return
                    phase_E2(P, g, j)
                    phase_E3(P, g, j)
                    if stop == f"E3_{L}":
                        return
                    phase_E4(P, g)
                    if stop == f"E4_{L}":
                        return
                    phase_E5(P, g, j)
                    if stop == f"E5_{L}":
                        return
                    phase_proj_res(P, g, g.abT, 16, g.w_out_fm[j], hsrc, g.hT, SEQ)
                else:
                    phase_O1(P, g, L, j)
                    if stop == f"O1_{L}":
                        return
                    phase_O2(P, g, j)
                    if stop == f"O2_{L}":
                        return
                    phase_proj_res(P, g, g.rrT, 32, g.ret_out_fm[j], g.hT, g.hT, 1024)
                if stop == f"mix_{L}":
                    return
                phase_F1(P, g, L)
                if stop == f"F1_{L}":
                    return
                phase_proj_res(P, g, g.ffT, 44, g.ffn_dn_fm[L], g.hT, g.hT, 1024)
                if stop == f"ffn_{L}":
                    return
                phase_PLE(P, g, L)
                if stop == f"ple_{L}":
                    return
            if g.yT is not None:
                phase_final(P, g)

        orig_phase = P.phase

        @contextlib.contextmanager
        def phase_keep():
            with orig_phase():
                P.dnext = ndma_persist
                yield
        P.phase = phase_keep
        run()
        P.barrier()
    return nc


def _fm(W):
    K, N = W.shape
    return np.ascontiguousarray(W.reshape(K // 128, 128, N // 128, 128).transpose(2, 1, 0, 3).reshape(N // 128, 128, K))


def _tm(W):
    K, N = W.shape
    return np.ascontiguousarray(W.reshape(K // 128, 128, N // 512, 512).transpose(2, 1, 0, 3).reshape(N // 512, 128, (K // 128) * 512))


def _pc(v):
    return np.ascontiguousarray(v.reshape(-1, 128).T)


def module_constants():
    H = 8
    log_g = np.log1p(-(2.0 ** (-5.0 - np.arange(H, dtype=np.float64))))
    gam = np.exp(log_g)
    consts = np.zeros((128, 16), np.float32)
    p = np.arange(128)
    consts[:, 0] = -np.pi
    consts[:, 1] = 10000.0 ** (-(p % 32).astype(np.float64) / 32)
    consts[:, 2] = 10000.0 ** (-p.astype(np.float64) / 128)
    consts[:, 3] = np.where((p % 64) < 32, -1.0, 1.0)
    consts[:, 4] = -1.0
    for h in range(H):
        consts[:, 5 + h] = gam[h] ** (127 - p)
    i = p[:, None]
    jj = p[None, :]
    DT = np.zeros((128, H * 128), np.float32)
    for h in range(H):
        Dm = np.where((i // 64) >= (jj // 64), gam[h] ** np.abs(i - jj).astype(np.float64), 0.0)
        DT[:, h * 128:(h + 1) * 128] = Dm.T
    qd4 = np.zeros((128, H * 512), np.float32)
    for h in range(H):
        qd4[:, h * 512:(h + 1) * 512] = (gam[h] ** ((np.arange(512) % 128) + 1.0))[None, :]
    cd128 = gam ** 128
    return dict(consts=consts, DT=DT, qd4=qd4, cd128=cd128, ident=np.eye(128, dtype=np.float32))


def prep_shared(inp):
    sh = {}
    c = module_constants()
    sh["consts"] = c["consts"]
    sh["DT"] = c["DT"]
    sh["qd4"] = c["qd4"]
    sh["ident"] = c["ident"]
    f = np.float32
    sh["mix_g"] = np.stack([_pc(v) for v in inp["mix_norm_g"]]).astype(f)
    sh["ffn_g"] = np.stack([_pc(v) for v in inp["ffn_norm_g"]]).astype(f)
    sh["ple_g"] = np.stack([_pc(v) for v in inp["ple_norm_g"]]).astype(f)
    sh["fin_g"] = _pc(inp["final_norm_g"]).astype(f)
    sh["qn_g"] = np.stack([_pc(v) for v in inp["mla_q_norm_g"]]).astype(f)
    sh["kvn_g"] = np.stack([_pc(v) for v in inp["mla_kv_norm_g"]]).astype(f)
    sh["sgu_lng"] = np.ascontiguousarray(np.broadcast_to(inp["sgu_ln_g"][:, None, :], (2, 128, 1024))).astype(f)
    sh["sgu_lnb"] = np.ascontiguousarray(np.broadcast_to(inp["sgu_ln_b"][:, None, :], (2, 128, 1024))).astype(f)
    bs = np.tile(inp["sgu_b_s"][:, :, None, :], (1, 1, 4, 1)).reshape(2, 1, 8 * 512)
    sh["sgu_bs4"] = np.ascontiguousarray(np.broadcast_to(bs, (2, 128, 4096))).astype(f)
    sh["sgu_wsT"] = np.ascontiguousarray(inp["sgu_w_s"].transpose(0, 3, 1, 2).reshape(2, 128, 1024)).astype(f)
    sh["gn_g"] = np.stack([_pc(v) for v in inp["ret_gn_g"]]).astype(f)
    sh["gn_b"] = np.stack([_pc(v) for v in inp["ret_gn_b"]]).astype(f)
    cw = inp["ffn_conv_w"]
    sh["conv_w"] = np.ascontiguousarray(cw.reshape(4, 3, 88, 128).transpose(0, 3, 2, 1).reshape(4, 128, 264)).astype(f)
    sh["conv_b"] = np.stack([_pc(v) for v in inp["ffn_conv_b"]]).astype(f)
    swap64 = np.concatenate([np.arange(32, 64), np.arange(0, 32)])
    w_in_fm, w_in_tm, w_q_fm, w_kv_fm, w_kv_tm, w_out_fm = [], [], [], [], [], []
    for j in range(2):
        W = inp["even_w_in"][j]
        kpe = W[:, 768:832]
        fmcols = np.concatenate([W[:, 0:768], kpe, kpe[:, swap64], W[:, 832:1856]], axis=1)
        w_in_fm.append(_fm(fmcols))
        w_in_tm.append(_tm(W[:, 1856:2880]))
        Wq = inp["mla_w_q_up"][j].reshape(512, 8, 192)
        nope = Wq[:, :, 0:128].reshape(512, 1024)
        rope = Wq[:, :, 128:192]
        w_q_fm.append(_fm(np.concatenate([nope, rope.reshape(512, 512), rope[:, :, swap64].reshape(512, 512)], axis=1)))
        Wkv = inp["mla_w_kv_up"][j].reshape(256, 8, 256)
        w_kv_fm.append(_fm(np.ascontiguousarray(Wkv[:, :, 0:128]).reshape(256, 1024)))
        w_kv_tm.append(_tm(np.ascontiguousarray(Wkv[:, :, 128:256]).reshape(256, 1024)))
        w_out_fm.append(_fm(inp["even_w_out"][j]))
    sh["w_in_fm"] = np.stack(w_in_fm)
    sh["w_in_tm"] = np.stack(w_in_tm)
    sh["w_q_fm"] = np.stack(w_q_fm)
    sh["w_kv_fm"] = np.stack(w_kv_fm)
    sh["w_kv_tm"] = np.stack(w_kv_tm)
    sh["w_out_fm"] = np.stack(w_out_fm)
    ret_in_fm, ret_in_tm, ret_out_fm = [], [], []
    for j in range(2):
        W = inp["ret_w_in"][j]
        ret_in_fm.append(_fm(np.concatenate([W[:, 0:4096], W[:, 8192:12288]], axis=1)))
        ret_in_tm.append(_tm(W[:, 4096:8192]))
        ret_out_fm.append(_fm(inp["ret_w_out"][j]))
    sh["ret_in_fm"] = np.stack(ret_in_fm)
    sh["ret_in_tm"] = np.stack(ret_in_tm)
    sh["ret_out_fm"] = np.stack(ret_out_fm)
    sh["ffn_up_fm"] = np.stack([_fm(inp["ffn_w_up"][L]) for L in range(DEPTH)])
    sh["ffn_dn_fm"] = np.stack([_fm(inp["ffn_w_down"][L]) for L in range(DEPTH)])
    sh["ple_gate_fm"] = np.stack([_fm(inp["ple_w_gate"][L]) for L in range(DEPTH)])
    sh["ple_up_fm"] = np.stack([_fm(inp["ple_w_up"][L]) for L in range(DEPTH)])
    return sh, c


def prep_core(inp, core):
    b0 = core * NSEQ
    x = inp["x"][b0:b0 + NSEQ]
    xT = np.ascontiguousarray(x.reshape(T, D).T)
    p = inp["p"][:, b0:b0 + NSEQ]
    pT = np.ascontiguousarray(p.reshape(DEPTH, T, 256).transpose(0, 2, 1))
    pos = np.ascontiguousarray(np.broadcast_to(inp["positions"][b0:b0 + NSEQ].reshape(1, T), (128, T))).astype(np.int32)
    return {"xT": xT.astype(np.float32), "pT": pT.astype(np.float32), "pos": pos}


def kernel(**inputs):
    inp = {k: np.asarray(v) for k, v in inputs.items()}
    sh, c = prep_shared(inp)
    nc = build_program(c)
    in_maps = []
    for core in range(NCORES):
        m = dict(sh)
        m.update(prep_core(inp, core))
        in_maps.append(m)
    res = run_bass_kernel_spmd(nc, in_maps, core_ids=list(range(NCORES)))
    out = np.empty((NCORES * NSEQ, SEQ, D), np.float32)
    for core in range(NCORES):
        yT = np.asarray(res.results[core]["yT"])
        out[core * NSEQ:(core + 1) * NSEQ] = yT.T.reshape(NSEQ, SEQ, D)
    return out
```

```python
import contextlib
import numpy as np
import concourse.bass as bass
import concourse.mybir as mybir
from concourse.bass_utils import run_bass_kernel_spmd

F32 = mybir.dt.float32
BF16 = mybir.dt.bfloat16
I32 = mybir.dt.int32
AF = mybir.ActivationFunctionType
ALU = mybir.AluOpType

D = 2048
SEQ = 2048
NSEQ = 2
T = NSEQ * SEQ
DEPTH = 4
DFF = 5632
EPS = 1e-6
NCORES = 8


class Sem:
    def __init__(self, h, key):
        self.h = h
        self.key = key
        self.count = 0


class Buf:
    def __init__(self, t, name="", dsem=None):
        self.t = t
        self.name = name
        self.lw = None
        self.rd = {}
        self.dsem = dsem
        self.excl = False


class Eng:
    def __init__(self, name, h, sem):
        self.name = name
        self.h = h
        self.sem = sem
        self.waited = {}


class Ring:
    def __init__(self, bufs):
        self.b = bufs
        self.i = 0

    def next(self):
        b = self.b[self.i % len(self.b)]
        self.i += 1
        return b


class Prog:
    def __init__(self, nc, es, ndma=60):
        self.nc = nc
        self.es = es
        self.sems = {}
        self.uid = 0

        def mk(name):
            h = es.enter_context(nc.semaphore(name))
            s = Sem(h, name)
            self.sems[name] = s
            return s

        self.E = {
            "pe": Eng("pe", nc.tensor, mk("c_pe")),
            "act": Eng("act", nc.scalar, mk("c_act")),
            "dve": Eng("dve", nc.vector, mk("c_dve")),
            "pool": Eng("pool", nc.gpsimd, mk("c_pool")),
            "sp": Eng("sp", nc.sync, None),
        }
        self.dpool = [mk(f"d{i}") for i in range(ndma)]
        self.dnext = 0
        self.pstack = None

    def _need(self, E, deps):
        for key, val in deps:
            if E.name == "pe" and E.sem is not None and key == E.sem.key:
                continue
            if E.waited.get(key, 0) >= val:
                continue
            E.h.wait_ge(self.sems[key].h, val)
            E.waited[key] = val

    def _deps(self, reads, writes, own=None):
        deps = []
        for b in reads:
            if b.lw:
                deps.append(b.lw)
            if b.excl:
                deps.extend((k, v) for k, v in b.rd.items() if k != own)
        for b in writes:
            if b.lw and b.lw[0] != own:
                deps.append(b.lw)
            deps.extend((k, v) for k, v in b.rd.items() if k != own)
        return deps

    def _mark(self, reads, writes, tk):
        for b in reads:
            if b.rd.get(tk[0], 0) < tk[1]:
                b.rd[tk[0]] = tk[1]
        for b in writes:
            b.lw = tk
            b.rd = {}

    def op(self, eng, fn, reads=(), writes=(), inc=True):
        E = self.E[eng]
        self._need(E, self._deps(reads, writes, E.sem.key))
        ins = fn(E.h)
        if inc:
            E.sem.count += 1
            ins.then_inc(E.sem.h, 1)
            tk = (E.sem.key, E.sem.count)
        else:
            tk = (E.sem.key, E.sem.count + 1)
        self._mark(reads, writes, tk)
        return ins

    def dma(self, q, out, in_, reads=(), writes=(), sem=None):
        Q = self.E[q]
        self._need(Q, self._deps(reads, writes))
        ins = Q.h.dma_start(out=out, in_=in_)
        sem.count += 16
        ins.then_inc(sem.h, 16)
        self._mark(reads, writes, (sem.key, sem.count))

    def barrier(self):
        allt = [(s.key, s.count) for s in self.sems.values() if s.count > 0]
        for E in self.E.values():
            self._need(E, allt)

    @contextlib.contextmanager
    def phase(self):
        self.dnext = 0
        with contextlib.ExitStack() as ps:
            self.pstack = ps
            yield
            self.barrier()
        self.pstack = None

    def sb(self, name, shape, dt, dma=False):
        self.uid += 1
        t = self.pstack.enter_context(self.nc.sbuf_tensor(f"{name}_{self.uid}", shape, dt))
        b = Buf(t, name)
        if dma:
            b.dsem = self.dpool[self.dnext]
            self.dnext += 1
        return b

    def sbring(self, name, n, shape, dt, dma=False):
        return Ring([self.sb(f"{name}{i}", shape, dt, dma) for i in range(n)])

    def psring(self, name, n, shape=(128, 512), dt=F32):
        bufs = []
        for i in range(n):
            self.uid += 1
            t = self.pstack.enter_context(self.nc.psum_tensor(f"{name}{i}_{self.uid}", list(shape), dt))
            b = Buf(t, name)
            b.excl = True
            bufs.append(b)
        return Ring(bufs)

    def load(self, buf, dst, src, q="sp"):
        self.dma(q, dst, src, writes=[buf], sem=buf.dsem)

    def store(self, buf, dst, src, q="sp"):
        self.dma(q, dst, src, reads=[buf], sem=buf.dsem)

    def act(self, out, in_, func, reads, writes, bias=None, scale=None, eng="act"):
        kw = {}
        if bias is not None:
            kw["bias"] = bias
        if scale is not None:
            kw["scale"] = scale
        return self.op(eng, lambda e: e.activation(out=out, in_=in_, func=func, **kw), reads, writes)

    def tt(self, out, in0, in1, op, reads, writes, eng="dve"):
        return self.op(eng, lambda e: e.tensor_tensor(out=out, in0=in0, in1=in1, op=op), reads, writes)

    def ts(self, out, in0, s1, s2, op0, op1, reads, writes, eng="dve"):
        if s2 is None:
            return self.op(eng, lambda e: e.tensor_scalar(out=out, in0=in0, scalar1=s1, scalar2=None, op0=op0), reads, writes)
        return self.op(eng, lambda e: e.tensor_scalar(out=out, in0=in0, scalar1=s1, scalar2=s2, op0=op0, op1=op1), reads, writes)

    def stt(self, out, in0, scalar, in1, op0, op1, reads, writes, eng="dve"):
        return self.op(eng, lambda e: e.scalar_tensor_tensor(out=out, in0=in0, scalar=scalar, in1=in1, op0=op0, op1=op1), reads, writes)

    def copy(self, out, in_, reads, writes, eng="dve"):
        if eng == "act":
            return self.op(eng, lambda e: e.copy(out=out, in_=in_), reads, writes)
        return self.op(eng, lambda e: e.tensor_copy(out=out, in_=in_), reads, writes)


def ps_ap(b):
    return b.t[:, :]


class G:
    pass


def matmul(P, ps, out_ap, lhsT, rhs, start, stop, reads):
    P.op("pe", lambda e: e.matmul(out_ap, lhsT, rhs, start=start, stop=stop),
         reads=reads, writes=[ps], inc=stop)


def norm_res(P, Kc, TT=256):
    r = G()
    r.TT = TT
    r.hst = P.sbring("hst", 2, [128, Kc, TT], F32, dma=True)
    r.sq = P.sbring("sq", 1, [128, Kc, TT], BF16)
    r.rstd = P.sbring("rstd", 2, [128, TT], F32)
    return r


def norm_block_gen(P, g, r, src, tok0, TB, gain, hn, Kc, psring, out_f32=None):
    TT = r.TT
    nfeat = Kc * 128
    for i in range(TB // TT):
        c0 = tok0 + i * TT
        hst = r.hst.next()
        P.load(hst, hst.t[:], src[0:nfeat, c0:c0 + TT].rearrange("(c p) t -> p c t", p=128))
        sq = r.sq.next()
        P.act(sq.t[:], hst.t[:], AF.Square, [hst], [sq])
        yield
        ps = psring.next()
        for c in range(Kc):
            matmul(P, ps, ps.t[:, 0:TT], g.ones.t[:], sq.t[:, c, :], c == 0, c == Kc - 1, [g.ones, sq])
        rstd = r.rstd.next()
        P.act(rstd.t[:], ps.t[:, 0:TT], AF.Sqrt, [ps], [rstd], bias=EPS, scale=1.0 / nfeat)
        P.op("dve", lambda e: e.reciprocal(out=rstd.t[:], in_=rstd.t[:]), [rstd], [rstd])
        if out_f32 is None:
            for c in range(Kc):
                P.stt(hn.t[:, c, i * TT:(i + 1) * TT], hst.t[:, c, :], gain.t[:, c:c + 1], rstd.t[:],
                      ALU.mult, ALU.mult, [hst, gain, rstd], [hn])
        else:
            stg_ring, dst_fn = out_f32
            st = stg_ring.next()
            for c in range(Kc):
                P.stt(st.t[:, c, :], hst.t[:, c, :], gain.t[:, c:c + 1], rstd.t[:],
                      ALU.mult, ALU.mult, [hst, gain, rstd], [st])
            P.store(st, dst_fn(c0, TT), st.t[:], q="pool")
        yield


def norm_block(*a, **kw):
    for _ in norm_block_gen(*a, **kw):
        pass


def gemm_fm_gen(P, act, Kc, wsrc, chunks, nt, wring, psring, post, pre=None):
    n = len(chunks)
    loaded = {}

    def ld(i):
        wb = wring.next()
        P.dma("pool", wb.t[:, 0:Kc * 128], wsrc(chunks[i]), writes=[wb], sem=wb.dsem)
        loaded[i] = wb

    pf = len(wring.b) - 1
    for i in range(min(pf, n)):
        ld(i)
    for i in range(n):
        if i + pf < n:
            ld(i + pf)
        wb = loaded.pop(i)
        for ti in range(nt):
            if pre is not None:
                pre(chunks[i], ti)
            ps = psring.next()
            for kc in range(Kc):
                matmul(P, ps, ps.t[:, :], wb.t[:, kc * 128:(kc + 1) * 128], act.t[:, kc, ti * 512:(ti + 1) * 512],
                       kc == 0, kc == Kc - 1, [wb, act])
            post(chunks[i], ti, ps)
            yield


def gemm_fm(*a, **kw):
    for _ in gemm_fm_gen(*a, **kw):
        pass


def gemm_tm(P, act, Kc, wsrc, panels, ntt, wring, psring, post):
    n = len(panels)
    loaded = {}

    def ld(i):
        wb = wring.next()
        P.dma("pool", wb.t[:, 0:Kc * 512], wsrc(panels[i]), writes=[wb], sem=wb.dsem)
        loaded[i] = wb

    ld(0)
    for i in range(n):
        if i + 1 < n:
            ld(i + 1)
        wb = loaded.pop(i)
        for tt in range(ntt):
            ps = psring.next()
            for kc in range(Kc):
                matmul(P, ps, ps.t[:, :], act.t[:, kc, tt * 128:(tt + 1) * 128], wb.t[:, kc * 512:(kc + 1) * 512],
                       kc == 0, kc == Kc - 1, [wb, act])
            post(panels[i], tt, ps)


def load_small(P, name, shape, src, dt=F32, q="sp"):
    b = P.sb(name, shape, dt, dma=True)
    P.load(b, b.t[:], src, q=q)
    return b


def phase_tables(P, g):
    INV2PI = float(1.0 / (2 * np.pi))
    C1 = 6.28125
    C2 = float(2 * np.pi - 6.28125)
    PI = float(np.pi)
    with P.phase():
        posi = load_small(P, "posi", [128, T], g.pos, dt=I32)
        posf = P.sb("posf", [128, T], F32)
        P.copy(posf.t[:], posi.t[:], [posi], [posf])
        ang = P.sb("ang", [128, T], F32)
        kf = P.sb("kf", [128, T], F32)
        ki = P.sb("ki", [128, T], I32)
        out = P.sbring("tout", 2, [128, T], F32, dma=True)
        cs = g.consts
        for (fcol, dC, dS, scol) in ((1, g.tCm, g.tSm, 3), (2, g.tCr, g.tSr, None)):
            P.ts(ang.t[:], posf.t[:], cs.t[:, fcol:fcol + 1], None, ALU.mult, None, [posf, cs], [ang])
            P.ts(kf.t[:], ang.t[:], INV2PI, None, ALU.mult, None, [ang], [kf])
            P.copy(ki.t[:], kf.t[:], [kf], [ki])
            P.copy(kf.t[:], ki.t[:], [ki], [kf])
            P.stt(ang.t[:], kf.t[:], -C1, ang.t[:], ALU.mult, ALU.add, [kf, ang], [ang])
            P.stt(ang.t[:], kf.t[:], -C2, ang.t[:], ALU.mult, ALU.add, [kf, ang], [ang])
            P.ts(ang.t[:], ang.t[:], -PI, PI, ALU.max, ALU.min, [ang], [ang])
            o = out.next()
            P.act(o.t[:], ang.t[:], AF.Sin, [ang], [o])
            if scol is not None:
                P.ts(o.t[:], o.t[:], cs.t[:, scol:scol + 1], None, ALU.mult, None, [o, cs], [o])
            P.store(o, dS, o.t[:])
            P.stt(kf.t[:], ang.t[:], -1.0, ang.t[:], ALU.mult, ALU.max, [ang], [kf])
            o = out.next()
            P.act(o.t[:], kf.t[:], AF.Sin, [kf], [o], bias=float(np.pi / 2), scale=-1.0)
            P.store(o, dC, o.t[:])


def phase_E1(P, g, L, j, hsrc):
    with P.phase():
        gain = load_small(P, "gain", [128, 16], g.mix_g[L])
        lng = load_small(P, "lng", [128, 1024], g.sgu_lng[j])
        lnb = load_small(P, "lnb", [128, 1024], g.sgu_lnb[j])
        nr = norm_res(P, 16)
        hn = P.sb("hn", [128, 16, SEQ], BF16)
        psr = P.psring("ps", 8)
        wring = P.sbring("w", 3, [128, 2048], BF16, dma=True)
        wv = P.sbring("wv", 2, [128, 16 * 512], BF16, dma=True)
        stg = P.sbring("stg", 3, [128, 512], F32, dma=True)
        stgb = P.sbring("stgb", 3, [128, 512], BF16, dma=True)
        tab = P.sbring("tab", 2, [128, SEQ], F32, dma=True)
        xv = P.sbring("xv", 2, [128, 1024], F32)
        xo = P.sbring("xo", 2, [128, 1024], BF16, dma=True)
        st6 = P.sbring("st6", 2, [128, 12], F32)
        mv = P.sbring("mv", 2, [128, 2], F32)
        tmp = P.sbring("tmp", 2, [64, 512], F32)
        for s in range(NSEQ):
            tok0 = s * SEQ
            norm_block(P, g, nr, hsrc, tok0, SEQ, gain, hn, 16, psr)
            tC = tab.next()
            P.load(tC, tC.t[:], g.tCm[:, tok0:tok0 + SEQ])
            tS = tab.next()
            P.load(tS, tS.t[:], g.tSm[:, tok0:tok0 + SEQ])

            def post(ci, ti, ps, tok0=tok0):
                cols = slice(tok0 + ti * 512, tok0 + (ti + 1) * 512)
                if ci < 4:
                    st = stg.next()
                    P.copy(st.t[:], ps.t[:], [ps], [st], eng="act")
                    P.store(st, g.cqT[ci * 128:(ci + 1) * 128, cols], st.t[:])
                elif ci < 6:
                    st = stg.next()
                    P.copy(st.t[:], ps.t[:], [ps], [st], eng="act")
                    P.store(st, g.ckvT[(ci - 4) * 128:(ci - 3) * 128, cols], st.t[:])
                else:
                    st = stgb.next()
                    P.act(st.t[:], ps.t[:], AF.Gelu_apprx_tanh, [ps], [st])
                    P.store(st, g.abT[1024 + (ci - 7) * 128:1024 + (ci - 6) * 128, cols], st.t[:])

            fm = gemm_fm_gen(P, hn, 16, lambda ci: g.w_in_fm[j, ci], [0, 1, 2, 3, 4, 5, 7, 8, 9, 10, 11, 12, 13, 14], 4, wring, psr, post)
            w0 = wv.next()
            P.dma("pool", w0.t[:], g.w_in_tm[j, 0], writes=[w0], sem=w0.dsem)
            w1 = wv.next()
            P.dma("pool", w1.t[:], g.w_in_tm[j, 1], writes=[w1], sem=w1.dsem)
            for tt in range(SEQ // 128):
                for _ in range(4):
                    next(fm, None)
                x = xv.next()
                for pi, wb2 in enumerate((w0, w1)):
                    ps = psr.next()
                    for kc in range(16):
                        matmul(P, ps, ps.t[:, :], hn.t[:, kc, tt * 128:(tt + 1) * 128], wb2.t[:, kc * 512:(kc + 1) * 512], kc == 0, kc == 15, [wb2, hn])
                    P.act(x.t[:, pi * 512:(pi + 1) * 512], ps.t[:], AF.Gelu_apprx_tanh, [ps], [x])
                s6 = st6.next()
                P.op("dve", lambda e: e.bn_stats(out=s6.t[:, 0:6], in_=x.t[:, 0:512]), [x], [s6])
                P.op("dve", lambda e: e.bn_stats(out=s6.t[:, 6:12], in_=x.t[:, 512:1024]), [x], [s6])
                m = mv.next()
                P.op("dve", lambda e: e.bn_aggr(out=m.t[:], in_=s6.t[:]), [s6], [m])
                P.act(m.t[:, 1:2], m.t[:, 1:2], AF.Sqrt, [m], [m], bias=EPS, scale=1.0)
                P.op("dve", lambda e: e.reciprocal(out=m.t[:, 1:2], in_=m.t[:, 1:2]), [m], [m])
                P.ts(x.t[:], x.t[:], m.t[:, 0:1], m.t[:, 1:2], ALU.subtract, ALU.mult, [x, m], [x])
                P.tt(x.t[:], x.t[:], lng.t[:], ALU.mult, [x, lng], [x])
                o = xo.next()
                P.tt(o.t[:], x.t[:], lnb.t[:], ALU.add, [x, lnb], [o])
                P.store(o, g.vnTM[tok0 + tt * 128:tok0 + (tt + 1) * 128, :], o.t[:])
            for _ in fm:
                pass
            wb = wring.next()
            P.dma("pool", wb.t[:], g.w_in_fm[j, 6], writes=[wb], sem=wb.dsem)
            for ti in range(4):
                cols = slice(tok0 + ti * 512, tok0 + (ti + 1) * 512)
                pa = psr.next()
                for kc in range(16):
                    matmul(P, pa, pa.t[0:64, :], wb.t[:, kc * 128:kc * 128 + 64], hn.t[:, kc, ti * 512:(ti + 1) * 512], kc == 0, kc == 15, [wb, hn])
                pb = psr.next()
                for kc in range(16):
                    matmul(P, pb, pb.t[0:64, :], wb.t[:, kc * 128 + 64:kc * 128 + 128], hn.t[:, kc, ti * 512:(ti + 1) * 512], kc == 0, kc == 15, [wb, hn])
                t1 = tmp.next()
                P.tt(t1.t[:], pa.t[0:64, :], tC.t[0:64, ti * 512:(ti + 1) * 512], ALU.mult, [pa, tC], [t1])
                t2 = tmp.next()
                P.tt(t2.t[:], pb.t[0:64, :], tS.t[0:64, ti * 512:(ti + 1) * 512], ALU.mult, [pb, tS], [t2])
                st = stgb.next()
                P.tt(st.t[0:64, :], t1.t[:], t2.t[:], ALU.add, [t1, t2], [st])
                P.store(st, g.kpeT[:, cols], st.t[0:64, :])


def phase_E2(P, g, j):
    with P.phase():
        gain = load_small(P, "gain", [128, 4], g.qn_g[j])
        nr = norm_res(P, 4)
        hn = P.sb("hn", [128, 4, SEQ], BF16)
        psr = P.psring("ps", 8)
        wring = P.sbring("w", 3, [128, 512], BF16, dma=True)
        stgb = P.sbring("stgb", 3, [128, 512], BF16, dma=True)
        tab = P.sbring("tab", 2, [128, SEQ], F32, dma=True)
        tmp = P.sbring("tmp", 3, [128, 512], F32)
        for s in range(NSEQ):
            tok0 = s * SEQ
            norm_block(P, g, nr, g.cqT, tok0, SEQ, gain, hn, 4, psr)
            tC = tab.next()
            P.load(tC, tC.t[:], g.tCm[:, tok0:tok0 + SEQ])
            tS = tab.next()
            P.load(tS, tS.t[:], g.tSm[:, tok0:tok0 + SEQ])

            def post(ci, ti, ps, tok0=tok0):
                cols = slice(tok0 + ti * 512, tok0 + (ti + 1) * 512)
                st = stgb.next()
                P.copy(st.t[:], ps.t[:], [ps], [st], eng="act")
                P.store(st, g.qnT[ci * 128:(ci + 1) * 128, cols], st.t[:])

            gemm_fm(P, hn, 4, lambda ci: g.w_q_fm[j, ci], list(range(8)), 4, wring, psr, post)
            for c in range(4):
                wa = wring.next()
                P.dma("pool", wa.t[:], g.w_q_fm[j, 8 + c], writes=[wa], sem=wa.dsem)
                wb = wring.next()
                P.dma("pool", wb.t[:], g.w_q_fm[j, 12 + c], writes=[wb], sem=wb.dsem)
                for ti in range(4):
                    cols = slice(tok0 + ti * 512, tok0 + (ti + 1) * 512)
                    tsl = slice(ti * 512, (ti + 1) * 512)
                    pa = psr.next()
                    for kc in range(4):
                        matmul(P, pa, pa.t[:, :], wa.t[:, kc * 128:(kc + 1) * 128], hn.t[:, kc, tsl], kc == 0, kc == 3, [wa, hn])
                    pb = psr.next()
                    for kc in range(4):
                        matmul(P, pb, pb.t[:, :], wb.t[:, kc * 128:(kc + 1) * 128], hn.t[:, kc, tsl], kc == 0, kc == 3, [wb, hn])
                    t1 = tmp.next()
                    P.tt(t1.t[:], pa.t[:], tC.t[:, tsl], ALU.mult, [pa, tC], [t1])
                    t2 = tmp.next()
                    P.tt(t2.t[:], pb.t[:], tS.t[:, tsl], ALU.mult, [pb, tS], [t2])
                    st = stgb.next()
                    P.tt(st.t[:], t1.t[:], t2.t[:], ALU.add, [t1, t2], [st])
                    P.store(st, g.qrT[c * 128:(c + 1) * 128, cols], st.t[:])


def phase_E3(P, g, j):
    with P.phase():
        gain = load_small(P, "gain", [128, 2], g.kvn_g[j])
        nr = norm_res(P, 2)
        hn = P.sb("hn", [128, 2, SEQ], BF16)
        psr = P.psring("ps", 8)
        wring = P.sbring("w", 3, [128, 256], BF16, dma=True)
        wv = P.sbring("wv", 2, [128, 2 * 512], BF16, dma=True)
        stgb = P.sbring("stgb", 4, [128, 512], BF16, dma=True)
        for s in range(NSEQ):
            tok0 = s * SEQ
            norm_block(P, g, nr, g.ckvT, tok0, SEQ, gain, hn, 2, psr)

            def post(ci, ti, ps, tok0=tok0):
                cols = slice(tok0 + ti * 512, tok0 + (ti + 1) * 512)
                st = stgb.next()
                P.copy(st.t[:], ps.t[:], [ps], [st], eng="act")
                P.store(st, g.knT[ci * 128:(ci + 1) * 128, cols], st.t[:])

            gemm_fm(P, hn, 2, lambda ci: g.w_kv_fm[j, ci], list(range(8)), 4, wring, psr, post)

            def postv(pi, tt, ps, tok0=tok0):
                st = stgb.next()
                P.copy(st.t[:], ps.t[:], [ps], [st], eng="act")
                P.store(st, g.vmTM[tok0 + tt * 128:tok0 + (tt + 1) * 128, pi * 512:(pi + 1) * 512], st.t[:])

            gemm_tm(P, hn, 2, lambda pi: g.w_kv_tm[j, pi], [0, 1], SEQ // 128, wv, psr, postv)


def phase_E4(P, g):
    scale = float((128 + 64) ** -0.5)
    LOOK = 3
    with P.phase():
        kpe = P.sb("kpe", [64, SEQ], BF16, dma=True)
        kn = P.sbring("kn", 2, [128, SEQ], BF16, dma=True)
        qn = P.sbring("qn", 2, [128, SEQ], BF16, dma=True)
        qr = P.sbring("qr", 2, [64, SEQ], BF16, dma=True)
        vv = P.sbring("vv", 2, [128, 16, 128], BF16, dma=True)
        pT = P.sbring("pT", 4, [128, 512], BF16)
        rec = P.sbring("rec", 2, [128, 512], F32)
        ost = P.sbring("ost", 2, [128, 512], BF16, dma=True)
        ps_s = P.psring("pss", 4)
        ps_o = P.psring("pso", 2)
        ps_d = P.psring("psd", 2)
        for s in range(NSEQ):
            tok0 = s * SEQ
            P.load(kpe, kpe.t[:], g.kpeT[:, tok0:tok0 + SEQ])
            heads = {}
            acc = {}

            def head(h, tok0=tok0):
                if h not in heads:
                    k_ = kn.next()
                    P.load(k_, k_.t[:], g.knT[h * 128:(h + 1) * 128, tok0:tok0 + SEQ])
                    q_ = qn.next()
                    P.load(q_, q_.t[:], g.qnT[h * 128:(h + 1) * 128, tok0:tok0 + SEQ])
                    r_ = qr.next()
                    P.load(r_, r_.t[:], g.qrT[h * 64:(h + 1) * 64, tok0:tok0 + SEQ])
                    v_ = vv.next()
                    P.load(v_, v_.t[:], g.vmTM[tok0:tok0 + SEQ, h * 128:(h + 1) * 128].rearrange("(b p) d -> p b d", p=128))
                    heads[h] = (k_, q_, r_, v_)
                return heads[h]

            def S(step):
                h, jq, kb = step
                k_, q_, r_, v_ = head(h)
                c0 = max(0, kb - 4 * jq) * 128
                qs = slice(jq * 512 + c0, (jq + 1) * 512)
                ks = slice(kb * 128, (kb + 1) * 128)
                pss = ps_s.next()
                matmul(P, pss, pss.t[:, c0:512], k_.t[:, ks], q_.t[:, qs], True, False, [k_, q_])
                matmul(P, pss, pss.t[:, c0:512], kpe.t[0:64, ks], r_.t[0:64, qs], False, True, [kpe, r_])
                return pss, c0

            def rest(step, pss, c0, tok0=tok0):
                h, jq, kb = step
                k_, q_, r_, v_ = heads[h]
                nkb = 4 * jq + 4
                if kb == 0:
                    acc[(h, jq)] = (ps_o.next(), ps_d.next())
                    if jq == 2 and h + 1 < 8:
                        head(h + 1)
                po, pd = acc[(h, jq)]
                p_ = pT.next()
                P.act(p_.t[:, c0:512], pss.t[:, c0:512], AF.Exp, [pss], [p_], scale=scale)
                if kb >= 4 * jq:
                    P.op("dve", lambda e: e.memset(p_.t[64:128, c0:c0 + 64], 0.0), [], [p_])
                last = kb == nkb - 1
                P.op("pe", lambda e: e.matmul(po.t[:, c0:512], v_.t[:, kb, :], p_.t[:, c0:512], start=(kb == 0), stop=last),
                     reads=[v_, p_], writes=[po], inc=last)
                P.op("pe", lambda e: e.matmul(pd.t[:, c0:512], g.ones.t[:], p_.t[:, c0:512], start=(kb == 0), stop=last),
                     reads=[g.ones, p_], writes=[pd], inc=True)
                if last:
                    rc = rec.next()
                    P.op("dve", lambda e: e.reciprocal(out=rc.t[:], in_=pd.t[:]), [pd], [rc])
                    o = ost.next()
                    P.tt(o.t[:], po.t[:], rc.t[:], ALU.mult, [po, rc], [o])
                    P.store(o, g.abT[h * 128:(h + 1) * 128, tok0 + jq * 512:tok0 + (jq + 1) * 512], o.t[:])
                    del acc[(h, jq)]

            steps = [(h, jq, kb) for h in range(8) for jq in range(4) for kb in range(4 * jq + 4)]
            pendq = []
            for i in range(min(LOOK, len(steps))):
                pendq.append(S(steps[i]))
            for i, st in enumerate(steps):
                if i + LOOK < len(steps):
                    pendq.append(S(steps[i + LOOK]))
                pss, c0 = pendq.pop(0)
                rest(st, pss, c0)


def phase_E5(P, g, j):
    with P.phase():
        wsf = load_small(P, "wsf", [128, 1024], g.sgu_wsT[j])
        ws = P.sb("ws", [128, 1024], BF16)
        P.copy(ws.t[:], wsf.t[:], [wsf], [ws])
        for gi in range(8):
            P.op("dve", lambda e, gi=gi: e.memset(ws.t[64:128, gi * 128:gi * 128 + 64], 0.0), [], [ws])
        bs4 = load_small(P, "bs4", [128, 8 * 512], g.sgu_bs4[j])
        vn = P.sbring("vn", 2, [128, 4, 1024], BF16, dma=True)
        ug = P.sbring("ug", 16, [128, 512], BF16, dma=True)
        tmp = P.sbring("tmp", 3, [128, 512], F32)
        ost = P.sbring("ost", 4, [128, 512], BF16, dma=True)
        psr = P.psring("ps", 8)
        for ti in range(T // 512):
            t0 = ti * 512
            v_ = vn.next()
            P.load(v_, v_.t[:], g.vnTM[t0:t0 + 512, :].rearrange("(b p) c -> p b c", p=128))
            pss_, us_ = [], []
            for gi in range(8):
                u_ = ug.next()
                P.load(u_, u_.t[:], g.abT[1024 + gi * 128:1024 + (gi + 1) * 128, t0:t0 + 512])
                us_.append(u_)
                ps = psr.next()
                for b in range(4):
                    P.op("pe", lambda e, ps=ps, v_=v_, b=b, gi=gi: e.matmul(
                        ps.t[:, b * 128:(b + 1) * 128], v_.t[:, b, gi * 128:(gi + 1) * 128], ws.t[:, gi * 128:(gi + 1) * 128],
                        start=True, stop=True), reads=[v_, ws], writes=[ps], inc=(b == 3))
                pss_.append(ps)
            for gi in range(8):
                ps = pss_[gi]
                u_ = us_[gi]
                t_ = tmp.next()
                P.tt(t_.t[:], ps.t[:], bs4.t[:, gi * 512:(gi + 1) * 512], ALU.add, [ps, bs4], [t_])
                o = ost.next()
                P.tt(o.t[:], t_.t[:], u_.t[:], ALU.mult, [t_, u_], [o])
                P.store(o, g.abT[1024 + gi * 128:1024 + (gi + 1) * 128, t0:t0 + 512], o.t[:])


def phase_proj_res(P, g, src, Kc, wfm, hsrc, hdst, TB):
    big = Kc >= 44
    with P.phase():
        acts = P.sbring("act", 2, [128, Kc, TB], BF16, dma=True)
        psr = P.psring("ps", 8)
        wring = P.sbring("w", 2 if big else 3, [128, Kc * 128], BF16, dma=True)
        hres = P.sbring("hres", 2 if big else 4, [128, 512], F32, dma=True)
        pend = {}
        nblk = T // TB
        grp = max(1, (2 << 20) // (128 * TB * 2))
        qsel = [0]

        def load_act(blk):
            a = acts.next()
            tok0 = blk * TB
            for c0 in range(0, Kc, grp):
                c1 = min(Kc, c0 + grp)
                q = "sp"
                qsel[0] += 1
                P.dma(q, a.t[:, c0:c1, :], src[c0 * 128:c1 * 128, tok0:tok0 + TB].rearrange("(c p) t -> p c t", p=128),
                      writes=[a], sem=a.dsem)
            return a

        nxt = load_act(0)
        for blk in range(nblk):
            tok0 = blk * TB
            act = nxt
            if blk + 1 < nblk:
                nxt = load_act(blk + 1)

            def pre(ci, ti, tok0=tok0):
                hr = hres.next()
                P.load(hr, hr.t[:], hsrc[ci * 128:(ci + 1) * 128, tok0 + ti * 512:tok0 + (ti + 1) * 512])
                pend[(ci, ti)] = hr

            def post(ci, ti, ps, tok0=tok0):
                hr = pend.pop((ci, ti))
                P.tt(hr.t[:], ps.t[:], hr.t[:], ALU.add, [ps, hr], [hr])
                P.store(hr, hdst[ci * 128:(ci + 1) * 128, tok0 + ti * 512:tok0 + (ti + 1) * 512], hr.t[:], q="act")

            gemm_fm(P, act, Kc, lambda ci: wfm[ci], list(range(16)), TB // 512, wring, psr, post, pre)


def phase_F1(P, g, L):
    with P.phase():
        gain = load_small(P, "gain", [128, 16], g.ffn_g[L])
        cw = load_small(P, "cw", [128, 88 * 3], g.conv_w[L])
        cb = load_small(P, "cb", [128, 88], g.conv_b[L])
        nr = norm_res(P, 16, TT=128)
        hns = [P.sb("hn0", [128, 16, SEQ], BF16), P.sb("hn1", [128, 16, SEQ], BF16)]
        psr = P.psring("ps", 8)
        wring = P.sbring("w", 4, [128, 2048], BF16, dma=True)
        asb = P.sbring("asb", 4, [128, 514], F32)
        cc = P.sbring("cc", 4, [128, 512], F32)
        gl = P.sbring("gl", 2, [128, 512], F32)
        ptmp = P.sbring("ptmp", 2, [128, 512], F32)
        ost = P.sbring("ost", 3, [128, 512], BF16, dma=True)
        norm_block(P, g, nr, g.hT, 0, SEQ, gain, hns[0], 16, psr)
        for s in range(NSEQ):
            tok0 = s * SEQ
            hn = hns[s % 2]
            nxt = norm_block_gen(P, g, nr, g.hT, tok0 + SEQ, SEQ, gain, hns[(s + 1) % 2], 16, psr) if s + 1 < NSEQ else iter(())
            loaded = {}
            it = 0

            def ld(i):
                for half in (0, 1):
                    wb = wring.next()
                    P.dma("pool", wb.t[:], g.ffn_up_fm[L, i + 44 * half], writes=[wb], sem=wb.dsem)
                    loaded[(i, half)] = wb

            ld(0)
            for i in range(44):
                if i + 1 < 44:
                    ld(i + 1)
                wbs = (loaded.pop((i, 0)), loaded.pop((i, 1)))
                prev = [None, None]
                for ti in range(4):
                    it += 1
                    if it % 4 == 2:
                        next(nxt, None)
                    cres = []
                    for half in (0, 1):
                        ci = i + 44 * half
                        wb = wbs[half]
                        ps = psr.next()
                        for kc in range(16):
                            matmul(P, ps, ps.t[:, :], wb.t[:, kc * 128:(kc + 1) * 128], hn.t[:, kc, ti * 512:(ti + 1) * 512],
                                   kc == 0, kc == 15, [wb, hn])
                        a = asb.next()
                        ve = "dve"
                        P.copy(a.t[:, 2:514], ps.t[:], [ps], [a], eng="act")
                        if ti == 0:
                            P.op(ve, lambda e, a=a: e.memset(a.t[:, 0:2], 0.0), [], [a])
                        else:
                            P.copy(a.t[:, 0:2], prev[half].t[:, 512:514], [prev[half]], [a], eng=ve)
                        prev[half] = a
                        c = cc.next()
                        P.act(c.t[:], ps.t[:], AF.Identity, [ps, cw, cb], [c],
                              bias=cb.t[:, ci:ci + 1], scale=cw.t[:, ci * 3 + 2:ci * 3 + 3])
                        P.stt(c.t[:], a.t[:, 1:513], cw.t[:, ci * 3 + 1:ci * 3 + 2], c.t[:], ALU.mult, ALU.add, [a, cw, c], [c])
                        P.stt(c.t[:], a.t[:, 0:512], cw.t[:, ci * 3:ci * 3 + 1], c.t[:], ALU.mult, ALU.add, [a, cw, c], [c])
                        cres.append(c)
                    gg = gl.next()
                    P.act(gg.t[:], cres[0].t[:], AF.Gelu_apprx_tanh, [cres[0]], [gg])
                    o = ost.next()
                    P.tt(o.t[:], gg.t[:], cres[1].t[:], ALU.mult, [gg, cres[1]], [o], eng="pool")
                    P.store(o, g.ffT[i * 128:(i + 1) * 128, tok0 + ti * 512:tok0 + (ti + 1) * 512], o.t[:])


def phase_PLE(P, g, L):
    with P.phase():
        gain = load_small(P, "gain", [128, 16], g.ple_g[L])
        nr = norm_res(P, 16, TT=128)
        hns = [P.sb("hn0", [128, 16, SEQ], BF16), P.sb("hn1", [128, 16, SEQ], BF16)]
        pb = P.sb("pb", [128, 2, SEQ], BF16, dma=True)
        psr = P.psring("ps", 8)
        wring = P.sbring("w", 3, [128, 2048], BF16, dma=True)
        wup = P.sbring("wup", 3, [128, 256], BF16, dma=True)
        hres = P.sbring("hres", 4, [128, 512], F32, dma=True)
        sg = P.sbring("sg", 3, [128, 512], F32)
        norm_block(P, g, nr, g.hT, 0, SEQ, gain, hns[0], 16, psr)
        for s in range(NSEQ):
            tok0 = s * SEQ
            hn = hns[s % 2]
            nxt = norm_block_gen(P, g, nr, g.hT, tok0 + SEQ, SEQ, gain, hns[(s + 1) % 2], 16, psr) if s + 1 < NSEQ else iter(())
            it = 0
            P.dma("pool", pb.t[:], g.pT[L, :, tok0:tok0 + SEQ].rearrange("(c p) t -> p c t", p=128), writes=[pb], sem=pb.dsem)
            loaded = {}

            def ld(i):
                wb = wring.next()
                P.dma("pool", wb.t[:], g.ple_gate_fm[L, i], writes=[wb], sem=wb.dsem)
                wu = wup.next()
                P.dma("pool", wu.t[:], g.ple_up_fm[L, i], writes=[wu], sem=wu.dsem)
                loaded[i] = (wb, wu)

            ld(0)
            ld(1)
            for i in range(16):
                if i + 2 < 16:
                    ld(i + 2)
                wb, wu = loaded.pop(i)
                for ti in range(4):
                    it += 1
                    if it % 2 == 1:
                        next(nxt, None)
                    tsl = slice(ti * 512, (ti + 1) * 512)
                    cols = slice(tok0 + ti * 512, tok0 + (ti + 1) * 512)
                    hr = hres.next()
                    P.load(hr, hr.t[:], g.hT[i * 128:(i + 1) * 128, cols], q="act")
                    ps = psr.next()
                    for kc in range(16):
                        matmul(P, ps, ps.t[:, :], wb.t[:, kc * 128:(kc + 1) * 128], hn.t[:, kc, tsl], kc == 0, kc == 15, [wb, hn])
                    ps2 = psr.next()
                    for kc in range(2):
                        matmul(P, ps2, ps2.t[:, :], wu.t[:, kc * 128:(kc + 1) * 128], pb.t[:, kc, tsl], kc == 0, kc == 1, [wu, pb])
                    s_ = sg.next()
                    P.act(s_.t[:], ps.t[:], AF.Sigmoid, [ps], [s_])
                    P.tt(s_.t[:], ps2.t[:], s_.t[:], ALU.mult, [ps2, s_], [s_])
                    P.tt(hr.t[:], s_.t[:], hr.t[:], ALU.add, [s_, hr], [hr])
                    P.store(hr, g.hT[i * 128:(i + 1) * 128, cols], hr.t[:])


def phase_O1(P, g, L, j):
    with P.phase():
        gain = load_small(P, "gain", [128, 16], g.mix_g[L])
        gng = load_small(P, "gng", [128, 32], g.gn_g[j])
        gnb = load_small(P, "gnb", [128, 32], g.gn_b[j])
        sgf = P.sbring("sgf", 3, [128, 512], F32)
        qd4 = load_small(P, "qd4", [128, 8 * 512], g.qd4)
        nr = norm_res(P, 16)
        hn = P.sb("hn", [128, 16, SEQ], BF16)
        psr = P.psring("ps", 8)
        wring = P.sbring("w", 4, [128, 2048], BF16, dma=True)
        wv = P.sbring("wv", 2, [128, 16 * 512], BF16, dma=True)
        stgb = P.sbring("stgb", 6, [128, 512], BF16, dma=True)
        tC = P.sb("tC", [128, SEQ], F32, dma=True)
        tS = P.sb("tS", [128, SEQ], F32, dma=True)
        tmp = P.sbring("tmp", 4, [128, 512], F32)
        for s in range(NSEQ):
            tok0 = s * SEQ
            norm_block(P, g, nr, g.hT, tok0, SEQ, gain, hn, 16, psr)
            P.load(tC, tC.t[:], g.tCr[:, tok0:tok0 + SEQ])
            P.load(tS, tS.t[:], g.tSr[:, tok0:tok0 + SEQ])
            for which in (0, 1):
                sc = 1.0 if which == 0 else 1.0 / 16.0
                dst = g.rqT if which == 0 else g.rkT
                for h in range(8):
                    c1 = which * 16 + 2 * h
                    w1 = wring.next()
                    P.dma("pool", w1.t[:], g.ret_in_fm[j, c1], writes=[w1], sem=w1.dsem)
                    w2 = wring.next()
                    P.dma("pool", w2.t[:], g.ret_in_fm[j, c1 + 1], writes=[w2], sem=w2.dsem)
                    for ti in range(4):
                        tsl = slice(ti * 512, (ti + 1) * 512)
                        cols = slice(tok0 + ti * 512, tok0 + (ti + 1) * 512)
                        p1 = psr.next()
                        for kc in range(16):
                            matmul(P, p1, p1.t[:, :], w1.t[:, kc * 128:(kc + 1) * 128], hn.t[:, kc, tsl], kc == 0, kc == 15, [w1, hn])
                        p2 = psr.next()
                        for kc in range(16):
                            matmul(P, p2, p2.t[:, :], w2.t[:, kc * 128:(kc + 1) * 128], hn.t[:, kc, tsl], kc == 0, kc == 15, [w2, hn])
                        t1 = tmp.next()
                        P.stt(t1.t[:], p1.t[:], sc, tC.t[:, tsl], ALU.mult, ALU.mult, [p1, tC], [t1])
                        t2 = tmp.next()
                        P.stt(t2.t[:], p2.t[:], sc, tS.t[:, tsl], ALU.mult, ALU.mult, [p2, tS], [t2])
                        t3 = tmp.next()
                        P.stt(t3.t[:], p2.t[:], sc, tC.t[:, tsl], ALU.mult, ALU.mult, [p2, tC], [t3])
                        t4 = tmp.next()
                        P.stt(t4.t[:], p1.t[:], sc, tS.t[:, tsl], ALU.mult, ALU.mult, [p1, tS], [t4])
                        if which == 1:
                            o1 = stgb.next()
                            P.tt(o1.t[:], t1.t[:], t2.t[:], ALU.subtract, [t1, t2], [o1])
                            P.store(o1, dst[(2 * h) * 128:(2 * h + 1) * 128, cols], o1.t[:])
                            o2 = stgb.next()
                            P.tt(o2.t[:], t3.t[:], t4.t[:], ALU.add, [t3, t4], [o2])
                            P.store(o2, dst[(2 * h + 1) * 128:(2 * h + 2) * 128, cols], o2.t[:])
                        else:
                            P.tt(t1.t[:], t1.t[:], t2.t[:], ALU.subtract, [t1, t2], [t1])
                            P.tt(t3.t[:], t3.t[:], t4.t[:], ALU.add, [t3, t4], [t3])
                            for (tx, row) in ((t1, 2 * h), (t3, 2 * h + 1)):
                                o1 = stgb.next()
                                P.copy(o1.t[:], tx.t[:], [tx], [o1], eng="act")
                                P.store(o1, g.rqT[row * 128:(row + 1) * 128, cols], o1.t[:])
                                o2 = stgb.next()
                                P.tt(o2.t[:], tx.t[:], qd4.t[:, h * 512:(h + 1) * 512], ALU.mult, [tx, qd4], [o2])
                                P.store(o2, g.rqsT[row * 128:(row + 1) * 128, cols], o2.t[:])

            def postg(ci, ti, ps, tok0=tok0):
                c = ci - 32
                sg_ = sgf.next()
                P.act(sg_.t[:], ps.t[:], AF.Silu, [ps], [sg_])
                st = stgb.next()
                P.ts(st.t[:], sg_.t[:], gng.t[:, c:c + 1], None, ALU.mult, None, [sg_, gng], [st])
                P.store(st, g.rgT[c * 128:(c + 1) * 128, tok0 + ti * 512:tok0 + (ti + 1) * 512], st.t[:])
                st2 = stgb.next()
                P.ts(st2.t[:], sg_.t[:], gnb.t[:, c:c + 1], None, ALU.mult, None, [sg_, gnb], [st2])
                P.store(st2, g.rg2T[c * 128:(c + 1) * 128, tok0 + ti * 512:tok0 + (ti + 1) * 512], st2.t[:])

            gemm_fm(P, hn, 16, lambda ci: g.ret_in_fm[j, ci], list(range(32, 64)), 4, wring, psr, postg)

            def postv(pi, tt, ps, tok0=tok0):
                st = stgb.next()
                P.copy(st.t[:], ps.t[:], [ps], [st], eng="act")
                P.store(st, g.rvTM[tok0 + tt * 128:tok0 + (tt + 1) * 128, pi * 512:(pi + 1) * 512], st.t[:])

            gemm_tm(P, hn, 16, lambda pi: g.ret_in_tm[j, pi], list(range(8)), SEQ // 128, wv, psr, postv)


def phase_O2(P, g, j):
    NH = 2
    with P.phase():
        DT = load_small(P, "DT", [128, 8 * 128], g.DT)
        gng = load_small(P, "gng", [128, 32], g.gn_g[j])
        gnb = load_small(P, "gnb", [128, 32], g.gn_b[j])
        cs = g.consts
        kt = P.sbring("kt", 2 * NH, [128, 2, 512], BF16, dma=True)
        qt = P.sbring("qt", 2 * NH, [128, 2, 512], BF16, dma=True)
        qst = P.sbring("qst", 2 * NH, [128, 2, 512], BF16, dma=True)
        vt = P.sbring("vt", 2 * NH, [128, 4, 512], BF16, dma=True)
        gt = P.sbring("gt", 2 * NH, [128, 4, 512], BF16, dma=True)
        gt2 = P.sbring("gt2", 2 * NH, [128, 4, 512], BF16, dma=True)
        rst = P.sbring("rst", 2 * NH, [128, 4, 512], BF16, dma=True)
        AT = P.sbring("AT", 4, [128, 128], BF16)
        kz = P.sbring("kz", 4, [128, 256], BF16)
        states = [P.sb(f"state{i}", [128, 2, 512], F32) for i in range(NH)]
        stbfs = [P.sb(f"stbf{i}", [128, 2, 512], BF16) for i in range(NH)]
        st6 = P.sbring("st6", 4, [128, 6], F32)
        mv = P.sbring("mv", 4, [128, 2], F32)
        nmr = P.sbring("nmr", 4, [128, 1], F32)
        xh = P.sbring("xh", 4, [128, 512], BF16)
        r1 = P.sbring("r1", 4, [128, 4, 128], F32)
        psS = P.psring("pss", NH).b
        psO = P.psring("pso", NH).b
        psU = P.psring("psu", NH).b
        psB = P.psring("psb", NH, (128, 1024), BF16).b

        def step(h, hi, b, first, lastblk, bufs):
            k_, q_, qs_, v_, g_, ro, g2_ = bufs
            state = states[hi]
            stbf = stbfs[hi]
            pss, po, pu, pb = psS[hi], psO[hi], psU[hi], psB[hi]
            cd128 = float(g.cd128[h])
            bs = slice(b * 128, (b + 1) * 128)
            for dc in range(2):
                matmul(P, pss, pss.t[:, 0:128], k_.t[:, dc, bs], q_.t[:, dc, bs], dc == 0, dc == 1, [k_, q_])
            yield
            a_ = AT.next()
            P.tt(a_.t[:], pss.t[:, 0:128], DT.t[:, h * 128:(h + 1) * 128], ALU.mult, [pss, DT], [a_])
            yield
            P.op("pe", lambda e: e.matmul(po.t[:, :], a_.t[:], v_.t[:, b, :], start=True, stop=first), [a_, v_], [po], inc=first)
            if not first:
                for dc in range(2):
                    P.op("pe", lambda e: e.matmul(po.t[:, :], qs_.t[:, dc, bs], stbf.t[:, dc, :], start=False, stop=(dc == 1)),
                         [qs_, stbf], [po], inc=(dc == 1))
            if not lastblk:
                for dc in range(2):
                    P.op("pe", lambda e: e.transpose(pb.t[:, dc * 128:(dc + 1) * 128], k_.t[:, dc, bs], g.ident.t[:]),
                         [k_, g.ident], [pb], inc=(dc == 1))
            yield
            if not lastblk:
                z_ = kz.next()
                P.act(z_.t[:], pb.t[:, 0:256], AF.Identity, [pb, cs], [z_], scale=cs.t[:, 5 + h:6 + h])
            s6 = st6.next()
            P.op("dve", lambda e: e.bn_stats(out=s6.t[:], in_=po.t[:]), [po], [s6])
            m = mv.next()
            P.op("dve", lambda e: e.bn_aggr(out=m.t[:], in_=s6.t[:]), [s6], [m])
            yield
            if not lastblk:
                P.op("pe", lambda e: e.matmul(pu.t[:, :], z_.t[:, 0:128], v_.t[:, b, :], start=True, stop=True), [z_, v_], [pu], inc=True)
            P.act(m.t[:, 1:2], m.t[:, 1:2], AF.Sqrt, [m], [m], bias=EPS, scale=1.0)
            yield
            if not lastblk:
                if first:
                    P.copy(state.t[:, 0, :], pu.t[:], [pu], [state], eng="dve")
                else:
                    P.stt(state.t[:, 0, :], state.t[:, 0, :], cd128, pu.t[:], ALU.mult, ALU.add, [state, pu], [state])
            P.op("dve", lambda e: e.reciprocal(out=m.t[:, 1:2], in_=m.t[:, 1:2]), [m], [m])
            nm = nmr.next()
            P.ts(nm.t[:], m.t[:, 0:1], -1.0, m.t[:, 1:2], ALU.mult, ALU.mult, [m], [nm])
            x_ = xh.next()
            P.act(x_.t[:], po.t[:], AF.Identity, [po, m, nm], [x_], bias=nm.t[:, 0:1], scale=m.t[:, 1:2])
            yield
            if not lastblk:
                P.op("pe", lambda e: e.matmul(pu.t[:, :], z_.t[:, 128:256], v_.t[:, b, :], start=True, stop=True), [z_, v_], [pu], inc=True)
            for ec in range(4):
                P.op("pe", lambda e: e.transpose(pb.t[:, 256 + ec * 128:256 + (ec + 1) * 128], x_.t[:, ec * 128:(ec + 1) * 128], g.ident.t[:]),
                     [x_, g.ident], [pb], inc=(ec == 3))
            yield
            if not lastblk:
                if first:
                    P.copy(state.t[:, 1, :], pu.t[:], [pu], [state], eng="dve")
                else:
                    P.stt(state.t[:, 1, :], state.t[:, 1, :], cd128, pu.t[:], ALU.mult, ALU.add, [state, pu], [state])
            r_ = r1.next()
            P.tt(r_.t[:], pb.t[:, 256:768].rearrange("p (c t) -> p c t", c=4), g_.t[:, :, bs], ALU.mult, [pb, g_], [r_])
            yield
            if not lastblk:
                P.copy(stbf.t[:], state.t[:], [state], [stbf], eng="act")
            P.tt(ro.t[:, :, bs], r_.t[:], g2_.t[:, :, bs], ALU.add, [r_, g2_], [ro], eng="pool")

        def load_tile(s, hg, ti):
                    t0 = s * SEQ + ti * 512
                    bufs = {}
                    for hi in range(NH):
                        h = hg * NH + hi
                        rows2 = slice(h * 256, (h + 1) * 256)
                        k_ = kt.next()
                        P.load(k_, k_.t[:], g.rkT[rows2, t0:t0 + 512].rearrange("(c p) t -> p c t", p=128))
                        q_ = qt.next()
                        P.load(q_, q_.t[:], g.rqT[rows2, t0:t0 + 512].rearrange("(c p) t -> p c t", p=128))
                        qs_ = qst.next()
                        P.load(qs_, qs_.t[:], g.rqsT[rows2, t0:t0 + 512].rearrange("(c p) t -> p c t", p=128))
                        v_ = vt.next()
                        P.load(v_, v_.t[:], g.rvTM[t0:t0 + 512, h * 512:(h + 1) * 512].rearrange("(b p) e -> p b e", p=128))
                        g_ = gt.next()
                        P.load(g_, g_.t[:], g.rgT[h * 512:(h + 1) * 512, t0:t0 + 512].rearrange("(c p) t -> p c t", p=128))
                        g2_ = gt2.next()
                        P.load(g2_, g2_.t[:], g.rg2T[h * 512:(h + 1) * 512, t0:t0 + 512].rearrange("(c p) t -> p c t", p=128))
                        bufs[hi] = (k_, q_, qs_, v_, g_, rst.next(), g2_)
                    return bufs

        items = [(s, hg, ti) for s in range(NSEQ) for hg in range(8 // NH) for ti in range(4)]
        nxt_bufs = load_tile(*items[0])
        for idx, (s, hg, ti) in enumerate(items):
                    t0 = s * SEQ + ti * 512
                    bufs = nxt_bufs
                    if idx + 1 < len(items):
                        nxt_bufs = load_tile(*items[idx + 1])
                    for b in range(4):
                        gens = [step(hg * NH + hi, hi, b, ti == 0 and b == 0, ti == 3 and b == 3, bufs[hi]) for hi in range(NH)]
                        while gens:
                            for gen in list(gens):
                                try:
                                    next(gen)
                                except StopIteration:
                                    gens.remove(gen)
                    for hi in range(NH):
                        h = hg * NH + hi
                        ro = bufs[hi][5]
                        P.store(ro, g.rrT[h * 512:(h + 1) * 512, t0:t0 + 512].rearrange("(c p) t -> p c t", p=128), ro.t[:], q="act")


def phase_final(P, g):
    with P.phase():
        gain = load_small(P, "gain", [128, 16], g.fin_g)
        nr = norm_res(P, 16, TT=512)
        psr = P.psring("ps", 4)
        stg = P.sbring("fst", 2, [128, 16, nr.TT], F32, dma=True)
        norm_block(P, g, nr, g.hT, 0, T, gain, None, 16, psr,
                   out_f32=(stg, lambda c0, TT: g.yT[:, c0:c0 + TT].rearrange("(c p) t -> p c t", p=128)))


def build_program(cst, nlayers=DEPTH, dbg=None, stop=None):
    nc = bass.Bass("TRN2", target_bir_lowering=False)
    g = G()
    g.cd128 = cst["cd128"]

    def ext(name, shape, dt=F32):
        return nc.dram_tensor(name, list(shape), dt, kind="ExternalInput").ap()

    def scr(name, shape, dt):
        kind = "ExternalOutput" if (dbg is not None and name in dbg.split(",")) else "Internal"
        return nc.dram_tensor(name, list(shape), dt, kind=kind).ap()

    g.xT = ext("xT", [D, T])
    g.pT = ext("pT", [DEPTH, 256, T])
    g.pos = ext("pos", [128, T], I32)
    cin = ext("consts", [128, 16])
    identf = ext("ident", [128, 128])
    g.DT = ext("DT", [128, 1024])
    g.qd4 = ext("qd4", [128, 4096])
    g.mix_g = ext("mix_g", [DEPTH, 128, 16])
    g.ffn_g = ext("ffn_g", [DEPTH, 128, 16])
    g.ple_g = ext("ple_g", [DEPTH, 128, 16])
    g.fin_g = ext("fin_g", [128, 16])
    g.qn_g = ext("qn_g", [2, 128, 4])
    g.kvn_g = ext("kvn_g", [2, 128, 2])
    g.sgu_lng = ext("sgu_lng", [2, 128, 1024])
    g.sgu_lnb = ext("sgu_lnb", [2, 128, 1024])
    g.sgu_bs4 = ext("sgu_bs4", [2, 128, 4096])
    g.sgu_wsT = ext("sgu_wsT", [2, 128, 1024])
    g.gn_g = ext("gn_g", [2, 128, 32])
    g.gn_b = ext("gn_b", [2, 128, 32])
    g.conv_w = ext("conv_w", [DEPTH, 128, 264])
    g.conv_b = ext("conv_b", [DEPTH, 128, 88])
    g.w_in_fm = ext("w_in_fm", [2, 15, 128, 2048])
    g.w_in_tm = ext("w_in_tm", [2, 2, 128, 16 * 512])
    g.w_q_fm = ext("w_q_fm", [2, 16, 128, 512])
    g.w_kv_fm = ext("w_kv_fm", [2, 8, 128, 256])
    g.w_kv_tm = ext("w_kv_tm", [2, 2, 128, 2 * 512])
    g.w_out_fm = ext("w_out_fm", [2, 16, 128, 2048])
    g.ret_in_fm = ext("ret_in_fm", [2, 64, 128, 2048])
    g.ret_in_tm = ext("ret_in_tm", [2, 8, 128, 16 * 512])
    g.ret_out_fm = ext("ret_out_fm", [2, 16, 128, 4096])
    g.ffn_up_fm = ext("ffn_up_fm", [DEPTH, 88, 128, 2048])
    g.ffn_dn_fm = ext("ffn_dn_fm", [DEPTH, 16, 128, 5632])
    g.ple_gate_fm = ext("ple_gate_fm", [DEPTH, 16, 128, 2048])
    g.ple_up_fm = ext("ple_up_fm", [DEPTH, 16, 128, 256])

    if dbg is not None and "hT" in dbg.split(","):
        g.hT = nc.dram_tensor("hT", [D, T], F32, kind="ExternalOutput").ap()
        g.yT = None
    else:
        g.hT = nc.dram_tensor("hT", [D, T], F32, kind="Internal").ap()
        g.yT = nc.dram_tensor("yT", [D, T], F32, kind="ExternalOutput").ap() if dbg is None else None
    g.cqT = scr("cqT", [512, T], F32)
    g.ckvT = scr("ckvT", [256, T], F32)
    g.kpeT = scr("kpeT", [64, T], BF16)
    g.vnTM = scr("vnTM", [T, 1024], BF16)
    g.qnT = scr("qnT", [1024, T], BF16)
    g.qrT = scr("qrT", [512, T], BF16)
    g.knT = scr("knT", [1024, T], BF16)
    g.vmTM = scr("vmTM", [T, 1024], BF16)
    g.abT = scr("abT", [2048, T], BF16)
    g.rqT = scr("rqT", [2048, T], BF16)
    g.rqsT = scr("rqsT", [2048, T], BF16)
    g.rkT = scr("rkT", [2048, T], BF16)
    g.rvTM = scr("rvTM", [T, 4096], BF16)
    g.rgT = scr("rgT", [4096, T], BF16)
    g.rg2T = scr("rg2T", [4096, T], BF16)
    g.rrT = scr("rrT", [4096, T], BF16)
    g.ffT = scr("ffT", [DFF, T], BF16)
    g.tCm = scr("tCm", [128, T], F32)
    g.tSm = scr("tSm", [128, T], F32)
    g.tCr = scr("tCr", [128, T], F32)
    g.tSr = scr("tSr", [128, T], F32)

    with contextlib.ExitStack() as es:
        P = Prog(nc, es)
        P.pstack = es
        g.consts = P.sb("consts", [128, 16], F32, dma=True)
        g.ones = P.sb("ones", [128, 128], BF16)
        g.ident = P.sb("ident", [128, 128], BF16, dma=True)
        P.load(g.consts, g.consts.t[:], cin)
        P.dma("pool", g.ident.t[:], identf, writes=[g.ident], sem=g.ident.dsem)
        P.op("dve", lambda e: e.memset(g.ones.t[:], 1.0), [], [g.ones])
        ndma_persist = P.dnext

        def run():
            phase_tables(P, g)
            if stop == "tables":
                return
            for L in range(nlayers):
                j = L // 2
                hsrc = g.xT if L == 0 else g.hT
                if L % 2 == 0:
                    phase_E1(P, g, L, j, hsrc)
                    if stop == f"E1_{L}":
                        return
                    phase_E2(P, g, j)
                    phase_E3(P, g, j)
                    if stop == f"E3_{L}":
                        return
                    phase_E4(P, g)
                    if stop == f"E4_{L}":
                        return
                    phase_E5(P, g, j)
                    if stop == f"E5_{L}":
                        return
                    phase_proj_res(P, g, g.abT, 16, g.w_out_fm[j], hsrc, g.hT, SEQ)
                else:
                    phase_O1(P, g, L, j)
                    if stop == f"O1_{L}":
                        return
                    phase_O2(P, g, j)
                    if stop == f"O2_{L}":
                        return
                    phase_proj_res(P, g, g.rrT, 32, g.ret_out_fm[j], g.hT, g.hT, 1024)
                if stop == f"mix_{L}":
                    return
                phase_F1(P, g, L)
                if stop == f"F1_{L}":
                    return
                phase_proj_res(P, g, g.ffT, 44, g.ffn_dn_fm[L], g.hT, g.hT, 1024)
                if stop == f"ffn_{L}":
                    return
                phase_PLE(P, g, L)
                if stop == f"ple_{L}":
                    return
            if g.yT is not None:
                phase_final(P, g)

        orig_phase = P.phase

        @contextlib.contextmanager
        def phase_keep():
            with orig_phase():
                P.dnext = ndma_persist
                yield
        P.phase = phase_keep
        run()
        P.barrier()
    return nc


def _fm(W):
    K, N = W.shape
    return np.ascontiguousarray(W.reshape(K // 128, 128, N // 128, 128).transpose(2, 1, 0, 3).reshape(N // 128, 128, K))


def _tm(W):
    K, N = W.shape
    return np.ascontiguousarray(W.reshape(K // 128, 128, N // 512, 512).transpose(2, 1, 0, 3).reshape(N // 512, 128, (K // 128) * 512))


def _pc(v):
    return np.ascontiguousarray(v.reshape(-1, 128).T)


def module_constants():
    H = 8
    log_g = np.log1p(-(2.0 ** (-5.0 - np.arange(H, dtype=np.float64))))
    gam = np.exp(log_g)
    consts = np.zeros((128, 16), np.float32)
    p = np.arange(128)
    consts[:, 0] = -np.pi
    consts[:, 1] = 10000.0 ** (-(p % 32).astype(np.float64) / 32)
    consts[:, 2] = 10000.0 ** (-p.astype(np.float64) / 128)
    consts[:, 3] = np.where((p % 64) < 32, -1.0, 1.0)
    consts[:, 4] = -1.0
    for h in range(H):
        consts[:, 5 + h] = gam[h] ** (127 - p)
    i = p[:, None]
    jj = p[None, :]
    DT = np.zeros((128, H * 128), np.float32)
    for h in range(H):
        Dm = np.where((i // 64) >= (jj // 64), gam[h] ** np.abs(i - jj).astype(np.float64), 0.0)
        DT[:, h * 128:(h + 1) * 128] = Dm.T
    qd4 = np.zeros((128, H * 512), np.float32)
    for h in range(H):
        qd4[:, h * 512:(h + 1) * 512] = (gam[h] ** ((np.arange(512) % 128) + 1.0))[None, :]
    cd128 = gam ** 128
    return dict(consts=consts, DT=DT, qd4=qd4, cd128=cd128, ident=np.eye(128, dtype=np.float32))


def prep_shared(inp):
    sh = {}
    c = module_constants()
    sh["consts"] = c["consts"]
    sh["DT"] = c["DT"]
    sh["qd4"] = c["qd4"]
    sh["ident"] = c["ident"]
    f = np.float32
    sh["mix_g"] = np.stack([_pc(v) for v in inp["mix_norm_g"]]).astype(f)
    sh["ffn_g"] = np.stack([_pc(v) for v in inp["ffn_norm_g"]]).astype(f)
    sh["ple_g"] = np.stack([_pc(v) for v in inp["ple_norm_g"]]).astype(f)
    sh["fin_g"] = _pc(inp["final_norm_g"]).astype(f)
    sh["qn_g"] = np.stack([_pc(v) for v in inp["mla_q_norm_g"]]).astype(f)
    sh["kvn_g"] = np.stack([_pc(v) for v in inp["mla_kv_norm_g"]]).astype(f)
    sh["sgu_lng"] = np.ascontiguousarray(np.broadcast_to(inp["sgu_ln_g"][:, None, :], (2, 128, 1024))).astype(f)
    sh["sgu_lnb"] = np.ascontiguousarray(np.broadcast_to(inp["sgu_ln_b"][:, None, :], (2, 128, 1024))).astype(f)
    bs = np.tile(inp["sgu_b_s"][:, :, None, :], (1, 1, 4, 1)).reshape(2, 1, 8 * 512)
    sh["sgu_bs4"] = np.ascontiguousarray(np.broadcast_to(bs, (2, 128, 4096))).astype(f)
    sh["sgu_wsT"] = np.ascontiguousarray(inp["sgu_w_s"].transpose(0, 3, 1, 2).reshape(2, 128, 1024)).astype(f)
    sh["gn_g"] = np.stack([_pc(v) for v in inp["ret_gn_g"]]).astype(f)
    sh["gn_b"] = np.stack([_pc(v) for v in inp["ret_gn_b"]]).astype(f)
    cw = inp["ffn_conv_w"]
    sh["conv_w"] = np.ascontiguousarray(cw.reshape(4, 3, 88, 128).transpose(0, 3, 2, 1).reshape(4, 128, 264)).astype(f)
    sh["conv_b"] = np.stack([_pc(v) for v in inp["ffn_conv_b"]]).astype(f)
    swap64 = np.concatenate([np.arange(32, 64), np.arange(0, 32)])
    w_in_fm, w_in_tm, w_q_fm, w_kv_fm, w_kv_tm, w_out_fm = [], [], [], [], [], []
    for j in range(2):
        W = inp["even_w_in"][j]
        kpe = W[:, 768:832]
        fmcols = np.concatenate([W[:, 0:768], kpe, kpe[:, swap64], W[:, 832:1856]], axis=1)
        w_in_fm.append(_fm(fmcols))
        w_in_tm.append(_tm(W[:, 1856:2880]))
        Wq = inp["mla_w_q_up"][j].reshape(512, 8, 192)
        nope = Wq[:, :, 0:128].reshape(512, 1024)
        rope = Wq[:, :, 128:192]
        w_q_fm.append(_fm(np.concatenate([nope, rope.reshape(512, 512), rope[:, :, swap64].reshape(512, 512)], axis=1)))
        Wkv = inp["mla_w_kv_up"][j].reshape(256, 8, 256)
        w_kv_fm.append(_fm(np.ascontiguousarray(Wkv[:, :, 0:128]).reshape(256, 1024)))
        w_kv_tm.append(_tm(np.ascontiguousarray(Wkv[:, :, 128:256]).reshape(256, 1024)))
        w_out_fm.append(_fm(inp["even_w_out"][j]))
    sh["w_in_fm"] = np.stack(w_in_fm)
    sh["w_in_tm"] = np.stack(w_in_tm)
    sh["w_q_fm"] = np.stack(w_q_fm)
    sh["w_kv_fm"] = np.stack(w_kv_fm)
    sh["w_kv_tm"] = np.stack(w_kv_tm)
    sh["w_out_fm"] = np.stack(w_out_fm)
    ret_in_fm, ret_in_tm, ret_out_fm = [], [], []
    for j in range(2):
        W = inp["ret_w_in"][j]
        ret_in_fm.append(_fm(np.concatenate([W[:, 0:4096], W[:, 8192:12288]], axis=1)))
        ret_in_tm.append(_tm(W[:, 4096:8192]))
        ret_out_fm.append(_fm(inp["ret_w_out"][j]))
    sh["ret_in_fm"] = np.stack(ret_in_fm)
    sh["ret_in_tm"] = np.stack(ret_in_tm)
    sh["ret_out_fm"] = np.stack(ret_out_fm)
    sh["ffn_up_fm"] = np.stack([_fm(inp["ffn_w_up"][L]) for L in range(DEPTH)])
    sh["ffn_dn_fm"] = np.stack([_fm(inp["ffn_w_down"][L]) for L in range(DEPTH)])
    sh["ple_gate_fm"] = np.stack([_fm(inp["ple_w_gate"][L]) for L in range(DEPTH)])
    sh["ple_up_fm"] = np.stack([_fm(inp["ple_w_up"][L]) for L in range(DEPTH)])
    return sh, c


def prep_core(inp, core):
    b0 = core * NSEQ
    x = inp["x"][b0:b0 + NSEQ]
    xT = np.ascontiguousarray(x.reshape(T, D).T)
    p = inp["p"][:, b0:b0 + NSEQ]
    pT = np.ascontiguousarray(p.reshape(DEPTH, T, 256).transpose(0, 2, 1))
    pos = np.ascontiguousarray(np.broadcast_to(inp["positions"][b0:b0 + NSEQ].reshape(1, T), (128, T))).astype(np.int32)
    return {"xT": xT.astype(np.float32), "pT": pT.astype(np.float32), "pos": pos}


def kernel(**inputs):
    inp = {k: np.asarray(v) for k, v in inputs.items()}
    sh, c = prep_shared(inp)
    nc = build_program(c)
    in_maps = []
    for core in range(NCORES):
        m = dict(sh)
        m.update(prep_core(inp, core))
        in_maps.append(m)
    res = run_bass_kernel_spmd(nc, in_maps, core_ids=list(range(NCORES)))
    out = np.empty((NCORES * NSEQ, SEQ, D), np.float32)
    for core in range(NCORES):
        yT = np.asarray(res.results[core]["yT"])
        out[core * NSEQ:(core + 1) * NSEQ] = yT.T.reshape(NSEQ, SEQ, D)
    return out
```

```python
import contextlib
import numpy as np
import concourse.bass as bass
import concourse.mybir as mybir
from concourse.bass_utils import run_bass_kernel_spmd

F32 = mybir.dt.float32
BF16 = mybir.dt.bfloat16
I32 = mybir.dt.int32
AF = mybir.ActivationFunctionType
ALU = mybir.AluOpType

D = 2048
SEQ = 2048
NSEQ = 2
T = NSEQ * SEQ
DEPTH = 4
DFF = 5632
EPS = 1e-6
NCORES = 8


class Sem:
    def __init__(self, h, key):
        self.h = h
        self.key = key
        self.count = 0


class Buf:
    def __init__(self, t, name="", dsem=None):
        self.t = t
        self.name = name
        self.lw = None
        self.rd = {}
        self.dsem = dsem
        self.excl = False


class Eng:
    def __init__(self, name, h, sem):
        self.name = name
        self.h = h
        self.sem = sem
        self.waited = {}


class Ring:
    def __init__(self, bufs):
        self.b = bufs
        self.i = 0

    def next(self):
        b = self.b[self.i % len(self.b)]
        self.i += 1
        return b


class Prog:
    def __init__(self, nc, es, ndma=60):
        self.nc = nc
        self.es = es
        self.sems = {}
        self.uid = 0

        def mk(name):
            h = es.enter_context(nc.semaphore(name))
            s = Sem(h, name)
            self.sems[name] = s
            return s

        self.E = {
            "pe": Eng("pe", nc.tensor, mk("c_pe")),
            "act": Eng("act", nc.scalar, mk("c_act")),
            "dve": Eng("dve", nc.vector, mk("c_dve")),
            "pool": Eng("pool", nc.gpsimd, mk("c_pool")),
            "sp": Eng("sp", nc.sync, None),
        }
        self.dpool = [mk(f"d{i}") for i in range(ndma)]
        self.dnext = 0
        self.pstack = None

    def _need(self, E, deps):
        for key, val in deps:
            if E.name == "pe" and E.sem is not None and key == E.sem.key:
                continue
            if E.waited.get(key, 0) >= val:
                continue
            E.h.wait_ge(self.sems[key].h, val)
            E.waited[key] = val

    def _deps(self, reads, writes, own=None):
        deps = []
        for b in reads:
            if b.lw:
                deps.append(b.lw)
            if b.excl:
                deps.extend((k, v) for k, v in b.rd.items() if k != own)
        for b in writes:
            if b.lw and b.lw[0] != own:
                deps.append(b.lw)
            deps.extend((k, v) for k, v in b.rd.items() if k != own)
        return deps

    def _mark(self, reads, writes, tk):
        for b in reads:
            if b.rd.get(tk[0], 0) < tk[1]:
                b.rd[tk[0]] = tk[1]
        for b in writes:
            b.lw = tk
            b.rd = {}

    def op(self, eng, fn, reads=(), writes=(), inc=True):
        E = self.E[eng]
        self._need(E, self._deps(reads, writes, E.sem.key))
        ins = fn(E.h)
        if inc:
            E.sem.count += 1
            ins.then_inc(E.sem.h, 1)
            tk = (E.sem.key, E.sem.count)
        else:
            tk = (E.sem.key, E.sem.count + 1)
        self._mark(reads, writes, tk)
        return ins

    def dma(self, q, out, in_, reads=(), writes=(), sem=None):
        Q = self.E[q]
        self._need(Q, self._deps(reads, writes))
        ins = Q.h.dma_start(out=out, in_=in_)
        sem.count += 16
        ins.then_inc(sem.h, 16)
        self._mark(reads, writes, (sem.key, sem.count))

    def barrier(self):
        allt = [(s.key, s.count) for s in self.sems.values() if s.count > 0]
        for E in self.E.values():
            self._need(E, allt)

    @contextlib.contextmanager
    def phase(self):
        self.dnext = 0
        with contextlib.ExitStack() as ps:
            self.pstack = ps
            yield
            self.barrier()
        self.pstack = None

    def sb(self, name, shape, dt, dma=False):
        self.uid += 1
        t = self.pstack.enter_context(self.nc.sbuf_tensor(f"{name}_{self.uid}", shape, dt))
        b = Buf(t, name)
        if dma:
            b.dsem = self.dpool[self.dnext]
            self.dnext += 1
        return b

    def sbring(self, name, n, shape, dt, dma=False):
        return Ring([self.sb(f"{name}{i}", shape, dt, dma) for i in range(n)])

    def psring(self, name, n, shape=(128, 512), dt=F32):
        bufs = []
        for i in range(n):
            self.uid += 1
            t = self.pstack.enter_context(self.nc.psum_tensor(f"{name}{i}_{self.uid}", list(shape), dt))
            b = Buf(t, name)
            b.excl = True
            bufs.append(b)
        return Ring(bufs)

    def load(self, buf, dst, src, q="sp"):
        self.dma(q, dst, src, writes=[buf], sem=buf.dsem)

    def store(self, buf, dst, src, q="sp"):
        self.dma(q, dst, src, reads=[buf], sem=buf.dsem)

    def act(self, out, in_, func, reads, writes, bias=None, scale=None, eng="act"):
        kw = {}
        if bias is not None:
            kw["bias"] = bias
        if scale is not None:
            kw["scale"] = scale
        return self.op(eng, lambda e: e.activation(out=out, in_=in_, func=func, **kw), reads, writes)

    def tt(self, out, in0, in1, op, reads, writes, eng="dve"):
        return self.op(eng, lambda e: e.tensor_tensor(out=out, in0=in0, in1=in1, op=op), reads, writes)

    def ts(self, out, in0, s1, s2, op0, op1, reads, writes, eng="dve"):
        if s2 is None:
            return self.op(eng, lambda e: e.tensor_scalar(out=out, in0=in0, scalar1=s1, scalar2=None, op0=op0), reads, writes)
        return self.op(eng, lambda e: e.tensor_scalar(out=out, in0=in0, scalar1=s1, scalar2=s2, op0=op0, op1=op1), reads, writes)

    def stt(self, out, in0, scalar, in1, op0, op1, reads, writes, eng="dve"):
        return self.op(eng, lambda e: e.scalar_tensor_tensor(out=out, in0=in0, scalar=scalar, in1=in1, op0=op0, op1=op1), reads, writes)

    def copy(self, out, in_, reads, writes, eng="dve"):
        if eng == "act":
            return self.op(eng, lambda e: e.copy(out=out, in_=in_), reads, writes)
        return self.op(eng, lambda e: e.tensor_copy(out=out, in_=in_), reads, writes)


def ps_ap(b):
    return b.t[:, :]


class G:
    pass


def matmul(P, ps, out_ap, lhsT, rhs, start, stop, reads):
    P.op("pe", lambda e: e.matmul(out_ap, lhsT, rhs, start=start, stop=stop),
         reads=reads, writes=[ps], inc=stop)


def norm_res(P, Kc, TT=256):
    r = G()
    r.TT = TT
    r.hst = P.sbring("hst", 2, [128, Kc, TT], F32, dma=True)
    r.sq = P.sbring("sq", 1, [128, Kc, TT], BF16)
    r.rstd = P.sbring("rstd", 2, [128, TT], F32)
    return r


def norm_block_gen(P, g, r, src, tok0, TB, gain, hn, Kc, psring, out_f32=None):
    TT = r.TT
    nfeat = Kc * 128
    for i in range(TB // TT):
        c0 = tok0 + i * TT
        hst = r.hst.next()
        P.load(hst, hst.t[:], src[0:nfeat, c0:c0 + TT].rearrange("(c p) t -> p c t", p=128))
        sq = r.sq.next()
        P.act(sq.t[:], hst.t[:], AF.Square, [hst], [sq])
        yield
        ps = psring.next()
        for c in range(Kc):
            matmul(P, ps, ps.t[:, 0:TT], g.ones.t[:], sq.t[:, c, :], c == 0, c == Kc - 1, [g.ones, sq])
        rstd = r.rstd.next()
        P.act(rstd.t[:], ps.t[:, 0:TT], AF.Sqrt, [ps], [rstd], bias=EPS, scale=1.0 / nfeat)
        P.op("dve", lambda e: e.reciprocal(out=rstd.t[:], in_=rstd.t[:]), [rstd], [rstd])
        if out_f32 is None:
            for c in range(Kc):
                P.stt(hn.t[:, c, i * TT:(i + 1) * TT], hst.t[:, c, :], gain.t[:, c:c + 1], rstd.t[:],
                      ALU.mult, ALU.mult, [hst, gain, rstd], [hn])
        else:
            stg_ring, dst_fn = out_f32
            st = stg_ring.next()
            for c in range(Kc):
                P.stt(st.t[:, c, :], hst.t[:, c, :], gain.t[:, c:c + 1], rstd.t[:],
                      ALU.mult, ALU.mult, [hst, gain, rstd], [st])
            P.store(st, dst_fn(c0, TT), st.t[:], q="pool")
        yield


def norm_block(*a, **kw):
    for _ in norm_block_gen(*a, **kw):
        pass


def gemm_fm_gen(P, act, Kc, wsrc, chunks, nt, wring, psring, post, pre=None):
    n = len(chunks)
    loaded = {}

    def ld(i):
        wb = wring.next()
        P.dma("pool", wb.t[:, 0:Kc * 128], wsrc(chunks[i]), writes=[wb], sem=wb.dsem)
        loaded[i] = wb

    pf = len(wring.b) - 1
    for i in range(min(pf, n)):
        ld(i)
    for i in range(n):
        if i + pf < n:
            ld(i + pf)
        wb = loaded.pop(i)
        for ti in range(nt):
            if pre is not None:
                pre(chunks[i], ti)
            ps = psring.next()
            for kc in range(Kc):
                matmul(P, ps, ps.t[:, :], wb.t[:, kc * 128:(kc + 1) * 128], act.t[:, kc, ti * 512:(ti + 1) * 512],
                       kc == 0, kc == Kc - 1, [wb, act])
            post(chunks[i], ti, ps)
            yield


def gemm_fm(*a, **kw):
    for _ in gemm_fm_gen(*a, **kw):
        pass


def gemm_tm(P, act, Kc, wsrc, panels, ntt, wring, psring, post):
    n = len(panels)
    loaded = {}

    def ld(i):
        wb = wring.next()
        P.dma("pool", wb.t[:, 0:Kc * 512], wsrc(panels[i]), writes=[wb], sem=wb.dsem)
        loaded[i] = wb

    ld(0)
    for i in range(n):
        if i + 1 < n:
            ld(i + 1)
        wb = loaded.pop(i)
        for tt in range(ntt):
            ps = psring.next()
            for kc in range(Kc):
                matmul(P, ps, ps.t[:, :], act.t[:, kc, tt * 128:(tt + 1) * 128], wb.t[:, kc * 512:(kc + 1) * 512],
                       kc == 0, kc == Kc - 1, [wb, act])
            post(panels[i], tt, ps)


def load_small(P, name, shape, src, dt=F32, q="sp"):
    b = P.sb(name, shape, dt, dma=True)
    P.load(b, b.t[:], src, q=q)
    return b


def phase_tables(P, g):
    INV2PI = float(1.0 / (2 * np.pi))
    C1 = 6.28125
    C2 = float(2 * np.pi - 6.28125)
    PI = float(np.pi)
    with P.phase():
        posi = load_small(P, "posi", [128, T], g.pos, dt=I32)
        posf = P.sb("posf", [128, T], F32)
        P.copy(posf.t[:], posi.t[:], [posi], [posf])
        ang = P.sb("ang", [128, T], F32)
        kf = P.sb("kf", [128, T], F32)
        ki = P.sb("ki", [128, T], I32)
        out = P.sbring("tout", 2, [128, T], F32, dma=True)
        cs = g.consts
        for (fcol, dC, dS, scol) in ((1, g.tCm, g.tSm, 3), (2, g.tCr, g.tSr, None)):
            P.ts(ang.t[:], posf.t[:], cs.t[:, fcol:fcol + 1], None, ALU.mult, None, [posf, cs], [ang])
            P.ts(kf.t[:], ang.t[:], INV2PI, None, ALU.mult, None, [ang], [kf])
            P.copy(ki.t[:], kf.t[:], [kf], [ki])
            P.copy(kf.t[:], ki.t[:], [ki], [kf])
            P.stt(ang.t[:], kf.t[:], -C1, ang.t[:], ALU.mult, ALU.add, [kf, ang], [ang])
            P.stt(ang.t[:], kf.t[:], -C2, ang.t[:], ALU.mult, ALU.add, [kf, ang], [ang])
            P.ts(ang.t[:], ang.t[:], -PI, PI, ALU.max, ALU.min, [ang], [ang])
            o = out.next()
            P.act(o.t[:], ang.t[:], AF.Sin, [ang], [o])
            if scol is not None:
                P.ts(o.t[:], o.t[:], cs.t[:, scol:scol + 1], None, ALU.mult, None, [o, cs], [o])
            P.store(o, dS, o.t[:])
            P.stt(kf.t[:], ang.t[:], -1.0, ang.t[:], ALU.mult, ALU.max, [ang], [kf])
            o = out.next()
            P.act(o.t[:], kf.t[:], AF.Sin, [kf], [o], bias=float(np.pi / 2), scale=-1.0)
            P.store(o, dC, o.t[:])


def phase_E1(P, g, L, j, hsrc):
    with P.phase():
        gain = load_small(P, "gain", [128, 16], g.mix_g[L])
        lng = load_small(P, "lng", [128, 1024], g.sgu_lng[j])
        lnb = load_small(P, "lnb", [128, 1024], g.sgu_lnb[j])
        nr = norm_res(P, 16)
        hn = P.sb("hn", [128, 16, SEQ], BF16)
        psr = P.psring("ps", 8)
        wring = P.sbring("w", 3, [128, 2048], BF16, dma=True)
        wv = P.sbring("wv", 2, [128, 16 * 512], BF16, dma=True)
        stg = P.sbring("stg", 3, [128, 512], F32, dma=True)
        stgb = P.sbring("stgb", 3, [128, 512], BF16, dma=True)
        tab = P.sbring("tab", 2, [128, SEQ], F32, dma=True)
        xv = P.sbring("xv", 2, [128, 1024], F32)
        xo = P.sbring("xo", 2, [128, 1024], BF16, dma=True)
        st6 = P.sbring("st6", 2, [128, 12], F32)
        mv = P.sbring("mv", 2, [128, 2], F32)
        tmp = P.sbring("tmp", 2, [64, 512], F32)
        for s in range(NSEQ):
            tok0 = s * SEQ
            norm_block(P, g, nr, hsrc, tok0, SEQ, gain, hn, 16, psr)
            tC = tab.next()
            P.load(tC, tC.t[:], g.tCm[:, tok0:tok0 + SEQ])
            tS = tab.next()
            P.load(tS, tS.t[:], g.tSm[:, tok0:tok0 + SEQ])

            def post(ci, ti, ps, tok0=tok0):
                cols = slice(tok0 + ti * 512, tok0 + (ti + 1) * 512)
                if ci < 4:
                    st = stg.next()
                    P.copy(st.t[:], ps.t[:], [ps], [st], eng="act")
                    P.store(st, g.cqT[ci * 128:(ci + 1) * 128, cols], st.t[:])
                elif ci < 6:
                    st = stg.next()
                    P.copy(st.t[:], ps.t[:], [ps], [st], eng="act")
                    P.store(st, g.ckvT[(ci - 4) * 128:(ci - 3) * 128, cols], st.t[:])
                else:
                    st = stgb.next()
                    P.act(st.t[:], ps.t[:], AF.Gelu_apprx_tanh, [ps], [st])
                    P.store(st, g.abT[1024 + (ci - 7) * 128:1024 + (ci - 6) * 128, cols], st.t[:])

            fm = gemm_fm_gen(P, hn, 16, lambda ci: g.w_in_fm[j, ci], [0, 1, 2, 3, 4, 5, 7, 8, 9, 10, 11, 12, 13, 14], 4, wring, psr, post)
            w0 = wv.next()
            P.dma("pool", w0.t[:], g.w_in_tm[j, 0], writes=[w0], sem=w0.dsem)
            w1 = wv.next()
            P.dma("pool", w1.t[:], g.w_in_tm[j, 1], writes=[w1], sem=w1.dsem)
            for tt in range(SEQ // 128):
                for _ in range(4 if tt % 2 == 0 else 3):
                    next(fm, None)
                x = xv.next()
                for pi, wb2 in enumerate((w0, w1)):
                    ps = psr.next()
                    for kc in range(16):
                        matmul(P, ps, ps.t[:, :], hn.t[:, kc, tt * 128:(tt + 1) * 128], wb2.t[:, kc * 512:(kc + 1) * 512], kc == 0, kc == 15, [wb2, hn])
                    P.act(x.t[:, pi * 512:(pi + 1) * 512], ps.t[:], AF.Gelu_apprx_tanh, [ps], [x])
                s6 = st6.next()
                P.op("dve", lambda e: e.bn_stats(out=s6.t[:, 0:6], in_=x.t[:, 0:512]), [x], [s6])
                P.op("dve", lambda e: e.bn_stats(out=s6.t[:, 6:12], in_=x.t[:, 512:1024]), [x], [s6])
                m = mv.next()
                P.op("dve", lambda e: e.bn_aggr(out=m.t[:], in_=s6.t[:]), [s6], [m])
                P.act(m.t[:, 1:2], m.t[:, 1:2], AF.Sqrt, [m], [m], bias=EPS, scale=1.0)
                P.op("dve", lambda e: e.reciprocal(out=m.t[:, 1:2], in_=m.t[:, 1:2]), [m], [m])
                P.ts(x.t[:], x.t[:], m.t[:, 0:1], m.t[:, 1:2], ALU.subtract, ALU.mult, [x, m], [x])
                P.tt(x.t[:], x.t[:], lng.t[:], ALU.mult, [x, lng], [x])
                o = xo.next()
                P.tt(o.t[:], x.t[:], lnb.t[:], ALU.add, [x, lnb], [o])
                P.store(o, g.vnTM[tok0 + tt * 128:tok0 + (tt + 1) * 128, :], o.t[:])
            for _ in fm:
                pass
            wb = wring.next()
            P.dma("pool", wb.t[:], g.w_in_fm[j, 6], writes=[wb], sem=wb.dsem)
            for ti in range(4):
                cols = slice(tok0 + ti * 512, tok0 + (ti + 1) * 512)
                pa = psr.next()
                for kc in range(16):
                    matmul(P, pa, pa.t[0:64, :], wb.t[:, kc * 128:kc * 128 + 64], hn.t[:, kc, ti * 512:(ti + 1) * 512], kc == 0, kc == 15, [wb, hn])
                pb = psr.next()
                for kc in range(16):
                    matmul(P, pb, pb.t[0:64, :], wb.t[:, kc * 128 + 64:kc * 128 + 128], hn.t[:, kc, ti * 512:(ti + 1) * 512], kc == 0, kc == 15, [wb, hn])
                t1 = tmp.next()
                P.tt(t1.t[:], pa.t[0:64, :], tC.t[0:64, ti * 512:(ti + 1) * 512], ALU.mult, [pa, tC], [t1])
                t2 = tmp.next()
                P.tt(t2.t[:], pb.t[0:64, :], tS.t[0:64, ti * 512:(ti + 1) * 512], ALU.mult, [pb, tS], [t2])
                st = stgb.next()
                P.tt(st.t[0:64, :], t1.t[:], t2.t[:], ALU.add, [t1, t2], [st])
                P.store(st, g.kpeT[:, cols], st.t[0:64, :])


def phase_E2(P, g, j):
    with P.phase():
        gain = load_small(P, "gain", [128, 4], g.qn_g[j])
        nr = norm_res(P, 4)
        hn = P.sb("hn", [128, 4, SEQ], BF16)
        psr = P.psring("ps", 8)
        wring = P.sbring("w", 3, [128, 512], BF16, dma=True)
        stgb = P.sbring("stgb", 3, [128, 512], BF16, dma=True)
        tab = P.sbring("tab", 2, [128, SEQ], F32, dma=True)
        tmp = P.sbring("tmp", 3, [128, 512], F32)
        for s in range(NSEQ):
            tok0 = s * SEQ
            norm_block(P, g, nr, g.cqT, tok0, SEQ, gain, hn, 4, psr)
            tC = tab.next()
            P.load(tC, tC.t[:], g.tCm[:, tok0:tok0 + SEQ])
            tS = tab.next()
            P.load(tS, tS.t[:], g.tSm[:, tok0:tok0 + SEQ])

            def post(ci, ti, ps, tok0=tok0):
                cols = slice(tok0 + ti * 512, tok0 + (ti + 1) * 512)
                st = stgb.next()
                P.copy(st.t[:], ps.t[:], [ps], [st], eng="act")
                P.store(st, g.qnT[ci * 128:(ci + 1) * 128, cols], st.t[:])

            gemm_fm(P, hn, 4, lambda ci: g.w_q_fm[j, ci], list(range(8)), 4, wring, psr, post)
            for c in range(4):
                wa = wring.next()
                P.dma("pool", wa.t[:], g.w_q_fm[j, 8 + c], writes=[wa], sem=wa.dsem)
                wb = wring.next()
                P.dma("pool", wb.t[:], g.w_q_fm[j, 12 + c], writes=[wb], sem=wb.dsem)
                for ti in range(4):
                    cols = slice(tok0 + ti * 512, tok0 + (ti + 1) * 512)
                    tsl = slice(ti * 512, (ti + 1) * 512)
                    pa = psr.next()
                    for kc in range(4):
                        matmul(P, pa, pa.t[:, :], wa.t[:, kc * 128:(kc + 1) * 128], hn.t[:, kc, tsl], kc == 0, kc == 3, [wa, hn])
                    pb = psr.next()
                    for kc in range(4):
                        matmul(P, pb, pb.t[:, :], wb.t[:, kc * 128:(kc + 1) * 128], hn.t[:, kc, tsl], kc == 0, kc == 3, [wb, hn])
                    t1 = tmp.next()
                    P.tt(t1.t[:], pa.t[:], tC.t[:, tsl], ALU.mult, [pa, tC], [t1])
                    t2 = tmp.next()
                    P.tt(t2.t[:], pb.t[:], tS.t[:, tsl], ALU.mult, [pb, tS], [t2])
                    st = stgb.next()
                    P.tt(st.t[:], t1.t[:], t2.t[:], ALU.add, [t1, t2], [st])
                    P.store(st, g.qrT[c * 128:(c + 1) * 128, cols], st.t[:])


def phase_E3(P, g, j):
    with P.phase():
        gain = load_small(P, "gain", [128, 2], g.kvn_g[j])
        nr = norm_res(P, 2)
        hn = P.sb("hn", [128, 2, SEQ], BF16)
        psr = P.psring("ps", 8)
        wring = P.sbring("w", 3, [128, 256], BF16, dma=True)
        wv = P.sbring("wv", 2, [128, 2 * 512], BF16, dma=True)
        stgb = P.sbring("stgb", 4, [128, 512], BF16, dma=True)
        for s in range(NSEQ):
            tok0 = s * SEQ
            norm_block(P, g, nr, g.ckvT, tok0, SEQ, gain, hn, 2, psr)

            def post(ci, ti, ps, tok0=tok0):
                cols = slice(tok0 + ti * 512, tok0 + (ti + 1) * 512)
                st = stgb.next()
                P.copy(st.t[:], ps.t[:], [ps], [st], eng="act")
                P.store(st, g.knT[ci * 128:(ci + 1) * 128, cols], st.t[:])

            gemm_fm(P, hn, 2, lambda ci: g.w_kv_fm[j, ci], list(range(8)), 4, wring, psr, post)

            def postv(pi, tt, ps, tok0=tok0):
                st = stgb.next()
                P.copy(st.t[:], ps.t[:], [ps], [st], eng="act")
                P.store(st, g.vmTM[tok0 + tt * 128:tok0 + (tt + 1) * 128, pi * 512:(pi + 1) * 512], st.t[:])

            gemm_tm(P, hn, 2, lambda pi: g.w_kv_tm[j, pi], [0, 1], SEQ // 128, wv, psr, postv)


def phase_E4(P, g):
    scale = float((128 + 64) ** -0.5)
    LOOK = 3
    with P.phase():
        kpe = P.sb("kpe", [64, SEQ], BF16, dma=True)
        kn = P.sbring("kn", 2, [128, SEQ], BF16, dma=True)
        qn = P.sbring("qn", 2, [128, SEQ], BF16, dma=True)
        qr = P.sbring("qr", 2, [64, SEQ], BF16, dma=True)
        vv = P.sbring("vv", 2, [128, 16, 128], BF16, dma=True)
        pT = P.sbring("pT", 4, [128, 512], BF16)
        rec = P.sbring("rec", 2, [128, 512], F32)
        ost = P.sbring("ost", 2, [128, 512], BF16, dma=True)
        ps_s = P.psring("pss", 4)
        ps_o = P.psring("pso", 2)
        ps_d = P.psring("psd", 2)
        for s in range(NSEQ):
            tok0 = s * SEQ
            P.load(kpe, kpe.t[:], g.kpeT[:, tok0:tok0 + SEQ])
            heads = {}
            acc = {}

            def head(h, tok0=tok0):
                if h not in heads:
                    k_ = kn.next()
                    P.load(k_, k_.t[:], g.knT[h * 128:(h + 1) * 128, tok0:tok0 + SEQ])
                    q_ = qn.next()
                    P.load(q_, q_.t[:], g.qnT[h * 128:(h + 1) * 128, tok0:tok0 + SEQ])
                    r_ = qr.next()
                    P.load(r_, r_.t[:], g.qrT[h * 64:(h + 1) * 64, tok0:tok0 + SEQ])
                    v_ = vv.next()
                    P.load(v_, v_.t[:], g.vmTM[tok0:tok0 + SEQ, h * 128:(h + 1) * 128].rearrange("(b p) d -> p b d", p=128))
                    heads[h] = (k_, q_, r_, v_)
                return heads[h]

            def S(step):
                h, jq, kb = step
                k_, q_, r_, v_ = head(h)
                c0 = max(0, kb - 4 * jq) * 128
                qs = slice(jq * 512 + c0, (jq + 1) * 512)
                ks = slice(kb * 128, (kb + 1) * 128)
                pss = ps_s.next()
                matmul(P, pss, pss.t[:, c0:512], k_.t[:, ks], q_.t[:, qs], True, False, [k_, q_])
                matmul(P, pss, pss.t[:, c0:512], kpe.t[0:64, ks], r_.t[0:64, qs], False, True, [kpe, r_])
                return pss, c0

            def rest(step, pss, c0, tok0=tok0):
                h, jq, kb = step
                k_, q_, r_, v_ = heads[h]
                nkb = 4 * jq + 4
                if kb == 0:
                    acc[(h, jq)] = (ps_o.next(), ps_d.next())
                    if jq == 2 and h + 1 < 8:
                        head(h + 1)
                po, pd = acc[(h, jq)]
                p_ = pT.next()
                P.act(p_.t[:, c0:512], pss.t[:, c0:512], AF.Exp, [pss], [p_], scale=scale)
                if kb >= 4 * jq:
                    P.op("dve", lambda e: e.memset(p_.t[64:128, c0:c0 + 64], 0.0), [], [p_])
                last = kb == nkb - 1
                P.op("pe", lambda e: e.matmul(po.t[:, c0:512], v_.t[:, kb, :], p_.t[:, c0:512], start=(kb == 0), stop=last),
                     reads=[v_, p_], writes=[po], inc=last)
                P.op("pe", lambda e: e.matmul(pd.t[:, c0:512], g.ones.t[:], p_.t[:, c0:512], start=(kb == 0), stop=last),
                     reads=[g.ones, p_], writes=[pd], inc=True)
                if last:
                    rc = rec.next()
                    P.op("dve", lambda e: e.reciprocal(out=rc.t[:], in_=pd.t[:]), [pd], [rc])
                    o = ost.next()
                    P.tt(o.t[:], po.t[:], rc.t[:], ALU.mult, [po, rc], [o])
                    P.store(o, g.abT[h * 128:(h + 1) * 128, tok0 + jq * 512:tok0 + (jq + 1) * 512], o.t[:])
                    del acc[(h, jq)]

            steps = [(h, jq, kb) for h in range(8) for jq in range(4) for kb in range(4 * jq + 4)]
            pendq = []
            for i in range(min(LOOK, len(steps))):
                pendq.append(S(steps[i]))
            for i, st in enumerate(steps):
                if i + LOOK < len(steps):
                    pendq.append(S(steps[i + LOOK]))
                pss, c0 = pendq.pop(0)
                rest(st, pss, c0)


def phase_E5(P, g, j):
    with P.phase():
        wsf = load_small(P, "wsf", [128, 1024], g.sgu_wsT[j])
        ws = P.sb("ws", [128, 1024], BF16)
        P.copy(ws.t[:], wsf.t[:], [wsf], [ws])
        for gi in range(8):
            P.op("dve", lambda e, gi=gi: e.memset(ws.t[64:128, gi * 128:gi * 128 + 64], 0.0), [], [ws])
        bs4 = load_small(P, "bs4", [128, 8 * 512], g.sgu_bs4[j])
        vn = P.sbring("vn", 2, [128, 4, 1024], BF16, dma=True)
        ug = P.sbring("ug", 16, [128, 512], BF16, dma=True)
        tmp = P.sbring("tmp", 3, [128, 512], F32)
        ost = P.sbring("ost", 4, [128, 512], BF16, dma=True)
        psr = P.psring("ps", 8)
        for ti in range(T // 512):
            t0 = ti * 512
            v_ = vn.next()
            P.load(v_, v_.t[:], g.vnTM[t0:t0 + 512, :].rearrange("(b p) c -> p b c", p=128))
            pss_, us_ = [], []
            for gi in range(8):
                u_ = ug.next()
                P.load(u_, u_.t[:], g.abT[1024 + gi * 128:1024 + (gi + 1) * 128, t0:t0 + 512])
                us_.append(u_)
                ps = psr.next()
                for b in range(4):
                    P.op("pe", lambda e, ps=ps, v_=v_, b=b, gi=gi: e.matmul(
                        ps.t[:, b * 128:(b + 1) * 128], v_.t[:, b, gi * 128:(gi + 1) * 128], ws.t[:, gi * 128:(gi + 1) * 128],
                        start=True, stop=True), reads=[v_, ws], writes=[ps], inc=(b == 3))
                pss_.append(ps)
            for gi in range(8):
                ps = pss_[gi]
                u_ = us_[gi]
                t_ = tmp.next()
                P.tt(t_.t[:], ps.t[:], bs4.t[:, gi * 512:(gi + 1) * 512], ALU.add, [ps, bs4], [t_])
                o = ost.next()
                P.tt(o.t[:], t_.t[:], u_.t[:], ALU.mult, [t_, u_], [o])
                P.store(o, g.abT[1024 + gi * 128:1024 + (gi + 1) * 128, t0:t0 + 512], o.t[:])


def phase_proj_res(P, g, src, Kc, wfm, hsrc, hdst, TB):
    big = Kc >= 44
    with P.phase():
        acts = P.sbring("act", 2, [128, Kc, TB], BF16, dma=True)
        psr = P.psring("ps", 8)
        wring = P.sbring("w", 2 if big else 3, [128, Kc * 128], BF16, dma=True)
        hres = P.sbring("hres", 2 if big else 4, [128, 512], F32, dma=True)
        pend = {}
        nblk = T // TB
        grp = max(1, (2 << 20) // (128 * TB * 2))
        qsel = [0]

        def load_act(blk):
            a = acts.next()
            tok0 = blk * TB
            for c0 in range(0, Kc, grp):
                c1 = min(Kc, c0 + grp)
                q = "sp"
                qsel[0] += 1
                P.dma(q, a.t[:, c0:c1, :], src[c0 * 128:c1 * 128, tok0:tok0 + TB].rearrange("(c p) t -> p c t", p=128),
                      writes=[a], sem=a.dsem)
            return a

        nxt = load_act(0)
        for blk in range(nblk):
            tok0 = blk * TB
            act = nxt
            if blk + 1 < nblk:
                nxt = load_act(blk + 1)

            def pre(ci, ti, tok0=tok0):
                hr = hres.next()
                P.load(hr, hr.t[:], hsrc[ci * 128:(ci + 1) * 128, tok0 + ti * 512:tok0 + (ti + 1) * 512])
                pend[(ci, ti)] = hr

            def post(ci, ti, ps, tok0=tok0):
                hr = pend.pop((ci, ti))
                P.tt(hr.t[:], ps.t[:], hr.t[:], ALU.add, [ps, hr], [hr])
                P.store(hr, hdst[ci * 128:(ci + 1) * 128, tok0 + ti * 512:tok0 + (ti + 1) * 512], hr.t[:], q="act")

            gemm_fm(P, act, Kc, lambda ci: wfm[ci], list(range(16)), TB // 512, wring, psr, post, pre)


def phase_F1(P, g, L):
    with P.phase():
        gain = load_small(P, "gain", [128, 16], g.ffn_g[L])
        cw = load_small(P, "cw", [128, 88 * 3], g.conv_w[L])
        cb = load_small(P, "cb", [128, 88], g.conv_b[L])
        nr = norm_res(P, 16, TT=128)
        hns = [P.sb("hn0", [128, 16, SEQ], BF16), P.sb("hn1", [128, 16, SEQ], BF16)]
        psr = P.psring("ps", 8)
        wring = P.sbring("w", 4, [128, 2048], BF16, dma=True)
        asb = P.sbring("asb", 4, [128, 514], F32)
        cc = P.sbring("cc", 4, [128, 512], F32)
        gl = P.sbring("gl", 2, [128, 512], F32)
        ptmp = P.sbring("ptmp", 2, [128, 512], F32)
        ost = P.sbring("ost", 3, [128, 512], BF16, dma=True)
        norm_block(P, g, nr, g.hT, 0, SEQ, gain, hns[0], 16, psr)
        for s in range(NSEQ):
            tok0 = s * SEQ
            hn = hns[s % 2]
            nxt = norm_block_gen(P, g, nr, g.hT, tok0 + SEQ, SEQ, gain, hns[(s + 1) % 2], 16, psr) if s + 1 < NSEQ else iter(())
            loaded = {}
            it = 0

            def ld(i):
                for half in (0, 1):
                    wb = wring.next()
                    P.dma("pool", wb.t[:], g.ffn_up_fm[L, i + 44 * half], writes=[wb], sem=wb.dsem)
                    loaded[(i, half)] = wb

            ld(0)
            for i in range(44):
                if i + 1 < 44:
                    ld(i + 1)
                wbs = (loaded.pop((i, 0)), loaded.pop((i, 1)))
                prev = [None, None]
                for ti in range(4):
                    it += 1
                    if it % 4 == 2:
                        next(nxt, None)
                    cres = []
                    for half in (0, 1):
                        ci = i + 44 * half
                        wb = wbs[half]
                        ps = psr.next()
                        for kc in range(16):
                            matmul(P, ps, ps.t[:, :], wb.t[:, kc * 128:(kc + 1) * 128], hn.t[:, kc, ti * 512:(ti + 1) * 512],
                                   kc == 0, kc == 15, [wb, hn])
                        a = asb.next()
                        ve = "dve"
                        P.copy(a.t[:, 2:514], ps.t[:], [ps], [a], eng="act")
                        if ti == 0:
                            P.op(ve, lambda e, a=a: e.memset(a.t[:, 0:2], 0.0), [], [a])
                        else:
                            P.copy(a.t[:, 0:2], prev[half].t[:, 512:514], [prev[half]], [a], eng=ve)
                        prev[half] = a
                        c = cc.next()
                        P.act(c.t[:], ps.t[:], AF.Identity, [ps, cw, cb], [c],
                              bias=cb.t[:, ci:ci + 1], scale=cw.t[:, ci * 3 + 2:ci * 3 + 3])
                        P.stt(c.t[:], a.t[:, 1:513], cw.t[:, ci * 3 + 1:ci * 3 + 2], c.t[:], ALU.mult, ALU.add, [a, cw, c], [c])
                        P.stt(c.t[:], a.t[:, 0:512], cw.t[:, ci * 3:ci * 3 + 1], c.t[:], ALU.mult, ALU.add, [a, cw, c], [c])
                        cres.append(c)
                    gg = gl.next()
                    P.act(gg.t[:], cres[0].t[:], AF.Gelu_apprx_tanh, [cres[0]], [gg])
                    o = ost.next()
                    P.tt(o.t[:], gg.t[:], cres[1].t[:], ALU.mult, [gg, cres[1]], [o], eng="pool")
                    P.store(o, g.ffT[i * 128:(i + 1) * 128, tok0 + ti * 512:tok0 + (ti + 1) * 512], o.t[:])


def phase_PLE(P, g, L):
    with P.phase():
        gain = load_small(P, "gain", [128, 16], g.ple_g[L])
        nr = norm_res(P, 16, TT=128)
        hns = [P.sb("hn0", [128, 16, SEQ], BF16), P.sb("hn1", [128, 16, SEQ], BF16)]
        pb = P.sb("pb", [128, 2, SEQ], BF16, dma=True)
        psr = P.psring("ps", 8)
        wring = P.sbring("w", 3, [128, 2048], BF16, dma=True)
        wup = P.sbring("wup", 3, [128, 256], BF16, dma=True)
        hres = P.sbring("hres", 6, [128, 512], F32, dma=True)
        sg = P.sbring("sg", 5, [128, 512], F32)
        norm_block(P, g, nr, g.hT, 0, SEQ, gain, hns[0], 16, psr)
        for s in range(NSEQ):
            tok0 = s * SEQ
            hn = hns[s % 2]
            nxt = norm_block_gen(P, g, nr, g.hT, tok0 + SEQ, SEQ, gain, hns[(s + 1) % 2], 16, psr) if s + 1 < NSEQ else iter(())
            it = 0
            P.dma("pool", pb.t[:], g.pT[L, :, tok0:tok0 + SEQ].rearrange("(c p) t -> p c t", p=128), writes=[pb], sem=pb.dsem)
            loaded = {}

            def ld(i):
                wb = wring.next()
                P.dma("pool", wb.t[:], g.ple_gate_fm[L, i], writes=[wb], sem=wb.dsem)
                wu = wup.next()
                P.dma("pool", wu.t[:], g.ple_up_fm[L, i], writes=[wu], sem=wu.dsem)
                loaded[i] = (wb, wu)

            ld(0)
            ld(1)
            for i in range(16):
                if i + 2 < 16:
                    ld(i + 2)
                wb, wu = loaded.pop(i)
                for ti in range(4):
                    it += 1
                    if it % 2 == 1:
                        next(nxt, None)
                    tsl = slice(ti * 512, (ti + 1) * 512)
                    cols = slice(tok0 + ti * 512, tok0 + (ti + 1) * 512)
                    hr = hres.next()
                    P.load(hr, hr.t[:], g.hT[i * 128:(i + 1) * 128, cols], q="act")
                    ps = psr.next()
                    for kc in range(16):
                        matmul(P, ps, ps.t[:, :], wb.t[:, kc * 128:(kc + 1) * 128], hn.t[:, kc, tsl], kc == 0, kc == 15, [wb, hn])
                    ps2 = psr.next()
                    for kc in range(2):
                        matmul(P, ps2, ps2.t[:, :], wu.t[:, kc * 128:(kc + 1) * 128], pb.t[:, kc, tsl], kc == 0, kc == 1, [wu, pb])
                    s_ = sg.next()
                    P.act(s_.t[:], ps.t[:], AF.Sigmoid, [ps], [s_])
                    P.tt(s_.t[:], ps2.t[:], s_.t[:], ALU.mult, [ps2, s_], [s_])
                    P.tt(hr.t[:], s_.t[:], hr.t[:], ALU.add, [s_, hr], [hr])
                    P.store(hr, g.hT[i * 128:(i + 1) * 128, cols], hr.t[:])


def phase_O1(P, g, L, j):
    with P.phase():
        gain = load_small(P, "gain", [128, 16], g.mix_g[L])
        gng = load_small(P, "gng", [128, 32], g.gn_g[j])
        gnb = load_small(P, "gnb", [128, 32], g.gn_b[j])
        sgf = P.sbring("sgf", 3, [128, 512], F32)
        qd4 = load_small(P, "qd4", [128, 8 * 512], g.qd4)
        nr = norm_res(P, 16)
        hn = P.sb("hn", [128, 16, SEQ], BF16)
        psr = P.psring("ps", 8)
        wring = P.sbring("w", 4, [128, 2048], BF16, dma=True)
        wv = P.sbring("wv", 2, [128, 16 * 512], BF16, dma=True)
        stgb = P.sbring("stgb", 6, [128, 512], BF16, dma=True)
        tC = P.sb("tC", [128, SEQ], F32, dma=True)
        tS = P.sb("tS", [128, SEQ], F32, dma=True)
        tmp = P.sbring("tmp", 4, [128, 512], F32)
        for s in range(NSEQ):
            tok0 = s * SEQ
            norm_block(P, g, nr, g.hT, tok0, SEQ, gain, hn, 16, psr)
            P.load(tC, tC.t[:], g.tCr[:, tok0:tok0 + SEQ])
            P.load(tS, tS.t[:], g.tSr[:, tok0:tok0 + SEQ])
            for which in (0, 1):
                sc = 1.0 if which == 0 else 1.0 / 16.0
                dst = g.rqT if which == 0 else g.rkT
                for h in range(8):
                    c1 = which * 16 + 2 * h
                    w1 = wring.next()
                    P.dma("pool", w1.t[:], g.ret_in_fm[j, c1], writes=[w1], sem=w1.dsem)
                    w2 = wring.next()
                    P.dma("pool", w2.t[:], g.ret_in_fm[j, c1 + 1], writes=[w2], sem=w2.dsem)
                    for ti in range(4):
                        tsl = slice(ti * 512, (ti + 1) * 512)
                        cols = slice(tok0 + ti * 512, tok0 + (ti + 1) * 512)
                        p1 = psr.next()
                        for kc in range(16):
                            matmul(P, p1, p1.t[:, :], w1.t[:, kc * 128:(kc + 1) * 128], hn.t[:, kc, tsl], kc == 0, kc == 15, [w1, hn])
                        p2 = psr.next()
                        for kc in range(16):
                            matmul(P, p2, p2.t[:, :], w2.t[:, kc * 128:(kc + 1) * 128], hn.t[:, kc, tsl], kc == 0, kc == 15, [w2, hn])
                        t1 = tmp.next()
                        P.stt(t1.t[:], p1.t[:], sc, tC.t[:, tsl], ALU.mult, ALU.mult, [p1, tC], [t1])
                        t2 = tmp.next()
                        P.stt(t2.t[:], p2.t[:], sc, tS.t[:, tsl], ALU.mult, ALU.mult, [p2, tS], [t2])
                        t3 = tmp.next()
                        P.stt(t3.t[:], p2.t[:], sc, tC.t[:, tsl], ALU.mult, ALU.mult, [p2, tC], [t3])
                        t4 = tmp.next()
                        P.stt(t4.t[:], p1.t[:], sc, tS.t[:, tsl], ALU.mult, ALU.mult, [p1, tS], [t4])
                        if which == 1:
                            o1 = stgb.next()
                            P.tt(o1.t[:], t1.t[:], t2.t[:], ALU.subtract, [t1, t2], [o1])
                            P.store(o1, dst[(2 * h) * 128:(2 * h + 1) * 128, cols], o1.t[:])
                            o2 = stgb.next()
                            P.tt(o2.t[:], t3.t[:], t4.t[:], ALU.add, [t3, t4], [o2])
                            P.store(o2, dst[(2 * h + 1) * 128:(2 * h + 2) * 128, cols], o2.t[:])
                        else:
                            P.tt(t1.t[:], t1.t[:], t2.t[:], ALU.subtract, [t1, t2], [t1])
                            P.tt(t3.t[:], t3.t[:], t4.t[:], ALU.add, [t3, t4], [t3])
                            for (tx, row) in ((t1, 2 * h), (t3, 2 * h + 1)):
                                o1 = stgb.next()
                                P.copy(o1.t[:], tx.t[:], [tx], [o1], eng="act")
                                P.store(o1, g.rqT[row * 128:(row + 1) * 128, cols], o1.t[:])
                                o2 = stgb.next()
                                P.tt(o2.t[:], tx.t[:], qd4.t[:, h * 512:(h + 1) * 512], ALU.mult, [tx, qd4], [o2])
                                P.store(o2, g.rqsT[row * 128:(row + 1) * 128, cols], o2.t[:])

            def postg(ci, ti, ps, tok0=tok0):
                c = ci - 32
                sg_ = sgf.next()
                P.act(sg_.t[:], ps.t[:], AF.Silu, [ps], [sg_])
                st = stgb.next()
                P.ts(st.t[:], sg_.t[:], gng.t[:, c:c + 1], None, ALU.mult, None, [sg_, gng], [st])
                P.store(st, g.rgT[c * 128:(c + 1) * 128, tok0 + ti * 512:tok0 + (ti + 1) * 512], st.t[:])
                st2 = stgb.next()
                P.ts(st2.t[:], sg_.t[:], gnb.t[:, c:c + 1], None, ALU.mult, None, [sg_, gnb], [st2])
                P.store(st2, g.rg2T[c * 128:(c + 1) * 128, tok0 + ti * 512:tok0 + (ti + 1) * 512], st2.t[:])

            gemm_fm(P, hn, 16, lambda ci: g.ret_in_fm[j, ci], list(range(32, 64)), 4, wring, psr, postg)

            def postv(pi, tt, ps, tok0=tok0):
                st = stgb.next()
                P.copy(st.t[:], ps.t[:], [ps], [st], eng="act")
                P.store(st, g.rvTM[tok0 + tt * 128:tok0 + (tt + 1) * 128, pi * 512:(pi + 1) * 512], st.t[:])

            gemm_tm(P, hn, 16, lambda pi: g.ret_in_tm[j, pi], list(range(8)), SEQ // 128, wv, psr, postv)


def phase_O2(P, g, j):
    NH = 2
    with P.phase():
        DT = load_small(P, "DT", [128, 8 * 128], g.DT)
        gng = load_small(P, "gng", [128, 32], g.gn_g[j])
        gnb = load_small(P, "gnb", [128, 32], g.gn_b[j])
        cs = g.consts
        kt = P.sbring("kt", 2 * NH, [128, 2, 512], BF16, dma=True)
        qt = P.sbring("qt", 2 * NH, [128, 2, 512], BF16, dma=True)
        qst = P.sbring("qst", 2 * NH, [128, 2, 512], BF16, dma=True)
        vt = P.sbring("vt", 2 * NH, [128, 4, 512], BF16, dma=True)
        gt = P.sbring("gt", 2 * NH, [128, 4, 512], BF16, dma=True)
        gt2 = P.sbring("gt2", 2 * NH, [128, 4, 512], BF16, dma=True)
        rst = P.sbring("rst", 2 * NH, [128, 4, 512], BF16, dma=True)
        AT = P.sbring("AT", 4, [128, 128], BF16)
        kz = P.sbring("kz", 4, [128, 256], BF16)
        states = [P.sb(f"state{i}", [128, 2, 512], F32) for i in range(NH)]
        stbfs = [P.sb(f"stbf{i}", [128, 2, 512], BF16) for i in range(NH)]
        st6 = P.sbring("st6", 4, [128, 6], F32)
        mv = P.sbring("mv", 4, [128, 2], F32)
        nmr = P.sbring("nmr", 4, [128, 1], F32)
        xh = P.sbring("xh", 4, [128, 512], BF16)
        r1 = P.sbring("r1", 4, [128, 4, 128], F32)
        psS = P.psring("pss", NH).b
        psO = P.psring("pso", NH).b
        psU = P.psring("psu", NH).b
        psB = P.psring("psb", NH, (128, 1024), BF16).b

        def step(h, hi, b, first, lastblk, bufs):
            k_, q_, qs_, v_, g_, ro, g2_ = bufs
            state = states[hi]
            stbf = stbfs[hi]
            pss, po, pu, pb = psS[hi], psO[hi], psU[hi], psB[hi]
            cd128 = float(g.cd128[h])
            bs = slice(b * 128, (b + 1) * 128)
            for dc in range(2):
                matmul(P, pss, pss.t[:, 0:128], k_.t[:, dc, bs], q_.t[:, dc, bs], dc == 0, dc == 1, [k_, q_])
            yield
            a_ = AT.next()
            P.tt(a_.t[:], pss.t[:, 0:128], DT.t[:, h * 128:(h + 1) * 128], ALU.mult, [pss, DT], [a_])
            yield
            P.op("pe", lambda e: e.matmul(po.t[:, :], a_.t[:], v_.t[:, b, :], start=True, stop=first), [a_, v_], [po], inc=first)
            if not first:
                for dc in range(2):
                    P.op("pe", lambda e: e.matmul(po.t[:, :], qs_.t[:, dc, bs], stbf.t[:, dc, :], start=False, stop=(dc == 1)),
                         [qs_, stbf], [po], inc=(dc == 1))
            if not lastblk:
                for dc in range(2):
                    P.op("pe", lambda e: e.transpose(pb.t[:, dc * 128:(dc + 1) * 128], k_.t[:, dc, bs], g.ident.t[:]),
                         [k_, g.ident], [pb], inc=(dc == 1))
            yield
            if not lastblk:
                z_ = kz.next()
                P.act(z_.t[:], pb.t[:, 0:256], AF.Identity, [pb, cs], [z_], scale=cs.t[:, 5 + h:6 + h])
            s6 = st6.next()
            P.op("dve", lambda e: e.bn_stats(out=s6.t[:], in_=po.t[:]), [po], [s6])
            m = mv.next()
            P.op("dve", lambda e: e.bn_aggr(out=m.t[:], in_=s6.t[:]), [s6], [m])
            yield
            if not lastblk:
                P.op("pe", lambda e: e.matmul(pu.t[:, :], z_.t[:, 0:128], v_.t[:, b, :], start=True, stop=True), [z_, v_], [pu], inc=True)
            P.act(m.t[:, 1:2], m.t[:, 1:2], AF.Sqrt, [m], [m], bias=EPS, scale=1.0)
            yield
            if not lastblk:
                if first:
                    P.copy(state.t[:, 0, :], pu.t[:], [pu], [state], eng="dve")
                else:
                    P.stt(state.t[:, 0, :], state.t[:, 0, :], cd128, pu.t[:], ALU.mult, ALU.add, [state, pu], [state])
            P.op("dve", lambda e: e.reciprocal(out=m.t[:, 1:2], in_=m.t[:, 1:2]), [m], [m])
            nm = nmr.next()
            P.ts(nm.t[:], m.t[:, 0:1], -1.0, m.t[:, 1:2], ALU.mult, ALU.mult, [m], [nm])
            x_ = xh.next()
            P.act(x_.t[:], po.t[:], AF.Identity, [po, m, nm], [x_], bias=nm.t[:, 0:1], scale=m.t[:, 1:2])
            yield
            if not lastblk:
                P.op("pe", lambda e: e.matmul(pu.t[:, :], z_.t[:, 128:256], v_.t[:, b, :], start=True, stop=True), [z_, v_], [pu], inc=True)
            for ec in range(4):
                P.op("pe", lambda e: e.transpose(pb.t[:, 256 + ec * 128:256 + (ec + 1) * 128], x_.t[:, ec * 128:(ec + 1) * 128], g.ident.t[:]),
                     [x_, g.ident], [pb], inc=(ec == 3))
            yield
            if not lastblk:
                if first:
                    P.copy(state.t[:, 1, :], pu.t[:], [pu], [state], eng="dve")
                else:
                    P.stt(state.t[:, 1, :], state.t[:, 1, :], cd128, pu.t[:], ALU.mult, ALU.add, [state, pu], [state])
            r_ = r1.next()
            P.tt(r_.t[:], pb.t[:, 256:768].rearrange("p (c t) -> p c t", c=4), g_.t[:, :, bs], ALU.mult, [pb, g_], [r_])
            yield
            if not lastblk:
                P.copy(stbf.t[:], state.t[:], [state], [stbf], eng="act")
            P.tt(ro.t[:, :, bs], r_.t[:], g2_.t[:, :, bs], ALU.add, [r_, g2_], [ro], eng="pool")

        def load_tile(s, hg, ti):
                    t0 = s * SEQ + ti * 512
                    bufs = {}
                    for hi in range(NH):
                        h = hg * NH + hi
                        rows2 = slice(h * 256, (h + 1) * 256)
                        k_ = kt.next()
                        P.load(k_, k_.t[:], g.rkT[rows2, t0:t0 + 512].rearrange("(c p) t -> p c t", p=128))
                        q_ = qt.next()
                        P.load(q_, q_.t[:], g.rqT[rows2, t0:t0 + 512].rearrange("(c p) t -> p c t", p=128))
                        qs_ = qst.next()
                        P.load(qs_, qs_.t[:], g.rqsT[rows2, t0:t0 + 512].rearrange("(c p) t -> p c t", p=128))
                        v_ = vt.next()
                        P.load(v_, v_.t[:], g.rvTM[t0:t0 + 512, h * 512:(h + 1) * 512].rearrange("(b p) e -> p b e", p=128))
                        g_ = gt.next()
                        P.load(g_, g_.t[:], g.rgT[h * 512:(h + 1) * 512, t0:t0 + 512].rearrange("(c p) t -> p c t", p=128))
                        g2_ = gt2.next()
                        P.load(g2_, g2_.t[:], g.rg2T[h * 512:(h + 1) * 512, t0:t0 + 512].rearrange("(c p) t -> p c t", p=128))
                        bufs[hi] = (k_, q_, qs_, v_, g_, rst.next(), g2_)
                    return bufs

        items = [(s, hg, ti) for s in range(NSEQ) for hg in range(8 // NH) for ti in range(4)]
        nxt_bufs = load_tile(*items[0])
        for idx, (s, hg, ti) in enumerate(items):
                    t0 = s * SEQ + ti * 512
                    bufs = nxt_bufs
                    if idx + 1 < len(items):
                        nxt_bufs = load_tile(*items[idx + 1])
                    for b in range(4):
                        gens = [step(hg * NH + hi, hi, b, ti == 0 and b == 0, ti == 3 and b == 3, bufs[hi]) for hi in range(NH)]
                        while gens:
                            for gen in list(gens):
                                try:
                                    next(gen)
                                except StopIteration:
                                    gens.remove(gen)
                    for hi in range(NH):
                        h = hg * NH + hi
                        ro = bufs[hi][5]
                        P.store(ro, g.rrT[h * 512:(h + 1) * 512, t0:t0 + 512].rearrange("(c p) t -> p c t", p=128), ro.t[:], q="act")


def phase_final(P, g):
    with P.phase():
        gain = load_small(P, "gain", [128, 16], g.fin_g)
        nr = norm_res(P, 16, TT=512)
        psr = P.psring("ps", 4)
        stg = P.sbring("fst", 2, [128, 16, nr.TT], F32, dma=True)
        norm_block(P, g, nr, g.hT, 0, T, gain, None, 16, psr,
                   out_f32=(stg, lambda c0, TT: g.yT[:, c0:c0 + TT].rearrange("(c p) t -> p c t", p=128)))


def build_program(cst, nlayers=DEPTH, dbg=None, stop=None):
    nc = bass.Bass("TRN2", target_bir_lowering=False)
    g = G()
    g.cd128 = cst["cd128"]

    def ext(name, shape, dt=F32):
        return nc.dram_tensor(name, list(shape), dt, kind="ExternalInput").ap()

    def scr(name, shape, dt):
        kind = "ExternalOutput" if (dbg is not None and name in dbg.split(",")) else "Internal"
        return nc.dram_tensor(name, list(shape), dt, kind=kind).ap()

    g.xT = ext("xT", [D, T])
    g.pT = ext("pT", [DEPTH, 256, T])
    g.pos = ext("pos", [128, T], I32)
    cin = ext("consts", [128, 16])
    identf = ext("ident", [128, 128])
    g.DT = ext("DT", [128, 1024])
    g.qd4 = ext("qd4", [128, 4096])
    g.mix_g = ext("mix_g", [DEPTH, 128, 16])
    g.ffn_g = ext("ffn_g", [DEPTH, 128, 16])
    g.ple_g = ext("ple_g", [DEPTH, 128, 16])
    g.fin_g = ext("fin_g", [128, 16])
    g.qn_g = ext("qn_g", [2, 128, 4])
    g.kvn_g = ext("kvn_g", [2, 128, 2])
    g.sgu_lng = ext("sgu_lng", [2, 128, 1024])
    g.sgu_lnb = ext("sgu_lnb", [2, 128, 1024])
    g.sgu_bs4 = ext("sgu_bs4", [2, 128, 4096])
    g.sgu_wsT = ext("sgu_wsT", [2, 128, 1024])
    g.gn_g = ext("gn_g", [2, 128, 32])
    g.gn_b = ext("gn_b", [2, 128, 32])
    g.conv_w = ext("conv_w", [DEPTH, 128, 264])
    g.conv_b = ext("conv_b", [DEPTH, 128, 88])
    g.w_in_fm = ext("w_in_fm", [2, 15, 128, 2048])
    g.w_in_tm = ext("w_in_tm", [2, 2, 128, 16 * 512])
    g.w_q_fm = ext("w_q_fm", [2, 16, 128, 512])
    g.w_kv_fm = ext("w_kv_fm", [2, 8, 128, 256])
    g.w_kv_tm = ext("w_kv_tm", [2, 2, 128, 2 * 512])
    g.w_out_fm = ext("w_out_fm", [2, 16, 128, 2048])
    g.ret_in_fm = ext("ret_in_fm", [2, 64, 128, 2048])
    g.ret_in_tm = ext("ret_in_tm", [2, 8, 128, 16 * 512])
    g.ret_out_fm = ext("ret_out_fm", [2, 16, 128, 4096])
    g.ffn_up_fm = ext("ffn_up_fm", [DEPTH, 88, 128, 2048])
    g.ffn_dn_fm = ext("ffn_dn_fm", [DEPTH, 16, 128, 5632])
    g.ple_gate_fm = ext("ple_gate_fm", [DEPTH, 16, 128, 2048])
    g.ple_up_fm = ext("ple_up_fm", [DEPTH, 16, 128, 256])

    if dbg is not None and "hT" in dbg.split(","):
        g.hT = nc.dram_tensor("hT", [D, T], F32, kind="ExternalOutput").ap()
        g.yT = None
    else:
        g.hT = nc.dram_tensor("hT", [D, T], F32, kind="Internal").ap()
        g.yT = nc.dram_tensor("yT", [D, T], F32, kind="ExternalOutput").ap() if dbg is None else None
    g.cqT = scr("cqT", [512, T], F32)
    g.ckvT = scr("ckvT", [256, T], F32)
    g.kpeT = scr("kpeT", [64, T], BF16)
    g.vnTM = scr("vnTM", [T, 1024], BF16)
    g.qnT = scr("qnT", [1024, T], BF16)
    g.qrT = scr("qrT", [512, T], BF16)
    g.knT = scr("knT", [1024, T], BF16)
    g.vmTM = scr("vmTM", [T, 1024], BF16)
    g.abT = scr("abT", [2048, T], BF16)
    g.rqT = scr("rqT", [2048, T], BF16)
    g.rqsT = scr("rqsT", [2048, T], BF16)
    g.rkT = scr("rkT", [2048, T], BF16)
    g.rvTM = scr("rvTM", [T, 4096], BF16)
    g.rgT = scr("rgT", [4096, T], BF16)
    g.rg2T = scr("rg2T", [4096, T], BF16)
    g.rrT = scr("rrT", [4096, T], BF16)
    g.ffT = scr("ffT", [DFF, T], BF16)
    g.tCm = scr("tCm", [128, T], F32)
    g.tSm = scr("tSm", [128, T], F32)
    g.tCr = scr("tCr", [128, T], F32)
    g.tSr = scr("tSr", [128, T], F32)

    with contextlib.ExitStack() as es:
        P = Prog(nc, es)
        P.pstack = es
        g.consts = P.sb("consts", [128, 16], F32, dma=True)
        g.ones = P.sb("ones", [128, 128], BF16)
        g.ident = P.sb("ident", [128, 128], BF16, dma=True)
        P.load(g.consts, g.consts.t[:], cin)
        P.dma("pool", g.ident.t[:], identf, writes=[g.ident], sem=g.ident.dsem)
        P.op("dve", lambda e: e.memset(g.ones.t[:], 1.0), [], [g.ones])
        ndma_persist = P.dnext

        def run():
            phase_tables(P, g)
            if stop == "tables":
                return
            for L in range(nlayers):
                j = L // 2
                hsrc = g.xT if L == 0 else g.hT
                if L % 2 == 0:
                    phase_E1(P, g, L, j, hsrc)
                    if stop == f"E1_{L}":
                        return
                    phase_E2(P, g, j)
                    phase_E3(P, g, j)
                    if stop == f"E3_{L}":
                        return
                    phase_E4(P, g)
                    if stop == f"E4_{L}":
                        return
                    phase_E5(P, g, j)
                    if stop == f"E5_{L}":
                        return
                    phase_proj_res(P, g, g.abT, 16, g.w_out_fm[j], hsrc, g.hT, SEQ)
                else:
                    phase_O1(P, g, L, j)
                    if stop == f"O1_{L}":
                        return
                    phase_O2(P, g, j)
                    if stop == f"O2_{L}":
                        return
                    phase_proj_res(P, g, g.rrT, 32, g.ret_out_fm[j], g.hT, g.hT, 1024)
                if stop == f"mix_{L}":
                    return
                phase_F1(P, g, L)
                if stop == f"F1_{L}":
                    return
                phase_proj_res(P, g, g.ffT, 44, g.ffn_dn_fm[L], g.hT, g.hT, 1024)
                if stop == f"ffn_{L}":
                    return
                phase_PLE(P, g, L)
                if stop == f"ple_{L}":
                    return
            if g.yT is not None:
                phase_final(P, g)

        orig_phase = P.phase

        @contextlib.contextmanager
        def phase_keep():
            with orig_phase():
                P.dnext = ndma_persist
                yield
        P.phase = phase_keep
        run()
        P.barrier()
    return nc


def _fm(W):
    K, N = W.shape
    return np.ascontiguousarray(W.reshape(K // 128, 128, N // 128, 128).transpose(2, 1, 0, 3).reshape(N // 128, 128, K))


def _tm(W):
    K, N = W.shape
    return np.ascontiguousarray(W.reshape(K // 128, 128, N // 512, 512).transpose(2, 1, 0, 3).reshape(N // 512, 128, (K // 128) * 512))


def _pc(v):
    return np.ascontiguousarray(v.reshape(-1, 128).T)


def module_constants():
    H = 8
    log_g = np.log1p(-(2.0 ** (-5.0 - np.arange(H, dtype=np.float64))))
    gam = np.exp(log_g)
    consts = np.zeros((128, 16), np.float32)
    p = np.arange(128)
    consts[:, 0] = -np.pi
    consts[:, 1] = 10000.0 ** (-(p % 32).astype(np.float64) / 32)
    consts[:, 2] = 10000.0 ** (-p.astype(np.float64) / 128)
    consts[:, 3] = np.where((p % 64) < 32, -1.0, 1.0)
    consts[:, 4] = -1.0
    for h in range(H):
        consts[:, 5 + h] = gam[h] ** (127 - p)
    i = p[:, None]
    jj = p[None, :]
    DT = np.zeros((128, H * 128), np.float32)
    for h in range(H):
        Dm = np.where((i // 64) >= (jj // 64), gam[h] ** np.abs(i - jj).astype(np.float64), 0.0)
        DT[:, h * 128:(h + 1) * 128] = Dm.T
    qd4 = np.zeros((128, H * 512), np.float32)
    for h in range(H):
        qd4[:, h * 512:(h + 1) * 512] = (gam[h] ** ((np.arange(512) % 128) + 1.0))[None, :]
    cd128 = gam ** 128
    return dict(consts=consts, DT=DT, qd4=qd4, cd128=cd128, ident=np.eye(128, dtype=np.float32))


def prep_shared(inp):
    sh = {}
    c = module_constants()
    sh["consts"] = c["consts"]
    sh["DT"] = c["DT"]
    sh["qd4"] = c["qd4"]
    sh["ident"] = c["ident"]
    f = np.float32
    sh["mix_g"] = np.stack([_pc(v) for v in inp["mix_norm_g"]]).astype(f)
    sh["ffn_g"] = np.stack([_pc(v) for v in inp["ffn_norm_g"]]).astype(f)
    sh["ple_g"] = np.stack([_pc(v) for v in inp["ple_norm_g"]]).astype(f)
    sh["fin_g"] = _pc(inp["final_norm_g"]).astype(f)
    sh["qn_g"] = np.stack([_pc(v) for v in inp["mla_q_norm_g"]]).astype(f)
    sh["kvn_g"] = np.stack([_pc(v) for v in inp["mla_kv_norm_g"]]).astype(f)
    sh["sgu_lng"] = np.ascontiguousarray(np.broadcast_to(inp["sgu_ln_g"][:, None, :], (2, 128, 1024))).astype(f)
    sh["sgu_lnb"] = np.ascontiguousarray(np.broadcast_to(inp["sgu_ln_b"][:, None, :], (2, 128, 1024))).astype(f)
    bs = np.tile(inp["sgu_b_s"][:, :, None, :], (1, 1, 4, 1)).reshape(2, 1, 8 * 512)
    sh["sgu_bs4"] = np.ascontiguousarray(np.broadcast_to(bs, (2, 128, 4096))).astype(f)
    sh["sgu_wsT"] = np.ascontiguousarray(inp["sgu_w_s"].transpose(0, 3, 1, 2).reshape(2, 128, 1024)).astype(f)
    sh["gn_g"] = np.stack([_pc(v) for v in inp["ret_gn_g"]]).astype(f)
    sh["gn_b"] = np.stack([_pc(v) for v in inp["ret_gn_b"]]).astype(f)
    cw = inp["ffn_conv_w"]
    sh["conv_w"] = np.ascontiguousarray(cw.reshape(4, 3, 88, 128).transpose(0, 3, 2, 1).reshape(4, 128, 264)).astype(f)
    sh["conv_b"] = np.stack([_pc(v) for v in inp["ffn_conv_b"]]).astype(f)
    swap64 = np.concatenate([np.arange(32, 64), np.arange(0, 32)])
    w_in_fm, w_in_tm, w_q_fm, w_kv_fm, w_kv_tm, w_out_fm = [], [], [], [], [], []
    for j in range(2):
        W = inp["even_w_in"][j]
        kpe = W[:, 768:832]
        fmcols = np.concatenate([W[:, 0:768], kpe, kpe[:, swap64], W[:, 832:1856]], axis=1)
        w_in_fm.append(_fm(fmcols))
        w_in_tm.append(_tm(W[:, 1856:2880]))
        Wq = inp["mla_w_q_up"][j].reshape(512, 8, 192)
        nope = Wq[:, :, 0:128].reshape(512, 1024)
        rope = Wq[:, :, 128:192]
        w_q_fm.append(_fm(np.concatenate([nope, rope.reshape(512, 512), rope[:, :, swap64].reshape(512, 512)], axis=1)))
        Wkv = inp["mla_w_kv_up"][j].reshape(256, 8, 256)
        w_kv_fm.append(_fm(np.ascontiguousarray(Wkv[:, :, 0:128]).reshape(256, 1024)))
        w_kv_tm.append(_tm(np.ascontiguousarray(Wkv[:, :, 128:256]).reshape(256, 1024)))
        w_out_fm.append(_fm(inp["even_w_out"][j]))
    sh["w_in_fm"] = np.stack(w_in_fm)
    sh["w_in_tm"] = np.stack(w_in_tm)
    sh["w_q_fm"] = np.stack(w_q_fm)
    sh["w_kv_fm"] = np.stack(w_kv_fm)
    sh["w_kv_tm"] = np.stack(w_kv_tm)
    sh["w_out_fm"] = np.stack(w_out_fm)
    ret_in_fm, ret_in_tm, ret_out_fm = [], [], []
    for j in range(2):
        W = inp["ret_w_in"][j]
        ret_in_fm.append(_fm(np.concatenate([W[:, 0:4096], W[:, 8192:12288]], axis=1)))
        ret_in_tm.append(_tm(W[:, 4096:8192]))
        ret_out_fm.append(_fm(inp["ret_w_out"][j]))
    sh["ret_in_fm"] = np.stack(ret_in_fm)
    sh["ret_in_tm"] = np.stack(ret_in_tm)
    sh["ret_out_fm"] = np.stack(ret_out_fm)
    sh["ffn_up_fm"] = np.stack([_fm(inp["ffn_w_up"][L]) for L in range(DEPTH)])
    sh["ffn_dn_fm"] = np.stack([_fm(inp["ffn_w_down"][L]) for L in range(DEPTH)])
    sh["ple_gate_fm"] = np.stack([_fm(inp["ple_w_gate"][L]) for L in range(DEPTH)])
    sh["ple_up_fm"] = np.stack([_fm(inp["ple_w_up"][L]) for L in range(DEPTH)])
    return sh, c


def prep_core(inp, core):
    b0 = core * NSEQ
    x = inp["x"][b0:b0 + NSEQ]
    xT = np.ascontiguousarray(x.reshape(T, D).T)
    p = inp["p"][:, b0:b0 + NSEQ]
    pT = np.ascontiguousarray(p.reshape(DEPTH, T, 256).transpose(0, 2, 1))
    pos = np.ascontiguousarray(np.broadcast_to(inp["positions"][b0:b0 + NSEQ].reshape(1, T), (128, T))).astype(np.int32)
    return {"xT": xT.astype(np.float32), "pT": pT.astype(np.float32), "pos": pos}


def kernel(**inputs):
    inp = {k: np.asarray(v) for k, v in inputs.items()}
    sh, c = prep_shared(inp)
    nc = build_program(c)
    in_maps = []
    for core in range(NCORES):
        m = dict(sh)
        m.update(prep_core(inp, core))
        in_maps.append(m)
    res = run_bass_kernel_spmd(nc, in_maps, core_ids=list(range(NCORES)))
    out = np.empty((NCORES * NSEQ, SEQ, D), np.float32)
    for core in range(NCORES):
        yT = np.asarray(res.results[core]["yT"])
        out[core * NSEQ:(core + 1) * NSEQ] = yT.T.reshape(NSEQ, SEQ, D)
    return out
```
